# Optimizing a Trainium2 kernel written in Bass

```python
import jax
import jax.numpy as jnp
from jax import lax
import numpy as np

D_MODEL = 2048
BATCH = 1
SEQ = 8192
DEPTH = 4

N_MIXERS = 4
HEADS_PER_GROUP = 4
GROUP_WIDTH = D_MODEL // N_MIXERS
HEAD_DIM = GROUP_WIDTH // HEADS_PER_GROUP
MIX_WIDTH = N_MIXERS * GROUP_WIDTH
ROPE_THETA = 10000.0
Q_BLOCK = 128
RET_CHUNK = 128
HGRN_CHUNK = 64
NSA_CMP_LEN = 32
NSA_CMP_STRIDE = 16
NSA_SLC_LEN = 64
NSA_TOPK = 16
NSA_WINDOW = 512
N_MEM = 256
XATTN_HEADS = 4
XATTN_DIM = 128
D_FF = 5632
CONV_WIDTH = 3
NORM_EPS = 1e-6
MASK_VALUE = -1e30
FORCED_SCORE = 1e4
MIN_FORGET = 1e-6
IN_COLS = 12 * GROUP_WIDTH + 6 * HEAD_DIM + 3 * HEADS_PER_GROUP

kernel_name = 'hybrid_parallel_head_groups_decoder'


def rms_norm(x, g):
    xf = x.astype(jnp.float32)
    y = xf * lax.rsqrt(jnp.mean(xf * xf, axis=-1, keepdims=True) + NORM_EPS)
    return (y * g.astype(jnp.float32)).astype(x.dtype)


def head_norm(o, g, center):
    of = o.astype(jnp.float32)
    if center:
        of = of - jnp.mean(of, axis=-1, keepdims=True)
    y = of * lax.rsqrt(jnp.mean(of * of, axis=-1, keepdims=True) + NORM_EPS)
    return y * g.astype(jnp.float32).reshape(o.shape[-2], o.shape[-1])


def rope(x, pos):
    half = x.shape[-1] // 2
    inv = ROPE_THETA ** (-jnp.arange(half, dtype=jnp.float32) / half)
    ang = pos.astype(jnp.float32)[:, :, None, None] * inv
    cos, sin = jnp.cos(ang), jnp.sin(ang)
    x1 = x[..., :half].astype(jnp.float32)
    x2 = x[..., half:].astype(jnp.float32)
    return jnp.concatenate([x1 * cos - x2 * sin, x2 * cos + x1 * sin], axis=-1).astype(x.dtype)


def masked_softmax(s, mask):
    s = jnp.where(mask, s, MASK_VALUE)
    m = jnp.max(s, axis=-1, keepdims=True)
    p = jnp.exp(s - m) * mask
    return p / jnp.maximum(jnp.sum(p, axis=-1, keepdims=True), 1e-30)


def retention(q, k, v):
    B, S, H, D = q.shape
    C = RET_CHUNK
    nc = S // C
    f32 = jnp.float32
    log_gamma = jnp.log1p(-jnp.exp2(-5.0 - jnp.arange(H, dtype=f32)))
    qc = q.astype(f32).reshape(B, nc, C, H, D)
    kc = k.astype(f32).reshape(B, nc, C, H, D) * D ** -0.5
    vc = v.astype(f32).reshape(B, nc, C, H, D)
    pos = jnp.arange(C, dtype=f32)
    rel = pos[:, None] - pos[None, :]
    causal = rel >= 0
    decay = jnp.where(causal, jnp.exp(log_gamma[:, None, None] * jnp.where(causal, rel, 0.0)), 0.0)
    scores = jnp.einsum('bcnhd,bcmhd->bchnm', qc, kc) * decay
    o_intra = jnp.einsum('bchnm,bcmhe->bcnhe', scores, vc)
    k_w = jnp.exp(log_gamma[:, None] * (C - 1.0 - pos))
    kv = jnp.einsum('bcmhd,hm,bcmhe->cbhde', kc, k_w, vc)
    chunk_decay = jnp.exp(log_gamma * C)[None, :, None, None]

    def step(state, kv_c):
        return state * chunk_decay + kv_c, state

    _, prev = lax.scan(step, jnp.zeros((B, H, D, D), f32), kv)
    q_w = jnp.exp(log_gamma[:, None] * (pos + 1.0))
    o_inter = jnp.einsum('bcnhd,cbhde,hn->bcnhe', qc, prev, q_w)
    return (o_intra + o_inter).reshape(B, S, H, D)


def stick_breaking(q, k, v):
    B, S, H, D = q.shape
    nb = S // Q_BLOCK
    kh = k.transpose(0, 2, 1, 3)
    vh = v.transpose(0, 2, 1, 3)
    qb = q.reshape(B, nb, Q_BLOCK, H, D).transpose(1, 0, 3, 2, 4)
    key_pos = jnp.arange(S)
    scale = D ** -0.5

    def block(args):
        qi, bi = args
        z = jnp.einsum('bhqd,bhkd->bhqk', qi, kh).astype(jnp.float32) * scale
        q_pos = bi * Q_BLOCK + jnp.arange(Q_BLOCK)
        strict = key_pos[None, :] < q_pos[:, None]
        log_1m = jnp.where(strict, jax.nn.log_sigmoid(-z), 0.0)
        between = lax.cumsum(log_1m, axis=3, reverse=True) - log_1m
        w = jnp.where(strict, jnp.exp(jax.nn.log_sigmoid(z) + between), 0.0)
        return jnp.einsum('bhqk,bhkd->bhqd', w.astype(vh.dtype), vh)

    o = lax.map(block, (qb, jnp.arange(nb)))
    return o.transpose(1, 0, 3, 2, 4).reshape(B, S, H, D)


def hgrn2(q, f_logit, inp, lower_bound):
    B, S, H, D = q.shape
    C = HGRN_CHUNK
    nc = S // C
    f32 = jnp.float32
    lb = lower_bound.astype(f32)
    f = lb + (1.0 - lb) * jax.nn.sigmoid(f_logit.astype(f32))
    log_f = jnp.log(jnp.maximum(f, MIN_FORGET))
    k = 1.0 - f

    def chunks(t):
        return t.astype(f32).reshape(B, nc, C, H, D).transpose(1, 0, 3, 2, 4)

    tril = jnp.tril(jnp.ones((C, C), dtype=bool))[..., None]

    def step(state, xs):
        qi, ki, vi, lfi = xs
        G = jnp.cumsum(lfi, axis=2)
        diff = G[:, :, :, None, :] - G[:, :, None, :, :]
        decay = jnp.where(tril, jnp.exp(jnp.where(tril, diff, 0.0)), 0.0)
        scores = jnp.einsum('bhnd,bhmd,bhnmd->bhnm', qi, ki, decay)
        o = (jnp.einsum('bhnm,bhme->bhne', scores, vi)
             + jnp.einsum('bhnd,bhde->bhne', qi * jnp.exp(G), state))
        g_last = G[:, :, -1, :]
        new_state = (jnp.exp(g_last)[..., None] * state
                     + jnp.einsum('bhmd,bhme->bhde', ki * jnp.exp(g_last[:, :, None, :] - G), vi))
        return new_state, o

    _, o = lax.scan(step, jnp.zeros((B, H, D, D), f32),
                    (chunks(q) * D ** -0.5, chunks(k), chunks(inp), chunks(log_f)))
    return o.transpose(1, 0, 3, 2, 4).reshape(B, S, H, D)


def nsa(q, kc, vc, ks, vs, kw, vw, gates, pos_k, pos_v, w_ck, w_cv):
    B, S, H, D = q.shape
    nb = S // Q_BLOCK
    n_cmp = (S - NSA_CMP_LEN) // NSA_CMP_STRIDE + 1
    n_slc = S // NSA_SLC_LEN
    top_k = min(NSA_TOPK, n_slc)
    scale = D ** -0.5
    f32 = jnp.float32
    cmp_start = np.arange(n_cmp) * NSA_CMP_STRIDE
    win_idx = cmp_start[:, None] + np.arange(NSA_CMP_LEN)[None, :]
    k_cmp = jnp.einsum('bnld,lde->bne', kc[:, win_idx] + pos_k, w_ck)
    v_cmp = jnp.einsum('bnld,lde->bne', vc[:, win_idx] + pos_v, w_cv)
    cmp_last = jnp.asarray(cmp_start + NSA_CMP_LEN - 1, dtype=jnp.int32)
    slc_start = np.arange(n_slc) * NSA_SLC_LEN
    ov = (np.minimum(cmp_start[:, None] + NSA_CMP_LEN, slc_start[None, :] + NSA_SLC_LEN)
          - np.maximum(cmp_start[:, None], slc_start[None, :]))
    overlap = jnp.asarray(np.clip(ov, 0, None) / NSA_CMP_LEN, dtype=f32)
    ks_blocks = ks.reshape(B, n_slc, NSA_SLC_LEN, D)
    vs_blocks = vs.reshape(B, n_slc, NSA_SLC_LEN, D)
    kw_pad = jnp.pad(kw, ((0, 0), (NSA_WINDOW, 0), (0, 0)))
    vw_pad = jnp.pad(vw, ((0, 0), (NSA_WINDOW, 0), (0, 0)))
    qb = q.reshape(B, nb, Q_BLOCK, H, D).transpose(1, 0, 3, 2, 4)
    gb = jax.nn.sigmoid(gates.astype(f32)).reshape(B, nb, Q_BLOCK, H, 3).transpose(1, 0, 3, 2, 4)
    gather = jax.vmap(lambda blocks, idx: blocks[idx])
    blk = jnp.arange(n_slc)

    def block(args):
        qi, gi, bi = args
        q_pos = bi * Q_BLOCK + jnp.arange(Q_BLOCK)
        s_cmp = jnp.einsum('bhqd,bnd->bhqn', qi, k_cmp).astype(f32) * scale
        p_cmp = masked_softmax(s_cmp, cmp_last[None, :] <= q_pos[:, None])
        o_cmp = jnp.einsum('bhqn,bnd->bhqd', p_cmp, v_cmp)
        imp = jnp.einsum('bhqn,nj->bqj', p_cmp, overlap)
        cur = q_pos // NSA_SLC_LEN
        forced = (blk[None, :] == 0) | (blk[None, :] == cur[:, None]) | (blk[None, :] == cur[:, None] - 1)
        future = blk[None, :] * NSA_SLC_LEN > q_pos[:, None]
        imp = jnp.where(future, -1.0, jnp.where(forced, FORCED_SCORE, imp))
        _, sel = lax.top_k(imp, top_k)
        k_sel = gather(ks_blocks, sel).reshape(B, Q_BLOCK, top_k * NSA_SLC_LEN, D)
        v_sel = gather(vs_blocks, sel).reshape(B, Q_BLOCK, top_k * NSA_SLC_LEN, D)
        tok_pos = (sel[..., None] * NSA_SLC_LEN + jnp.arange(NSA_SLC_LEN)).reshape(B, Q_BLOCK, top_k * NSA_SLC_LEN)
        s_slc = jnp.einsum('bhqd,bqnd->bhqn', qi, k_sel).astype(f32) * scale
        p_slc = masked_softmax(s_slc, (tok_pos <= q_pos[None, :, None])[:, None])
        o_slc = jnp.einsum('bhqn,bqnd->bhqd', p_slc, v_sel)
        k_win = lax.dynamic_slice_in_dim(kw_pad, bi * Q_BLOCK, NSA_WINDOW + Q_BLOCK, axis=1)
        v_win = lax.dynamic_slice_in_dim(vw_pad, bi * Q_BLOCK, NSA_WINDOW + Q_BLOCK, axis=1)
        w_pos = bi * Q_BLOCK - NSA_WINDOW + jnp.arange(NSA_WINDOW + Q_BLOCK)
        m_win = ((w_pos[None, :] >= 0) & (w_pos[None, :] <= q_pos[:, None])
                 & (q_pos[:, None] - w_pos[None, :] < NSA_WINDOW))
        s_win = jnp.einsum('bhqd,bkd->bhqk', qi, k_win).astype(f32) * scale
        o_win = jnp.einsum('bhqk,bkd->bhqd', masked_softmax(s_win, m_win), v_win)
        return gi[..., 0:1] * o_cmp + gi[..., 1:2] * o_slc + gi[..., 2:3] * o_win

    o = lax.map(block, (qb, gb, jnp.arange(nb)))
    return o.transpose(1, 0, 3, 2, 4).reshape(B, S, H, D)


def cross_attention(h, mem_n, wq, wk, wv, wo):
    B, S, _ = h.shape
    M = mem_n.shape[1]
    q = (h @ wq).reshape(B, S, XATTN_HEADS, XATTN_DIM)
    k = (mem_n @ wk).reshape(B, M, XATTN_HEADS, XATTN_DIM)
    v = (mem_n @ wv).reshape(B, M, XATTN_HEADS, XATTN_DIM)
    s = jnp.einsum('bshd,bmhd->bhsm', q, k).astype(jnp.float32) * XATTN_DIM ** -0.5
    p = jax.nn.softmax(s, axis=-1)
    o = jnp.einsum('bhsm,bmhd->bshd', p.astype(v.dtype), v).reshape(B, S, XATTN_HEADS * XATTN_DIM)
    return (o @ wo).astype(h.dtype)


def conv_ffn(h, w_up, conv_w, conv_b, w_down):
    u = h @ w_up
    S = u.shape[1]
    up = jnp.pad(u, ((0, 0), (CONV_WIDTH - 1, 0), (0, 0)))
    c = conv_b
    for j in range(CONV_WIDTH):
        c = c + conv_w[j] * up[:, j:j + S]
    gate, val = jnp.split(c, 2, axis=-1)
    return ((jax.nn.silu(gate) * val) @ w_down).astype(h.dtype)


def setup_inputs(seed: int = 0) -> dict:
    key = jax.random.key(seed)
    k = jax.random.split(key, 26)
    f32 = jnp.float32

    def normal(kk, shape, scale):
        return jax.random.normal(kk, shape, f32) * scale

    def gain(kk, shape):
        return 1.0 + 0.02 * jax.random.normal(kk, shape, f32)

    L = DEPTH
    XW = XATTN_HEADS * XATTN_DIM
    offset = jax.random.randint(k[2], (BATCH, 1), 0, 1024, dtype=jnp.int32)
    return {
        'x': normal(k[0], (BATCH, SEQ, D_MODEL), 1.0),
        'mem': normal(k[1], (BATCH, N_MEM, D_MODEL), 1.0),
        'positions': offset + jnp.arange(SEQ, dtype=jnp.int32)[None, :],
        'mix_norm': gain(k[3], (L, D_MODEL)),
        'w_in': normal(k[4], (L, D_MODEL, IN_COLS), D_MODEL ** -0.5),
        'ret_norm': gain(k[5], (L, GROUP_WIDTH)),
        'hgrn_lb_logits': normal(k[6], (L, GROUP_WIDTH), 0.5),
        'hgrn_norm': gain(k[7], (L, GROUP_WIDTH)),
        'nsa_pos_k': normal(k[8], (L, NSA_CMP_LEN, HEAD_DIM), 0.1),
        'nsa_pos_v': normal(k[9], (L, NSA_CMP_LEN, HEAD_DIM), 0.1),
        'nsa_w_ck': normal(k[10], (L, NSA_CMP_LEN, HEAD_DIM, HEAD_DIM), (NSA_CMP_LEN * HEAD_DIM) ** -0.5),
        'nsa_w_cv': normal(k[11], (L, NSA_CMP_LEN, HEAD_DIM, HEAD_DIM), (NSA_CMP_LEN * HEAD_DIM) ** -0.5),
        'w_out': normal(k[12], (L, MIX_WIDTH, D_MODEL), MIX_WIDTH ** -0.5),
        'xattn_norm': gain(k[13], (L, D_MODEL)),
        'mem_norm': gain(k[14], (L, D_MODEL)),
        'xattn_wq': normal(k[15], (L, D_MODEL, XW), D_MODEL ** -0.5),
        'xattn_wk': normal(k[16], (L, D_MODEL, XW), D_MODEL ** -0.5),
        'xattn_wv': normal(k[17], (L, D_MODEL, XW), D_MODEL ** -0.5),
        'xattn_wo': normal(k[18], (L, XW, D_MODEL), XW ** -0.5),
        'ffn_norm': gain(k[19], (L, D_MODEL)),
        'ffn_w_up': normal(k[20], (L, D_MODEL, 2 * D_FF), D_MODEL ** -0.5),
        'ffn_conv_w': normal(k[21], (L, CONV_WIDTH, 2 * D_FF), CONV_WIDTH ** -0.5),
        'ffn_conv_b': normal(k[22], (L, 2 * D_FF), 0.01),
        'ffn_w_down': normal(k[23], (L, D_FF, D_MODEL), D_FF ** -0.5),
        'final_norm': gain(k[24], (D_MODEL,)),
    }


def reference(x, mem, positions, mix_norm, w_in, ret_norm, hgrn_lb_logits, hgrn_norm,
              nsa_pos_k, nsa_pos_v, nsa_w_ck, nsa_w_cv, w_out, xattn_norm, mem_norm,
              xattn_wq, xattn_wk, xattn_wv, xattn_wo, ffn_norm, ffn_w_up, ffn_conv_w,
              ffn_conv_b, ffn_w_down, final_norm):
    B, S, _ = x.shape
    H, D = HEADS_PER_GROUP, HEAD_DIM
    lb_p = jax.nn.softmax(hgrn_lb_logits.astype(jnp.float32), axis=0)
    lower_bounds = jnp.cumsum(lb_p, axis=0) - lb_p[0]
    widths = [GROUP_WIDTH] * 12 + [HEAD_DIM] * 6
    split_points = [int(v) for v in np.cumsum(widths)]

    def heads(t):
        return t.reshape(B, S, H, D)

    def rope_kv(t):
        return rope(t[:, :, None, :], positions)[:, :, 0, :]

    for layer in range(DEPTH):
        h = rms_norm(x, mix_norm[layer])
        proj = h @ w_in[layer]
        (rq, rk, rv, rg, sq, sk, sv, gq, gf, gi, gg, nq,
         kc, vc, ks, vs, kw, vw, ng) = jnp.split(proj, split_points, axis=-1)
        o_ret = retention(rope(heads(rq), positions), rope(heads(rk), positions), heads(rv))
        o_ret = head_norm(o_ret, ret_norm[layer], True) * jax.nn.silu(heads(rg).astype(jnp.float32))
        o_sb = stick_breaking(heads(sq), heads(sk), heads(sv))
        o_hg = hgrn2(heads(gq), heads(gf), heads(gi), lower_bounds[layer].reshape(H, D))
        o_hg = head_norm(o_hg, hgrn_norm[layer], False) * jax.nn.silu(heads(gg).astype(jnp.float32))
        o_nsa = nsa(rope(heads(nq), positions), rope_kv(kc), vc, rope_kv(ks), vs, rope_kv(kw), vw,
                    ng.reshape(B, S, H, 3), nsa_pos_k[layer], nsa_pos_v[layer],
                    nsa_w_ck[layer], nsa_w_cv[layer])
        mix = jnp.concatenate([o.reshape(B, S, GROUP_WIDTH).astype(x.dtype)
                               for o in (o_ret, o_sb, o_hg, o_nsa)], axis=-1)
        x = x + (mix @ w_out[layer]).astype(x.dtype)
        x = x + cross_attention(rms_norm(x, xattn_norm[layer]), rms_norm(mem, mem_norm[layer]),
                                xattn_wq[layer], xattn_wk[layer], xattn_wv[layer], xattn_wo[layer])
        x = x + conv_ffn(rms_norm(x, ffn_norm[layer]), ffn_w_up[layer], ffn_conv_w[layer],
                         ffn_conv_b[layer], ffn_w_down[layer])
    return rms_norm(x, final_norm)
```

```python
import numpy as np
from contextlib import ExitStack
import concourse.bass as bass
import concourse.mybir as mybir
from concourse.bass_utils import run_bass_kernel_spmd

F32 = mybir.dt.float32
BF16 = mybir.dt.bfloat16
I32 = mybir.dt.int32
AF = mybir.ActivationFunctionType
ALU = mybir.AluOpType
AX = mybir.AxisListType

NSLOT = 12


class Res:
    __slots__ = ("name", "last_w", "readers")

    def __init__(self, name=""):
        self.name = name
        self.last_w = None
        self.readers = {}


class T:
    def __init__(self, ap, name, nres=1):
        self.ap = ap
        self.name = name
        self.res = [Res(f"{name}.{i}") for i in range(nres)]

    def __getitem__(self, idx):
        return self.ap[idx]

    @property
    def r(self):
        return self.res[0]


class Prog:
    ENG = ("pe", "act", "dve", "pool", "sp")

    def __init__(self, nc):
        self.nc = nc
        self.es = ExitStack()
        self.ops = {e: [] for e in self.ENG}
        self.sems = {}
        self.cnt = {}
        self.known = {e: {} for e in self.ENG}
        self.dma_i = {e: 0 for e in self.ENG}
        self.nalloc = 0
        self.scopes = []
        for e in ("pe", "act", "dve", "pool"):
            self._mksem(e)
        for q in ("sp", "act", "pool"):
            for s in range(NSLOT):
                self._mksem(("dma", q, s))

    def _mksem(self, key):
        nm = "s_" + "_".join(str(k) for k in (key if isinstance(key, tuple) else (key,)))
        self.sems[key] = self.es.enter_context(self.nc.semaphore(nm))
        self.cnt[key] = 0

    def sb(self, shape, dtype=F32, name=None, nres=1):
        self.nalloc += 1
        name = name or f"t{self.nalloc}"
        es = self.scopes[-1] if self.scopes else self.es
        t = es.enter_context(self.nc.sbuf_tensor(f"{name}_{self.nalloc}", list(shape), dtype))
        return T(t, name, nres)

    def mark(self):
        self.scopes.append(ExitStack())
        return len(self.scopes) - 1

    def release(self, mark):
        self.barrier()
        self.flush()
        while len(self.scopes) > mark:
            self.scopes.pop().close()

    def barrier(self):
        for e in self.ENG:
            waits = []
            for k, v in self.cnt.items():
                if v == 0 or (k == "pe" and e == "pe"):
                    continue
                if self.known[e].get(k, 0) < v:
                    self.known[e][k] = v
                    waits.append((k, v))
            if waits:
                self.ops[e].append((waits, None, None))

    def ps(self, shape, dtype=F32, name=None, nres=1):
        self.nalloc += 1
        name = name or f"p{self.nalloc}"
        es = self.scopes[-1] if self.scopes else self.es
        t = es.enter_context(self.nc.psum_tensor(f"{name}_{self.nalloc}", list(shape), dtype))
        return T(t, name, nres)

    def dram(self, name, shape, dtype, kind):
        t = self.nc.dram_tensor(name, list(shape), dtype, kind=kind)
        return T(t.ap(), name)

    def _deps(self, eng, reads, writes):
        deps = {}

        def add(kv):
            if kv is None:
                return
            k, v = kv
            if deps.get(k, 0) < v:
                deps[k] = v
        for r in reads:
            add(r.last_w)
        for w in writes:
            add(w.last_w)
            for k, v in w.readers.items():
                add((k, v))
        out = []
        kn = self.known[eng]
        for k, v in deps.items():
            if k == "pe" and eng == "pe":
                continue
            if kn.get(k, 0) >= v:
                continue
            kn[k] = v
            out.append((k, v))
        return out

    @staticmethod
    def _rl(x):
        out = []
        for i in x:
            if isinstance(i, T):
                out.extend(i.res)
            elif isinstance(i, Res):
                out.append(i)
            elif i is None:
                pass
            else:
                out.extend(Prog._rl(i))
        return out

    def op(self, eng, fn, reads=(), writes=(), sync=True):
        reads = self._rl(reads)
        writes = self._rl(writes)
        waits = self._deps(eng, reads, writes)
        if sync:
            self.cnt[eng] += 1
            v = self.cnt[eng]
            self.ops[eng].append((waits, fn, (eng, 1)))
        else:
            v = self.cnt[eng] + 1
            self.ops[eng].append((waits, fn, None))
        for r in reads:
            if r.readers.get(eng, 0) < v:
                r.readers[eng] = v
        for w in writes:
            w.last_w = (eng, v)
            w.readers = {}

    def dma(self, q, out_ap, in_ap, reads=(), writes=(), **kw):
        reads = self._rl(reads)
        writes = self._rl(writes)
        i = self.dma_i[q]
        self.dma_i[q] += 1
        key = ("dma", q, i % NSLOT)
        waits = self._deps(q, reads, writes)
        prev = self.cnt[key]
        if prev > 0 and self.known[q].get(key, 0) < prev:
            self.known[q][key] = prev
            waits.append((key, prev))
        self.cnt[key] += 16
        v = self.cnt[key]

        def fn(e, out_ap=out_ap, in_ap=in_ap, kw=kw):
            return e.dma_start(out=out_ap, in_=in_ap, **kw)
        self.ops[q].append((waits, fn, (key, 16)))
        for r in reads:
            if r.readers.get(key, 0) < v:
                r.readers[key] = v
        for w in writes:
            w.last_w = (key, v)
            w.readers = {}

    def finish(self, out_res):
        for r in self._rl(out_res):
            waits = self._deps("sp", [r], [])
            if waits:
                self.ops["sp"].append((waits, None, None))

    def flush(self):
        nc = self.nc
        sems = self.sems
        ops = self.ops
        if not any(ops[e] for e in self.ENG):
            return

        def replay(e, lst):
            for waits, fn, inc in lst:
                for k, v in waits:
                    e.wait_ge(sems[k], v)
                if fn is None:
                    continue
                ins = fn(e)
                if inc is not None:
                    ins.then_inc(sems[inc[0]], inc[1])

        with nc.Block() as block:
            @block.tensor
            def _(e):
                replay(e, ops["pe"])

            @block.scalar
            def _(e):
                replay(e, ops["act"])

            @block.vector
            def _(e):
                replay(e, ops["dve"])

            @block.gpsimd
            def _(e):
                replay(e, ops["pool"])

            @block.sync
            def _(e):
                replay(e, ops["sp"])
        self.nops = getattr(self, "nops", 0) + sum(len(v) for v in ops.values())
        self.ops = {e: [] for e in self.ENG}

    def build(self):
        self.flush()
        while self.scopes:
            self.scopes.pop().close()
        self.es.close()

    def mm(self, out, lhsT, rhs, start, stop, reads, writes, sync=None):
        if sync is None:
            sync = stop
        self.op("pe", lambda e: e.matmul(out, lhsT, rhs, start=start, stop=stop),
                reads, writes, sync=sync)

    def tr(self, out, in_, ident, reads, writes):
        self.op("pe", lambda e: e.transpose(out, in_, ident), reads, writes)

    def actf(self, out, in_, func, reads, writes, bias=None, scale=None, accum_out=None, eng="act"):
        kw = {}
        if bias is not None:
            kw["bias"] = bias
        if scale is not None:
            kw["scale"] = scale
        if accum_out is not None:
            kw["accum_out"] = accum_out
        self.op("act", lambda e: e.activation(out, in_, func, **kw), reads, writes)

    def tt(self, eng, out, in0, in1, op, reads, writes):
        self.op(eng, lambda e: e.tensor_tensor(out, in0, in1, op), reads, writes)

    def ts(self, eng, out, in0, s1, s2, op0, op1, reads, writes):
        if op1 is None:
            self.op(eng, lambda e: e.tensor_scalar(out, in0, s1, None, op0), reads, writes)
        else:
            self.op(eng, lambda e: e.tensor_scalar(out, in0, s1, s2, op0, op1), reads, writes)

    def stt(self, out, in0, scalar, in1, op0, op1, reads, writes):
        self.op("dve", lambda e: e.scalar_tensor_tensor(out, in0, scalar, in1, op0, op1), reads, writes)

    def copy(self, eng, out, in_, reads, writes):
        if eng == "act":
            self.op("act", lambda e: e.copy(out, in_), reads, writes)
        else:
            self.op(eng, lambda e: e.tensor_copy(out, in_), reads, writes)

    def memset(self, eng, ap, val, writes):
        self.op(eng, lambda e: e.memset(ap, val), (), writes)


import math
import ml_dtypes

D = 2048
KC = 16
HD = 128
DFF = 5632
NCOL = 6924
NROPE = 15
NCE = NCOL + NROPE * 128
EPS = 1e-6
SCALE = HD ** -0.5

C_RQ, C_RK, C_RV, C_RG = 0, 512, 1024, 1536
C_SQ, C_SK, C_SV = 2048, 2560, 3072
C_GQ, C_GF, C_GI, C_GG = 3584, 4096, 4608, 5120
C_NQ = 5632
C_KC, C_VC, C_KS, C_VS, C_KW, C_VW, C_NG = 6144, 6272, 6400, 6528, 6656, 6784, 6912
ROPE_COLS = [C_RQ + 128 * h for h in range(4)] + [C_RK + 128 * h for h in range(4)] + \
            [C_NQ + 128 * h for h in range(4)] + [C_KC, C_KS, C_KW]
FB_RQ, FB_RK, FB_NQ, FB_KC, FB_KS, FB_KW = 0, 4, 8, 12, 13, 14
FB_SQ, FB_SK, FB_GQ, FB_VC = 15, 19, 23, 27
NFB = 28
FMB_PLAIN = [(C_SQ + 128 * h, FB_SQ + h) for h in range(4)] + [(C_SK + 128 * h, FB_SK + h) for h in range(4)] + \
            [(C_GQ + 128 * h, FB_GQ + h) for h in range(4)] + [(C_VC, FB_VC)]
FF_GF, FF_RG, FF_GG = 0, 4, 8
NFF = 12
FMF = [(C_GF + 128 * h, FF_GF + h) for h in range(4)] + [(C_RG + 128 * h, FF_RG + h) for h in range(4)] + \
      [(C_GG + 128 * h, FF_GG + h) for h in range(4)]
TMB = [(C_RV, 512, 0), (C_SV, 512, 512), (C_GI, 512, 1024), (C_VS, 128, 1536), (C_VW, 128, 1664)]
TM_RV, TM_SV, TM_GI, TM_VS, TM_VW = 0, 512, 1024, 1536, 1664
NTMC = 1792


def host_consts(S):
    c = {}
    f = np.zeros((128, 968), np.float32)
    f[:, 0:128] = np.eye(128)
    f[:, 128:256] = 1.0
    s_ = np.arange(128)
    f[:, 256:384] = (s_[:, None] >= s_[None, :]).astype(np.float32)
    half = 64
    inv = (10000.0 ** (-np.arange(half, dtype=np.float32) / half)).astype(np.float32)
    f[:, 384] = np.concatenate([inv, inv])
    sign = np.concatenate([-np.ones(64), np.ones(64)]).astype(np.float32)
    f[:, 385] = sign
    f[:, 386] = -math.pi * sign
    f[:, 387] = -math.pi
    f[:, 388] = 1.0
    f[:, 389] = EPS
    f[:64, 392:456] = (np.arange(64)[:, None] <= np.arange(64)[None, :]).astype(np.float32)
    rm = np.ones(512, np.float32)
    rm[::64] = 0.0
    f[:, 456:968] = rm[None, :]
    c["cf32"] = f
    dec = np.zeros((128, 4, 5, 512), np.float32)
    ml = np.arange(128)[:, None].astype(np.float64)
    nl = np.arange(512)[None, :].astype(np.float64)
    for h in range(4):
        lg = math.log1p(-2.0 ** (-5.0 - h))
        dec[:, h, 0, :] = np.exp(lg * (nl - ml))
        for j in range(4):
            e = nl - (128 * j + ml)
            dec[:, h, 1 + j, :] = np.where(e >= 0, np.exp(lg * np.maximum(e, 0)), 0.0)
    c["dec"] = dec.reshape(128, -1)
    mk = np.zeros((128, 21, 512), np.float32)
    for j in range(4):
        mk[:, j, :] = (128 * j + ml < nl)
        mk[:, 4 + j, :] = (128 * j + ml <= nl)
    for i, dj in enumerate(range(-4, 4)):
        key = 128 * dj + ml
        mk[:, 8 + i, :] = (key <= nl) & (nl - key < 512)
    for i, dl in enumerate([0, 512, 1024, 1536, 2048]):
        mk[:, 16 + i, :] = (16 * ml + 31 <= dl + nl)
    c["masks"] = mk.reshape(128, -1).astype(ml_dtypes.bfloat16)
    NSL = S // 64
    NCMP = (S - 32) // 16 + 1
    NCT = (NCMP + 127) // 128
    ex = (np.arange(S)[None, :] // 64 == np.arange(128)[:, None]).astype(np.float32)
    c["ex"] = ex.astype(ml_dtypes.bfloat16)
    n_ = np.arange(NCT * 128)
    cs = n_ * 16
    ss = np.arange(NSL) * 64
    ov = np.minimum(cs[:, None] + 32, ss[None, :] + 64) - np.maximum(cs[:, None], ss[None, :])
    ov = np.clip(ov, 0, None) / 32.0
    ov[n_ >= NCMP] = 0.0
    c["ovl"] = ov.reshape(NCT, 128, NSL).transpose(1, 0, 2).reshape(128, NCT * NSL).astype(ml_dtypes.bfloat16)
    t_ = np.arange(S)[:, None]
    j_ = np.arange(NSL)[None, :]
    cur = t_ // 64
    future = j_ > cur
    A = np.ones((S, NSL), np.float32)
    B = np.zeros((S, NSL), np.float32)
    for cond, val in ((j_ == 0, 1e4), (j_ == cur, 1e4 + 1), (j_ == cur - 1, 1e4 + 2)):
        cond = np.broadcast_to(cond, (S, NSL))
        A[cond] = 0.0
        B[cond] = val
    fut = np.broadcast_to(future, (S, NSL))
    A[fut] = 0.0
    B[fut] = -1.0
    c["impA"] = A
    c["impB"] = B
    cs_ = np.zeros((128, 16 + 512), np.float32)
    for h in range(4):
        cs_[:, h * 4 + h] = 1.0
        cs_[h, 16 + h * 128:16 + (h + 1) * 128] = 1.0
    c["csel"] = cs_
    return c


class MK:
    def __init__(self, S, L, do_hg=True, do_nsa=True):
        self.S, self.L = S, L
        self.NT = S // 512
        self.do_hg, self.do_nsa = do_hg, do_nsa
        nc = bass.Bass("TRN2", target_bir_lowering=False)
        self.nc = nc
        p = self.p = Prog(nc)
        NT = self.NT
        di = lambda n, sh, dt=F32: p.dram(n, sh, dt, "ExternalInput")
        self.x = di("x", [S, D])
        self.mem = di("mem", [256, D])
        self.pos = di("positions", [1, S], I32)
        self.g_mix = di("mix_norm", [L * 16, 128])
        self.w_in = di("w_in", [L * D, NCOL])
        self.g_ret = di("ret_norm", [L * 4, 128])
        self.lbl = di("hgrn_lb_logits", [L * 4, 128])
        self.g_hg = di("hgrn_norm", [L * 4, 128])
        self.pos_k = di("nsa_pos_k", [L * 32, 128])
        self.pos_v = di("nsa_pos_v", [L * 32, 128])
        self.w_ck = di("nsa_w_ck", [L * 32 * 128, 128])
        self.w_cv = di("nsa_w_cv", [L * 32 * 128, 128])
        self.w_out = di("w_out", [L * D, D])
        self.g_xa = di("xattn_norm", [L * 16, 128])
        self.g_mem = di("mem_norm", [L * 16, 128])
        self.wq = di("xattn_wq", [L * D, 512])
        self.wk = di("xattn_wk", [L * D, 512])
        self.wv = di("xattn_wv", [L * D, 512])
        self.wo = di("xattn_wo", [L * 512, D])
        self.g_ffn = di("ffn_norm", [L * 16, 128])
        self.w_up = di("ffn_w_up", [L * D, 2 * DFF])
        self.cw = di("ffn_conv_w", [L * 3 * 88, 128])
        self.cb = di("ffn_conv_b", [L * 88, 128])
        self.w_dn = di("ffn_w_down", [L * DFF, D])
        self.g_fin = di("final_norm", [16, 128])
        self.cf32_d = di("cf32", [128, 968])
        self.dec_d = di("dec", [128, 4 * 5 * 512])
        self.masks_d = di("masks", [128, 21 * 512], BF16)
        self.NSL = S // 64
        self.NCMP = (S - 32) // 16 + 1
        self.NCT = (self.NCMP + 127) // 128
        self.ex_d = di("ex", [128, S], BF16)
        self.ovl_d = di("ovl", [128, self.NCT * self.NSL], BF16)
        self.impA_d = di("impA", [S, self.NSL])
        self.impB_d = di("impB", [S, self.NSL])
        self.csel_d = di("csel", [128, 528])
        self.out = p.dram("out", [S, D], F32, "ExternalOutput")
        ds = lambda n, sh, dt, nres=1: self._scratch(n, sh, dt, nres)
        self.xT = ds("xT", [D, S], F32, NT)
        self.cosT = ds("cosT", [128, S], F32)
        self.sinT = ds("sinT", [128, S], F32)
        self.wb_in_set = [ds("wb_in_a", [(55 + NROPE) * 128, KC * 128], BF16), ds("wb_in_b", [(55 + NROPE) * 128, KC * 128], BF16)]
        self.wb_out_set = [ds("wb_out_a", [16 * 128, KC * 128], BF16), ds("wb_out_b", [16 * 128, KC * 128], BF16)]
        self.wb_q_set = [ds("wb_q_a", [4 * 128, KC * 128], BF16), ds("wb_q_b", [4 * 128, KC * 128], BF16)]
        self.wb_k_set = [ds("wb_k_a", [4 * 128, KC * 128], BF16), ds("wb_k_b", [4 * 128, KC * 128], BF16)]
        self.wb_v_set = [ds("wb_v_a", [4 * 128, KC * 128], BF16), ds("wb_v_b", [4 * 128, KC * 128], BF16)]
        self.wb_o_set = [ds("wb_o_a", [16 * 128, 4 * 128], BF16), ds("wb_o_b", [16 * 128, 4 * 128], BF16)]
        self.wb_up_set = [ds("wb_up_a", [88 * 128, KC * 128], BF16), ds("wb_up_b", [88 * 128, KC * 128], BF16)]
        self.wb_dn_set = [ds("wb_dn_a", [16 * 128, 44 * 128], BF16), ds("wb_dn_b", [16 * 128, 44 * 128], BF16)]
        self.fmb = ds("fmb", [NFB * 128, S], BF16)
        self.fmf = ds("fmf", [NFF * 128, S], F32)
        self.ngT = ds("ngT", [12, S], F32)
        self.tmb = ds("tmb", [S, NTMC], BF16)
        self.mixT = ds("mixT", [D, S], BF16)
        self.cf = p.sb([128, 968], F32, "cf")
        p.dma("sp", self.cf[:], self.cf32_d[:], [self.cf32_d], [self.cf])
        self.ident = self.cf[:, 0:128]
        self.ones_f = self.cf[:, 128:256]
        self.uincl = self.cf[:, 256:384]
        self.ones_b = p.sb([128, 128], BF16, "ones_b")
        p.copy("dve", self.ones_b[:], self.ones_f, [self.cf], [self.ones_b])
        self.ident_b = p.sb([128, 128], BF16, "ident_b")
        p.copy("dve", self.ident_b[:], self.ident, [self.cf], [self.ident_b])
        self.masks = p.sb([128, 4 * 512], BF16, "masks")
        p.dma("sp", self.masks[:], self.masks_d[:, 0:4 * 512], [self.masks_d], [self.masks])
        self.tmp_rows = p.sb([128, 128], F32, "tmp_rows")
        self.gains = p.sb([128, 88 + 4 * 88], F32, "gains")
        self.stg_f = [p.sb([128, 512], F32, f"stgf{i}") for i in range(4)]
        self.stg_b = [p.sb([128, 512], BF16, f"stgb{i}") for i in range(4)]
        self.memhat = p.sb([128, KC, 256], F32, "memhat")
        self.lb = p.sb([128, L * 4], F32, "lb")
        self.oml = p.sb([128, L * 4], F32, "oml")
        self.si = 0
        self.qi = 0
        npairs = 4 * sum(4 * qc + 6 for qc in range(self.NT))
        self.bg_every = max(1, npairs // 330)

    def _scratch(self, n, sh, dt, nres=1):
        import os
        if os.environ.get("MKDEBUG"):
            t = self.nc.dram_tensor(n, list(sh), dt, kind="ExternalOutput")
        else:
            t = self.nc.dram_tensor(n, list(sh), dt)
        tt = T(t.ap(), n, nres)
        return tt

    def phase(self):
        m = self.p.mark()
        self.psum = [self.p.ps([128, 512], F32, f"ps{i}") for i in range(8)]
        return m

    def mask(self, i):
        return self.masks[:, i * 512:(i + 1) * 512]

    def load_cols(self, src, r0, R, dst_ap, dst_t):
        p = self.p
        tmp = self.tmp_rows
        p.dma("sp", tmp[0:R, :], src[r0:r0 + R, :], [src], [tmp])
        ps = self.psum[7]
        p.op("pe", lambda e: e.transpose(ps[:, 0:R], tmp[0:R, :], self.ident[0:R, 0:R]), [tmp, self.cf], [ps])
        p.copy("dve", dst_ap, ps[:, 0:R], [ps], [dst_t])

    def precast_all(self, l, engs):
        ws = {nm: getattr(self, nm + "_set")[l % 2] for nm in
              ("wb_in", "wb_out", "wb_q", "wb_k", "wb_v", "wb_o", "wb_up", "wb_dn")}
        yield from self.precast(self.w_in, l * D, D, NCOL, ws["wb_in"], ext=ROPE_COLS, engs=engs)
        yield from self.precast(self.w_out, l * D, D, D, ws["wb_out"], engs=engs)
        yield from self.precast(self.wq, l * D, D, 512, ws["wb_q"], engs=engs)
        yield from self.precast(self.wk, l * D, D, 512, ws["wb_k"], engs=engs)
        yield from self.precast(self.wv, l * D, D, 512, ws["wb_v"], engs=engs)
        yield from self.precast(self.wo, l * 512, 512, D, ws["wb_o"], engs=engs)
        yield from self.precast(self.w_up, l * D, D, 2 * DFF, ws["wb_up"], engs=engs)
        yield from self.precast(self.w_dn, l * DFF, DFF, D, ws["wb_dn"], engs=engs)

    def precast(self, src, r0, K, N, dst, ext=None, engs=("act", "dve", "pool")):
        p = self.p
        CH = 2048
        i = 0
        for kt in range(K // 128):
            kc0 = kt * 128
            for c0 in range(0, N, CH):
                n = min(CH, N - c0)
                st = self.cast_f[i % 2]
                sb_ = self.cast_b[i % 2]
                p.dma("sp", st[:, 0:n], src[r0 + kt * 128:r0 + (kt + 1) * 128, c0:c0 + n], [src], [st])
                eng = engs[i % len(engs)]
                p.copy(eng, sb_[:, 0:n], st[:, 0:n], [st], [sb_])
                ct0 = c0 // 128
                nfull = n // 128
                if nfull:
                    p.dma("pool", dst[ct0 * 128:(ct0 + nfull) * 128, kc0:kc0 + 128].rearrange("(ct p) c -> p ct c", p=128),
                          sb_[:, 0:nfull * 128].rearrange("p (ct c) -> p ct c", c=128), [sb_], [dst])
                rem = n - nfull * 128
                if rem:
                    ctl = ct0 + nfull
                    p.dma("pool", dst[ctl * 128:(ctl + 1) * 128, kc0:kc0 + rem], sb_[:, nfull * 128:n], [sb_], [dst])
                if ext is not None:
                    for ri, rc in enumerate(ext):
                        if c0 <= rc and rc + 128 <= c0 + n:
                            et = 55 + ri
                            lo = rc - c0
                            p.dma("pool", dst[et * 128:(et + 1) * 128, kc0:kc0 + 64], sb_[:, lo + 64:lo + 128], [sb_], [dst])
                            p.dma("pool", dst[et * 128:(et + 1) * 128, kc0 + 64:kc0 + 128], sb_[:, lo:lo + 64], [sb_], [dst])
                        else:
                            assert not (rc < c0 + n and rc + 128 > c0), "rope head straddles cast chunk"
                i += 1
                yield

    def norm_tile(self, tt, gcols, hT, hoff):
        p = self.p
        xs = self.xs[tt % 2]
        p.dma("sp", xs[:], self.xT[:, tt * 512:(tt + 1) * 512].rearrange("(k p) n -> p k n", p=128),
              [self.xT.res[tt]], [xs])
        self.norm_from_sbuf(xs, gcols, hT, hoff)
        return xs

    def norm_from_sbuf(self, xs, gcols, hT, hoff, n=512):
        p = self.p
        sq = self.sq
        p.op("act", lambda e: e.activation(sq[:, :, 0:n], xs[:, :, 0:n], AF.Square), [xs], [sq])
        ps = self.psum[6]
        for k in range(KC):
            p.mm(ps[:, 0:n], self.ones_b[:], sq[:, k, 0:n], k == 0, k == KC - 1, [self.ones_b, sq], [ps])
        rs = self.rstd
        p.op("act", lambda e: e.activation(rs[:, 0:n], ps[:, 0:n], AF.Sqrt, bias=self.cf[:, 389:390], scale=1.0 / D),
             [ps, self.cf], [rs])
        p.op("dve", lambda e: e.reciprocal(rs[:, 0:n], rs[:, 0:n]), [rs], [rs])
        for k in range(KC):
            p.stt(hT[:, k, hoff:hoff + n], xs[:, k, 0:n], gcols[:, k:k + 1], rs[:, 0:n], ALU.mult, ALU.mult,
                  [xs, rs, self.gains], [hT])

    def load_w(self, wb, c0, n, kchunks=KC, r0=0):
        p = self.p
        wt = self.wt[self.qi % len(self.wt)]
        self.qi += 1
        ct = c0 // 128
        assert c0 % 128 == 0
        src = wb[ct * 128:(ct + 1) * 128, 0:kchunks * 128].rearrange("p (k c) -> p k c", c=128)
        if n == 128:
            p.dma("sp", wt[:, 0:kchunks, :], src, [wb], [wt])
        else:
            p.dma("sp", wt[:, 0:kchunks, 0:n], src[:, :, 0:n], [wb], [wt])
        return wt

    def build(self):
        p, S, L, NT = self.p, self.S, self.L, self.NT
        mk0 = self.phase()
        ps = self.psum
        self.cast_f = [p.sb([128, 2048], F32, f"castf{i}") for i in range(2)]
        self.xs = [p.sb([128, KC, 512], F32, f"xs{i}") for i in range(2)]
        G = self.gains

        x4 = self.xs[0]
        for tt in range(NT):
            xin = self.xs[0]
            p.dma("sp", xin[:].rearrange("p k (j c) -> p j (k c)", j=4)[:, :, :] if False else
                  xin[:].rearrange("p k n -> p (k n)").rearrange("p (j f) -> p j f", j=4),
                  self.x[tt * 512:(tt + 1) * 512, :].rearrange("(j p) f -> p j f", p=128), [self.x], [xin])
            xv = xin[:].rearrange("p k n -> p (k n)").rearrange("p (j f) -> p j f", j=4)
            xo = self.xs[1]
            for k in range(KC):
                pk = ps[k % 4]
                for j in range(4):
                    p.op("pe", lambda e, pk=pk, j=j, k=k: e.transpose(pk[:, j * 128:(j + 1) * 128],
                                                                       xv[:, j, k * 128:(k + 1) * 128], self.ident),
                         [xin, self.cf], [pk])
                p.copy("act" if k % 2 else "dve", xo[:, k, :], pk[:], [pk], [xo])
            p.dma("pool", self.xT[:, tt * 512:(tt + 1) * 512].rearrange("(k p) n -> p k n", p=128), xo[:],
                  [xo], [self.xT.res[tt]])

        posi = p.sb([128, 512], I32, "posi")
        self.stg_rope = [p.sb([128, 512], F32, f"stgr{i}") for i in range(4)]
        for tt in range(NT):
            p.dma("sp", posi[:], self.pos[0:1, tt * 512:(tt + 1) * 512].to_broadcast([128, 512]), [self.pos], [posi])
            pf = self.stg_rope[0]
            p.copy("dve", pf[:], posi[:], [posi], [pf])
            for which, (dst, shift) in enumerate(((self.cosT, 0.5 * math.pi), (self.sinT, 0.0))):
                a = self.stg_rope[1 + which]
                nf = self.stg_rope[3]
                p.ts("dve", a[:], pf[:], self.cf[:, 384:385], shift, ALU.mult, ALU.add, [pf, self.cf], [a])
                p.ts("dve", nf[:], a[:], 1.0 / (2.0 * math.pi), None, ALU.mult, None, [a], [nf])
                p.copy("dve", posi[:], nf[:], [nf], [posi])
                p.copy("dve", nf[:], posi[:], [posi], [nf])
                p.stt(a[:], nf[:], -2.0 * math.pi, a[:], ALU.mult, ALU.add, [nf, a], [a])
                p.ts("dve", nf[:], a[:], math.pi, 2.0 * math.pi, ALU.is_gt, ALU.mult, [a], [nf])
                p.tt("dve", a[:], a[:], nf[:], ALU.subtract, [a, nf], [a])
                p.ts("dve", nf[:], a[:], -math.pi, 2.0 * math.pi, ALU.is_lt, ALU.mult, [a], [nf])
                p.tt("dve", a[:], a[:], nf[:], ALU.add, [a, nf], [a])
                if which == 0:
                    p.op("act", lambda e, a=a: e.activation(a[:], a[:], AF.Sin), [a], [a])
                else:
                    p.op("act", lambda e, a=a: e.activation(a[:], a[:], AF.Sin, scale=self.cf[:, 385:386]),
                         [a, self.cf], [a])
                p.dma("pool", dst[:, tt * 512:(tt + 1) * 512], a[:], [a], [dst])

        L4 = L * 4
        lbe = self.stg_rope[0]
        self.load_cols(self.lbl, 0, L4, lbe[:, 0:L4], lbe)
        p.op("act", lambda e: e.activation(lbe[:, 0:L4], lbe[:, 0:L4], AF.Exp), [lbe], [lbe])
        ssum = self.stg_rope[1]
        p.copy("dve", ssum[:, 0:4], lbe[:, 0:4], [lbe], [ssum])
        for l in range(1, L):
            p.tt("dve", ssum[:, 0:4], ssum[:, 0:4], lbe[:, l * 4:(l + 1) * 4], ALU.add, [ssum, lbe], [ssum])
        p.op("dve", lambda e: e.reciprocal(ssum[:, 0:4], ssum[:, 0:4]), [ssum], [ssum])
        for l in range(L):
            p.tt("dve", lbe[:, l * 4:(l + 1) * 4], lbe[:, l * 4:(l + 1) * 4], ssum[:, 0:4], ALU.mult, [ssum, lbe], [lbe])
        lb = self.lb
        p.memset("dve", lb[:, 0:4], 0.0, [lb])
        for l in range(1, L):
            p.tt("dve", lb[:, l * 4:(l + 1) * 4], lb[:, (l - 1) * 4:l * 4], lbe[:, l * 4:(l + 1) * 4], ALU.add, [lb, lbe], [lb])
        p.ts("dve", self.oml[:], lb[:], -1.0, 1.0, ALU.mult, ALU.add, [lb], [self.oml])
        p.release(mk0)
        import os
        if os.environ.get("MKSTOP") == "p0":
            p.finish([self.xT, self.cosT, self.sinT])
            p.build()
            return self.nc
        for l in range(L):
            self.layer(l)

        mf = self.phase()
        ps = self.psum
        self.cast_f = [p.sb([128, 2048], F32, f"castf{i}") for i in range(2)]
        self.xs = [p.sb([128, KC, 512], F32, f"xs{i}") for i in range(2)]
        self.sq = p.sb([128, KC, 512], BF16, "sq")
        self.rstd = p.sb([128, 512], F32, "rstd")
        self.hT = p.sb([128, KC, 512], BF16, "hT")
        self.load_cols(self.g_fin, 0, 16, G[:, 0:16], G)
        for tt in range(NT):
            hT = self.hT
            self.norm_tile(tt, G[:, 0:16], hT, 0)
            xs = self.xs[tt % 2]
            yo = self.xs[(tt + 1) % 2]
            for k in range(KC):
                p.stt(yo[:, k, :], xs[:, k, :], G[:, k:k + 1], self.rstd[:], ALU.mult, ALU.mult, [xs, self.rstd, G], [yo])
            import os
            if os.environ.get("MKDEBUG"):
                d = self._scratch(f"dbg_rstd{tt}", [128, 512], F32)
                p.dma("sp", d[:], self.rstd[:], [self.rstd], [d])
                p.finish([d])
            ot = self.sq
            for j in range(4):
                of = self.cast_f[j % 2]
                for k in range(KC):
                    pk = ps[k % 4]
                    p.op("pe", lambda e, pk=pk, j=j, k=k, yo=yo: e.transpose(pk[:, 0:128], yo[:, k, j * 128:(j + 1) * 128],
                                                                       self.ident), [yo, self.cf], [pk])
                    p.copy("act" if k % 2 else "dve", of[:, k * 128:(k + 1) * 128], pk[:, 0:128], [pk], [of])
                p.dma("pool", self.out[tt * 512 + j * 128: tt * 512 + (j + 1) * 128, :], of[:], [of], [self.out])
        p.finish([self.out])
        p.release(mf)
        p.build()
        return self.nc

    def stage(self, kind):
        self.si += 1
        return (self.stg_f if kind == "f" else self.stg_b)[self.si % 4]

    def fresh(self, kind):
        return self.p.sb([128, 512], F32 if kind == "f" else BF16)

    def layer(self, l):
        p, S, L, NT = self.p, self.S, self.L, self.NT
        G = self.gains
        m = self.phase()
        self.load_cols(self.g_mix, l * 16, 16, G[:, 0:16], G)
        self.load_cols(self.g_xa, l * 16, 16, G[:, 16:32], G)
        self.load_cols(self.g_mem, l * 16, 16, G[:, 32:48], G)
        self.load_cols(self.g_ffn, l * 16, 16, G[:, 48:64], G)
        self.load_cols(self.g_ret, l * 4, 4, G[:, 64:68], G)
        self.load_cols(self.g_hg, l * 4, 4, G[:, 68:72], G)
        for j in range(3):
            self.load_cols(self.cw, (l * 3 + j) * 88, 88, G[:, 88 + j * 88: 88 + (j + 1) * 88], G)
        self.load_cols(self.cb, l * 88, 88, G[:, 88 + 3 * 88: 88 + 4 * 88], G)
        for nm in ("wb_in", "wb_out", "wb_q", "wb_k", "wb_v", "wb_o", "wb_up", "wb_dn"):
            setattr(self, nm, getattr(self, nm + "_set")[l % 2])
        if l == 0:
            self.cast_f = [p.sb([128, 2048], F32, f"castf{i}") for i in range(2)]
            self.cast_b = [p.sb([128, 2048], BF16, f"castb{i}") for i in range(2)]
            for _ in self.precast_all(0, ("act", "dve", "pool")):
                pass
        p.release(m)

        def norm_bufs(ntok):
            self.xs = [p.sb([128, KC, 512], F32, "xs0")] * 2
            self.sq = p.sb([128, KC, 512], BF16, "sq")
            self.rstd = p.sb([128, 512], F32, "rstd")
            self.hT = p.sb([128, KC, ntok], BF16, "hT")
        m = self.phase()
        norm_bufs(1024)
        self.wt = [p.sb([128, KC, 128], BF16, f"wt{i}") for i in range(3)]
        self.cast_f = [p.sb([128, 1024], F32, f"rope{i}") for i in range(2)]
        self.in_proj(l)
        p.release(m)
        m = self.phase()
        self.big_b = [p.sb([128, S], BF16, f"bigb{i}") for i in range(2)]
        self.dec_sb = p.sb([128, 5 * 512], F32, "dec_sb")
        self.qbuf = [p.sb([128, 512], BF16, f"qb{i}") for i in range(4)]
        self.rwbuf = [p.sb([128, 512], BF16, f"rw{i}") for i in range(3)]
        self.retention(l)
        p.release(m)
        m = self.phase()
        self.big_b = [p.sb([128, S], BF16, f"bigb{i}") for i in range(2)]
        self.spsum = p.sb([128, 512], F32, "spsum")
        self.qbuf = [p.sb([128, 512], BF16, f"qb{i}") for i in range(4)]
        self.bg = None
        if l + 1 < L:
            self.cast_f = [p.sb([128, 2048], F32, f"castf{i}") for i in range(2)]
            self.cast_b = [p.sb([128, 2048], BF16, f"castb{i}") for i in range(2)]
            self.bg = self.precast_all(l + 1, ("dve",))
        self.stickbreak(l)
        if self.bg is not None:
            for _ in self.bg:
                pass
            self.bg = None
        p.release(m)
        m = self.phase()
        if self.do_hg:
            self.hgrn(l)
        else:
            self.zero_mix(1024, 1536)
        p.release(m)
        m = self.phase()
        if self.do_nsa:
            self.nsa(l)
        else:
            self.zero_mix(1536, 2048)
        p.release(m)
        m = self.phase()
        self.hT = p.sb([128, KC, 1024], BF16, "hT")
        self.wt = [p.sb([128, KC, 128], BF16, f"wt{i}") for i in range(3)]
        self.out_proj(l)
        self.snap("dbg_x1")
        p.release(m)
        m = self.phase()
        norm_bufs(1024)
        self.wt = [p.sb([128, KC, 128], BF16, f"wt{i}") for i in range(3)]
        self.cast_f = [p.sb([128, 2048], F32, f"memin{i}") for i in range(2)]
        self.memT = p.sb([128, KC, 256], BF16, "memT")
        self.kTm = p.sb([128, 4, 256], BF16, "kTm")
        self.vm = p.sb([128, 2, 512], BF16, "vm")
        self.oT = p.sb([128, 4, 1024], BF16, "oT")
        self.qbuf = [p.sb([128, 512], BF16, f"qb{i}") for i in range(2)]
        self.xattn(l)
        self.snap("dbg_x2")
        p.release(m)
        m = self.phase()
        self.carry = p.sb([128, 88, 2], F32, "carry")
        self.ffn(l)
        self.snap("dbg_x3")
        p.release(m)

    def snap(self, name):
        import os
        if not os.environ.get("MKDEBUG"):
            return
        p = self.p
        d = self._scratch(name + f"_{self.p.nalloc}", [D, self.S], F32)
        self.p.nalloc += 1
        self.dbg = getattr(self, "dbg", {})
        self.dbg[name] = d
        for tt in range(self.NT):
            p.dma("sp", d[:, tt * 512:(tt + 1) * 512], self.xT[:, tt * 512:(tt + 1) * 512], [self.xT.res[tt]], [d])
        p.finish([d])

    def zero_mix(self, r0, r1):
        p = self.p
        z = self.stg_b[0]
        p.memset("dve", z[:], 0.0, [z])
        for r in range(r0, r1, 128):
            for tt in range(self.NT):
                p.dma("pool", self.mixT[r:r + 128, tt * 512:(tt + 1) * 512], z[:], [z], [self.mixT])

    def in_proj(self, l):
        p, S, NT = self.p, self.S, self.NT
        ps = self.psum
        G = self.gains
        hT = self.hT
        for st in range(NT // 2):
            for j in range(2):
                self.norm_tile(st * 2 + j, G[:, 0:16], hT, j * 512)
            cs = [p.sb([128, 512], F32, f"cs{st}_{i}") for i in range(0)]
            cos_t = self.cast_f[0]
            sin_t = self.cast_f[1]
            p.dma("sp", cos_t[:, 0:1024], self.cosT[:, st * 1024:(st + 1) * 1024], [self.cosT], [cos_t])
            p.dma("sp", sin_t[:, 0:1024], self.sinT[:, st * 1024:(st + 1) * 1024], [self.sinT], [sin_t])
            for ri, c0 in enumerate(ROPE_COLS):
                wa = self.load_w(self.wb_in, c0, 128)
                wb = self.load_w(self.wb_in, (55 + ri) * 128, 128)
                for j in range(2):
                    pa, pb = ps[(2 * j) % 4], ps[(2 * j + 1) % 4]
                    for k in range(KC):
                        p.mm(pa[:], wa[:, k, :], hT[:, k, j * 512:(j + 1) * 512], k == 0, k == KC - 1, [wa, hT], [pa])
                    for k in range(KC):
                        p.mm(pb[:], wb[:, k, :], hT[:, k, j * 512:(j + 1) * 512], k == 0, k == KC - 1, [wb, hT], [pb])
                    t1 = self.stage("f")
                    t2 = self.stage("f")
                    ob = self.stage("b")
                    p.tt("dve", t1[:], pa[:], cos_t[:, j * 512:(j + 1) * 512], ALU.mult, [pa, cos_t], [t1])
                    p.tt("dve", t2[:], pb[:], sin_t[:, j * 512:(j + 1) * 512], ALU.mult, [pb, sin_t], [t2])
                    p.tt("pool", ob[:], t1[:], t2[:], ALU.add, [t1, t2], [ob])
                    tt = st * 2 + j
                    p.dma("pool", self.fmb[ri * 128:(ri + 1) * 128, tt * 512:(tt + 1) * 512], ob[:], [ob], [self.fmb])
            for (c0, idx), kind in [(x_, "b") for x_ in FMB_PLAIN] + [(x_, "f") for x_ in FMF] + [((C_NG, 0), "g")]:
                ncol = 12 if kind == "g" else 128
                wa = self.load_w(self.wb_in, c0, ncol)
                for j in range(2):
                    pa = ps[j % 4]
                    for k in range(KC):
                        p.mm(pa[0:ncol, :], wa[:, k, 0:ncol], hT[:, k, j * 512:(j + 1) * 512], k == 0, k == KC - 1,
                             [wa, hT], [pa])
                    tt = st * 2 + j
                    if kind == "b":
                        ob = self.stage("b")
                        p.copy("act", ob[:], pa[:], [pa], [ob])
                        p.dma("pool", self.fmb[idx * 128:(idx + 1) * 128, tt * 512:(tt + 1) * 512], ob[:], [ob], [self.fmb])
                    elif kind == "f":
                        of = self.stage("f")
                        p.copy("act", of[:], pa[:], [pa], [of])
                        p.dma("pool", self.fmf[idx * 128:(idx + 1) * 128, tt * 512:(tt + 1) * 512], of[:], [of], [self.fmf])
                    else:
                        of = self.stage("f")
                        p.copy("act", of[0:12, :], pa[0:12, :], [pa], [of])
                        p.dma("pool", self.ngT[:, tt * 512:(tt + 1) * 512], of[0:12, :], [of], [self.ngT])
            for (c0, ncol, t0) in TMB:
                for cc in range(0, ncol, 128):
                    wa = self.load_w(self.wb_in, c0 + cc, 128)
                    for tk in range(8):
                        pa = ps[tk % 4]
                        for k in range(KC):
                            p.mm(pa[:, 0:128], hT[:, k, tk * 128:(tk + 1) * 128], wa[:, k, :], k == 0, k == KC - 1,
                                 [wa, hT], [pa])
                        ob = self.stage("b")
                        p.copy("act" if tk % 2 else "dve", ob[:, 0:128], pa[:, 0:128], [pa], [ob])
                        r0 = st * 1024 + tk * 128
                        p.dma("pool", self.tmb[r0:r0 + 128, t0 + cc:t0 + cc + 128], ob[:, 0:128], [ob], [self.tmb])

    def headnorm_gate(self, o_ps, center, gcol, gate_idx, mix_row, tt):
        p = self.p
        ps = self.psum
        o = self.stage("f")
        p.copy("act", o[:], o_ps[:], [o_ps], [o])
        st = ps[5]
        if center:
            p.mm(st[:], self.ones_f, o[:], True, True, [self.cf, o], [st])
            cen = self.stage("f")
            p.stt(cen[:], st[:], -1.0 / HD, o[:], ALU.mult, ALU.add, [st, o], [cen])
        else:
            cen = o
        sq = self.stage("f")
        p.op("act", lambda e: e.activation(sq[:], cen[:], AF.Square), [cen], [sq])
        p.mm(st[:], self.ones_f, sq[:], True, True, [self.cf, sq], [st])
        rs = self.stage("f")
        p.op("act", lambda e: e.activation(rs[:], st[:], AF.Sqrt, bias=self.cf[:, 389:390], scale=1.0 / HD),
             [st, self.cf], [rs])
        p.op("dve", lambda e: e.reciprocal(rs[:], rs[:]), [rs], [rs])
        y = sq
        p.stt(y[:], cen[:], gcol, rs[:], ALU.mult, ALU.mult, [cen, rs, self.gains], [y])
        g = self.stage("f")
        p.dma("sp", g[:], self.fmf[gate_idx * 128:(gate_idx + 1) * 128, tt * 512:(tt + 1) * 512], [self.fmf], [g])
        p.op("act", lambda e: e.activation(g[:], g[:], AF.Silu), [g], [g])
        ob = self.stage("b")
        p.tt("dve", ob[:], y[:], g[:], ALU.mult, [y, g], [ob])
        p.dma("pool", self.mixT[mix_row:mix_row + 128, tt * 512:(tt + 1) * 512], ob[:], [ob], [self.mixT])

    def load_head(self, k_idx, v_col):
        p, S = self.p, self.S
        kT = self.big_b[0]
        v = self.big_b[1]
        p.dma("sp", kT[:], self.fmb[k_idx * 128:(k_idx + 1) * 128, :], [self.fmb], [kT])
        p.dma("sp", v[:].rearrange("p (t e) -> p t e", e=128),
              self.tmb[:, v_col:v_col + 128].rearrange("(t p) e -> p t e", p=128), [self.tmb], [v])
        return kT, v

    def retention(self, l):
        p, S, NT = self.p, self.S, self.NT
        ps = self.psum
        dec = self.dec_sb
        for h in range(4):
            lg = math.log1p(-2.0 ** (-5.0 - h))
            kT, v = self.load_head(FB_RK + h, TM_RV + h * 128)
            p.dma("sp", dec[:], self.dec_d[:, h * 5 * 512:(h + 1) * 5 * 512], [self.dec_d], [dec])
            for qc in range(NT):
                qT = self.qbuf[qc % 2]
                p.dma("sp", qT[:], self.fmb[(FB_RQ + h) * 128:(FB_RQ + h + 1) * 128, qc * 512:(qc + 1) * 512],
                      [self.fmb], [qT])
                po = ps[4]
                kts = [kt for kt in range(4 * qc + 4)
                       if kt >= 4 * qc or lg * (512 * qc - 128 * kt - 127) > -85.0]
                n = len(kts)

                def rA(i):
                    kt = kts[i]
                    pa = ps[i % 3]
                    p.mm(pa[:], kT[:, kt * 128:(kt + 1) * 128], qT[:], True, True, [kT, qT], [pa])
                    w = self.rwbuf[i % 3]
                    if kt >= 4 * qc:
                        j = kt - 4 * qc
                        dt = dec[:, (1 + j) * 512:(2 + j) * 512]
                        c = SCALE
                    else:
                        dt = dec[:, 0:512]
                        c = SCALE * math.exp(lg * (512 * qc - 128 * kt))
                    p.stt(w[:], pa[:], c, dt, ALU.mult, ALU.mult, [pa, self.dec_sb], [w])

                def rE(i):
                    kt = kts[i]
                    w = self.rwbuf[i % 3]
                    p.mm(po[:], v[:, kt * 128:(kt + 1) * 128], w[:], i == 0, i == n - 1, [v, w], [po])

                for step in range(n + 2):
                    if step < n:
                        rA(step)
                    if 0 <= step - 2 < n:
                        rE(step - 2)
                self.headnorm_gate(po, True, self.gains[:, 64 + h:65 + h], FF_RG + h, h * 128, qc)

    def stickbreak(self, l):
        p, S, NT = self.p, self.S, self.NT
        ps = self.psum
        spsum = self.spsum
        ebuf = [p.sb([128, 512], F32, f"sbe{i}") for i in range(2)]
        spbuf = [p.sb([128, 512], F32, f"sbsp{i}") for i in range(3)]
        wbuf = [p.sb([128, 512], BF16, f"sbw{i}") for i in range(3)]
        for h in range(4):
            kT, v = self.load_head(FB_SK + h, TM_SV + h * 128)
            for qc in range(NT):
                qT = self.qbuf[qc % 2]
                p.dma("sp", qT[:], self.fmb[(FB_SQ + h) * 128:(FB_SQ + h + 1) * 128, qc * 512:(qc + 1) * 512],
                      [self.fmb], [qT])
                nqT = self.qbuf[2 + qc % 2]
                p.op("act", lambda e, nqT=nqT, qT=qT: e.mul(nqT[:], qT[:], -SCALE), [qT], [nqT])
                p.memset("pool", spsum[:], 0.0, [spsum])
                po = ps[7]
                kts = list(range(4 * qc + 3, -1, -1))
                n = len(kts)

                def stA(i):
                    kt = kts[i]
                    pa = ps[i % 3]
                    p.mm(pa[:], kT[:, kt * 128:(kt + 1) * 128], qT[:], True, True, [kT, qT], [pa])
                    e_ = ebuf[i % 2]
                    p.op("act", lambda e, e_=e_, pa=pa: e.activation(e_[:], pa[:], AF.Exp, scale=SCALE), [pa], [e_])
                    sp = spbuf[i % 3]
                    p.op("act", lambda e, e_=e_, sp=sp: e.activation(sp[:], e_[:], AF.Ln, bias=self.cf[:, 388:389], scale=1.0),
                         [e_, self.cf], [sp])
                    if kt >= 4 * qc:
                        p.tt("dve", sp[:], sp[:], self.mask(kt - 4 * qc), ALU.mult, [sp, self.masks], [sp])

                def stC(i):
                    kt = kts[i]
                    pc = ps[3 + i % 2]
                    sp = spbuf[i % 3]
                    p.mm(pc[:], self.uincl, sp[:], True, False, [self.cf, sp], [pc], sync=False)
                    p.mm(pc[:], self.ones_f, spsum[:], False, False, [self.cf, spsum], [pc], sync=False)
                    p.mm(pc[:], kT[:, kt * 128:(kt + 1) * 128], nqT[:], False, True, [kT, nqT], [pc])
                    w = wbuf[i % 3]
                    p.op("act", lambda e, w=w, pc=pc: e.activation(w[:], pc[:], AF.Exp, scale=-1.0), [pc], [w])
                    if kt >= 4 * qc:
                        p.tt("dve", w[:], w[:], self.mask(kt - 4 * qc), ALU.mult, [w, self.masks], [w])
                    p.tt("pool", spsum[:], spsum[:], sp[:], ALU.add, [spsum, sp], [spsum])

                def stE(i):
                    kt = kts[i]
                    w = wbuf[i % 3]
                    p.mm(po[:], v[:, kt * 128:(kt + 1) * 128], w[:], i == 0, i == n - 1, [v, w], [po])

                for step in range(n + 2):
                    if step < n:
                        stA(step)
                    if 0 <= step - 1 < n:
                        stC(step - 1)
                    if 0 <= step - 2 < n:
                        stE(step - 2)
                    if self.bg is not None and step % self.bg_every == 0:
                        try:
                            next(self.bg)
                        except StopIteration:
                            self.bg = None
                ob = self.stage("b")
                p.copy("act", ob[:], po[:], [po], [ob])
                p.dma("pool", self.mixT[512 + h * 128:512 + (h + 1) * 128, qc * 512:(qc + 1) * 512], ob[:], [ob], [self.mixT])

    def hgrn(self, l):
        p, S, NT = self.p, self.S, self.NT
        ps = self.psum
        f32t = lambda n: p.sb([128, 512], F32, n)
        gf, fk, lg, Gt, A1, A3, E1, E1n, E2, E3 = [f32t(n) for n in
                                                   ("gf", "fk", "lg", "Gt", "A1", "A3", "E1", "E1n", "E2", "E3")]
        kk = f32t("kk")
        kl = f32t("kl")
        qb, qg, qG, kg = [p.sb([128, 512], BF16, n) for n in ("qb", "qg", "qG", "kg")]
        klT = p.sb([64, 8, 128], BF16, "klT")
        vt = p.sb([64, 8, 128], BF16, "vt")
        Sf = p.sb([128, 128], F32, "Sf")
        Sb = p.sb([128, 128], BF16, "Sb")
        egl = p.sb([128, 8], F32, "egl")
        scs = [p.sb([64, 64], BF16, f"scs{i}") for i in range(2)]
        tri = self.cf[0:64, 392:456]
        rmask = self.cf[:, 456:968]
        v3 = lambda t: t[:].rearrange("p (c t) -> p c t", t=64)
        for h in range(4):
            col = l * 4 + h
            p.memset("dve", Sf[:], 0.0, [Sf])
            p.memset("pool", Sb[:], 0.0, [Sb])
            for tt in range(NT):
                cs = slice(tt * 512, (tt + 1) * 512)
                p.dma("sp", gf[:], self.fmf[(FF_GF + h) * 128:(FF_GF + h + 1) * 128, cs], [self.fmf], [gf])
                p.dma("sp", qb[:], self.fmb[(FB_GQ + h) * 128:(FB_GQ + h + 1) * 128, cs], [self.fmb], [qb])
                p.dma("sp", vt[:], self.tmb[tt * 512:(tt + 1) * 512, TM_GI + h * 128:TM_GI + (h + 1) * 128]
                      .rearrange("(c m) e -> m c e", m=64), [self.tmb], [vt])
                p.op("act", lambda e: e.activation(gf[:], gf[:], AF.Sigmoid), [gf], [gf])
                p.ts("dve", fk[:], gf[:], self.oml[:, col:col + 1], self.lb[:, col:col + 1], ALU.mult, ALU.add,
                     [gf, self.oml, self.lb], [fk])
                p.ts("dve", kk[:], fk[:], -1.0, 1.0, ALU.mult, ALU.add, [fk], [kk])
                p.ts("dve", fk[:], fk[:], 1e-6, None, ALU.max, None, [fk], [fk])
                p.op("act", lambda e: e.activation(lg[:], fk[:], AF.Ln), [fk], [lg])
                p.op("dve", lambda e: e.tensor_tensor_scan(Gt[:], rmask, lg[:], 0.0, ALU.mult, ALU.add),
                     [self.cf, lg], [Gt])
                G3 = v3(Gt)
                p.tt("dve", v3(A1), G3, G3[:, :, 31:32].to_broadcast([128, 8, 64]), ALU.subtract, [Gt], [A1])
                p.tt("dve", v3(A3), G3, G3[:, :, 63:64].to_broadcast([128, 8, 64]), ALU.subtract, [Gt], [A3])
                p.op("act", lambda e: e.activation(E1[:], A1[:], AF.Exp), [A1], [E1])
                p.op("act", lambda e: e.activation(E1n[:], A1[:], AF.Exp, scale=-1.0), [A1], [E1n])
                p.op("act", lambda e: e.activation(E2[:], Gt[:], AF.Exp), [Gt], [E2])
                p.op("act", lambda e: e.activation(E3[:], A3[:], AF.Exp, scale=-1.0), [A3], [E3])
                p.op("act", lambda e: e.activation(egl[:].rearrange("p (c o) -> p c o", o=1), G3[:, :, 63:64], AF.Exp),
                     [Gt], [egl])
                p.stt(qg[:], qb[:], SCALE, E1[:], ALU.mult, ALU.mult, [qb, E1], [qg])
                p.stt(qG[:], qb[:], SCALE, E2[:], ALU.mult, ALU.mult, [qb, E2], [qG])
                p.tt("dve", kg[:], kk[:], E1n[:], ALU.mult, [kk, E1n], [kg])
                p.tt("pool", kl[:], kk[:], E3[:], ALU.mult, [kk, E3], [kl])
                for half in range(2):
                    pk = ps[half]
                    for c4 in range(4):
                        c = half * 4 + c4
                        p.op("pe", lambda e, pk=pk, c=c, c4=c4: e.transpose(pk[0:64, c4 * 128:(c4 + 1) * 128],
                                                                           kl[:, c * 64:(c + 1) * 64], self.ident),
                             [kl, self.cf], [pk])
                    p.copy("act", klT[:, half * 4:(half + 1) * 4, :],
                           pk[0:64, :].rearrange("p (c d) -> p c d", d=128), [pk], [klT])
                po = ps[4]
                for c in range(8):
                    cc = slice(c * 64, (c + 1) * 64)
                    psc = ps[2 + c % 2]
                    p.mm(psc[0:64, 0:64], kg[:, cc], qg[:, cc], True, True, [kg, qg], [psc])
                    sc = scs[c % 2]
                    p.tt("dve", sc[:], psc[0:64, 0:64], tri, ALU.mult, [psc, self.cf], [sc])
                    p.mm(po[:, cc], vt[:, c, :], sc[:], True, False, [vt, sc], [po], sync=False)
                    p.mm(po[:, cc], Sb[:], qG[:, cc], False, True, [Sb, qG], [po])
                    pS = ps[6 + c % 2]
                    p.mm(pS[:, 0:128], klT[:, c, :], vt[:, c, :], True, True, [klT, vt], [pS])
                    p.stt(Sf[:], Sf[:], egl[:, c:c + 1], pS[:, 0:128], ALU.mult, ALU.add, [Sf, egl, pS], [Sf])
                    p.copy("act", Sb[:], Sf[:], [Sf], [Sb])
                self.headnorm_gate(po, False, self.gains[:, 68 + h:69 + h], FF_GG + h, 1024 + h * 128, tt)

    def nsa(self, l):
        p, S, NT = self.p, self.S, self.NT
        ps = self.psum
        NSL, NCMP, NCT = self.NSL, self.NCMP, self.NCT
        kcmpT = p.sb([128, NCT * 128], BF16, "kcmpT")
        vcmp = p.sb([128, NCT, 128], BF16, "vcmp")
        p.memset("pool", kcmpT[:], 0.0, [kcmpT])
        p.memset("pool", vcmp[:], 0.0, [vcmp])
        m0 = p.mark()
        kcT = p.sb([128, S], BF16, "kcT")
        vcT = p.sb([128, S], BF16, "vcT")
        p.dma("sp", kcT[:], self.fmb[FB_KC * 128:(FB_KC + 1) * 128, :], [self.fmb], [kcT])
        p.dma("sp", vcT[:], self.fmb[FB_VC * 128:(FB_VC + 1) * 128, :], [self.fmb], [vcT])
        wst = p.sb([128, 32, 128], F32, "wst")
        wck = p.sb([128, 32, 128], BF16, "wck")
        wcv = p.sb([128, 32, 128], BF16, "wcv")
        for src, dst in ((self.w_ck, wck), (self.w_cv, wcv)):
            p.dma("sp", wst[:], src[l * 4096:(l + 1) * 4096, :].rearrange("(li d) e -> d li e", d=128), [src], [wst])
            p.copy("dve", dst[:], wst[:], [wst], [dst])
        posf = p.sb([128, 64], F32, "posf")
        posb = p.sb([128, 64], BF16, "posb")
        self.load_cols(self.pos_k, l * 32, 32, posf[:, 0:32], posf)
        self.load_cols(self.pos_v, l * 32, 32, posf[:, 32:64], posf)
        p.copy("dve", posb[:], posf[:], [posf], [posb])
        bk = p.sb([128, 1], F32, "bk")
        pb = ps[1]
        for li in range(32):
            p.mm(pb[:, 0:1], wck[:, li, :], posb[:, li:li + 1], li == 0, li == 31, [wck, posb], [pb])
        p.copy("dve", bk[:], pb[:, 0:1], [pb], [bk])
        for c0 in range(0, NCMP, 512):
            n = min(512, NCMP - c0)
            pa = ps[0]
            for li in range(32):
                st_ = li + 16 * c0
                p.mm(pa[:, 0:n], wck[:, li, :], kcT[:, st_:st_ + 16 * (n - 1) + 1:16], li == 0, li == 31, [wck, kcT], [pa])
            p.ts("dve", kcmpT[:, c0:c0 + n], pa[:, 0:n], bk[:, 0:1], None, ALU.add, None, [pa, bk], [kcmpT])
        bvr = p.sb([1, 128], BF16, "bvr")
        pr = ps[2]
        for li in range(32):
            p.mm(pr[0:1, 0:128], posb[:, 32 + li:33 + li], wcv[:, li, :], li == 0, li == 31, [posb, wcv], [pr])
        p.copy("dve", bvr[:], pr[0:1, 0:128], [pr], [bvr])
        for nt in range(NCT):
            cnt = min(128, NCMP - nt * 128)
            pa = ps[3]
            for li in range(32):
                st_ = li + 16 * nt * 128
                p.mm(pa[0:cnt, 0:128], vcT[:, st_:st_ + 16 * (cnt - 1) + 1:16], wcv[:, li, :], li == 0, False,
                     [vcT, wcv], [pa], sync=False)
            p.mm(pa[0:cnt, 0:128], self.ones_b[0:1, 0:cnt], bvr[0:1, :], False, True, [self.ones_b, bvr], [pa])
            p.copy("act", vcmp[0:cnt, nt, :], pa[0:cnt, 0:128], [pa], [vcmp])
        p.release(m0)
        ksT = p.sb([128, S], BF16, "ksT")
        kwT = p.sb([128, S], BF16, "kwT")
        vs = p.sb([128, S // 128, 128], BF16, "vs")
        vw = p.sb([128, S // 128, 128], BF16, "vw")
        p.dma("sp", ksT[:], self.fmb[FB_KS * 128:(FB_KS + 1) * 128, :], [self.fmb], [ksT])
        p.dma("sp", kwT[:], self.fmb[FB_KW * 128:(FB_KW + 1) * 128, :], [self.fmb], [kwT])
        p.dma("sp", vs[:], self.tmb[:, TM_VS:TM_VS + 128].rearrange("(t p) e -> p t e", p=128), [self.tmb], [vs])
        p.dma("sp", vw[:], self.tmb[:, TM_VW:TM_VW + 128].rearrange("(t p) e -> p t e", p=128), [self.tmb], [vw])
        ex = p.sb([128, S], BF16, "ex")
        p.dma("sp", ex[:], self.ex_d[:], [self.ex_d], [ex])
        mk = p.sb([128, 17 * 512], BF16, "mk2")
        p.dma("sp", mk[:], self.masks_d[:, 4 * 512:21 * 512], [self.masks_d], [mk])
        msk = lambda i: mk[:, (i - 4) * 512:(i - 3) * 512]
        ovl = p.sb([128, NCT, NSL], BF16, "ovl")
        p.dma("sp", ovl[:], self.ovl_d[:].rearrange("p (t j) -> p t j", j=NSL), [self.ovl_d], [ovl])
        csel = p.sb([128, 528], F32, "csel")
        p.dma("sp", csel[:], self.csel_d[:], [self.csel_d], [csel])
        onesel = p.sb([128, 16], BF16, "onesel")
        p.copy("dve", onesel[:], csel[:, 0:16], [csel], [onesel])
        acc = p.sb([128, 4, 512], F32, "acc")
        imp = p.sb([128, 4, NSL], F32, "imp")
        At = p.sb([128, 4, NSL], F32, "At")
        Bt = p.sb([128, 4, NSL], F32, "Bt")
        impF = p.sb([128, NSL], F32, "impF")
        wk1 = p.sb([128, NSL], F32, "wk1")
        wk2 = p.sb([128, NSL], F32, "wk2")
        sel = p.sb([128, NSL], F32, "sel")
        m8 = p.sb([128, 8], F32, "m8")
        selT = p.sb([128, 512], BF16, "selT")
        qh = p.sb([128, 4, 512], BF16, "qh")
        mfb = [p.sb([128, 512], BF16, f"mfb{i}") for i in range(2)]
        pbuf = [p.sb([128, 512], BF16, f"pbuf{i}") for i in range(4)]
        zc = p.sb([128, 4], F32, "zc")
        z4 = p.sb([4, 512], F32, "z4")
        g4 = p.sb([4, 512], F32, "g4")
        zr = p.sb([1, 512], F32, "zr")
        gr = p.sb([1, 512], F32, "gr")

        def branch4(qc, kts, kT, vT, maskfn, expand, gate_b, first):
            items = [(i, kt, h) for i, kt in enumerate(kts) for h in range(4)]
            nit = len(items)
            mstate = {}

            def bA(t):
                i, kt, h = items[t]
                if h == 0:
                    if expand:
                        pm = ps[7]
                        p.mm(pm[:], ex[0:NSL, kt * 128:(kt + 1) * 128], selT[0:NSL, :], True, True, [ex, selT], [pm])
                        Mf = mfb[i % 2]
                        mi = maskfn(kt)
                        if mi is not None:
                            p.tt("dve", Mf[:], pm[:], msk(mi), ALU.mult, [pm, mk], [Mf])
                        else:
                            p.copy("dve", Mf[:], pm[:], [pm], [Mf])
                        mstate[i] = (Mf[:], [Mf])
                    else:
                        mstate[i] = (msk(maskfn(kt)), [mk])
                Mfa, Mfr = mstate[i]
                pa = ps[5 + t % 2]
                p.mm(pa[:], kT[:, kt * 128:(kt + 1) * 128], qh[:, h, :], True, True, [kT, qh], [pa])
                P = pbuf[t % 4]
                p.op("act", lambda e, P=P, pa=pa: e.activation(P[:], pa[:], AF.Exp, scale=SCALE), [pa], [P])
                p.tt("pool" if h % 2 else "dve", P[:], P[:], Mfa, ALU.mult, [P] + Mfr, [P])

            def bE(t):
                i, kt, h = items[t]
                P = pbuf[t % 4]
                last = i == len(kts) - 1
                p.mm(ps[h][:], vT[:, kt, :], P[:], i == 0, last, [vT, P], [ps[h]])
                p.mm(ps[4][0:4, :], onesel[:, h * 4:(h + 1) * 4], P[:], t == 0, t == nit - 1,
                     [onesel, P], [ps[4]], sync=(t == nit - 1))

            for step in range(nit + 2):
                if step < nit:
                    bA(step)
                if 0 <= step - 2 < nit:
                    bE(step - 2)
            p.ts("dve", z4[:], ps[4][0:4, :], 1e-30, None, ALU.max, None, [ps[4]], [z4])
            p.op("dve", lambda e: e.reciprocal(z4[:], z4[:]), [z4], [z4])
            p.dma("sp", g4[:], self.ngT[gate_b:12:3, qc * 512:(qc + 1) * 512], [self.ngT], [g4])
            p.op("act", lambda e: e.activation(g4[:], g4[:], AF.Sigmoid), [g4], [g4])
            p.tt("dve", g4[:], g4[:], z4[:], ALU.mult, [g4, z4], [g4])
            for h in range(4):
                pcb = ps[5 + h % 2]
                p.mm(pcb[:], csel[0:4, 16 + h * 128:16 + (h + 1) * 128], g4[0:4, :], True, True, [csel, g4], [pcb])
                o = self.stage("f")
                p.copy("act", o[:], ps[h][:], [ps[h]], [o])
                if first:
                    p.tt("dve", acc[:, h, :], o[:], pcb[:], ALU.mult, [o, pcb], [acc])
                else:
                    p.tt("dve", o[:], o[:], pcb[:], ALU.mult, [o, pcb], [o])
                    p.tt("pool", acc[:, h, :], acc[:, h, :], o[:], ALU.add, [acc, o], [acc])

        for qc in range(NT):
            cs = slice(qc * 512, (qc + 1) * 512)
            for h in range(4):
                p.dma("sp", qh[:, h, :], self.fmb[(FB_NQ + h) * 128:(FB_NQ + h + 1) * 128, cs], [self.fmb], [qh])
            p.dma("sp", At[:], self.impA_d[qc * 512:(qc + 1) * 512, :].rearrange("(j p) n -> p j n", p=128), [self.impA_d], [At])
            p.dma("sp", Bt[:], self.impB_d[qc * 512:(qc + 1) * 512, :].rearrange("(j p) n -> p j n", p=128), [self.impB_d], [Bt])
            nts = [nt for nt in range(NCT) if 512 * qc - 2048 * nt >= 0]
            for h in range(4):
                po, pi, pc, pz = ps[0], ps[1], ps[2], ps[4]
                for i, nt in enumerate(nts):
                    last = i == len(nts) - 1
                    dl = 512 * qc - 2048 * nt
                    pa = ps[5 + i % 2]
                    p.mm(pa[:], kcmpT[:, nt * 128:(nt + 1) * 128], qh[:, h, :], True, True, [kcmpT, qh], [pa])
                    P = self.stage("b")
                    p.op("act", lambda e, P=P, pa=pa: e.activation(P[:], pa[:], AF.Exp, scale=SCALE), [pa], [P])
                    if dl < 2560:
                        p.tt("dve", P[:], P[:], msk(16 + dl // 512), ALU.mult, [P, mk], [P])
                    p.mm(po[:], vcmp[:, nt, :], P[:], i == 0, last, [vcmp, P], [po])
                    p.mm(pz[0:1, :], self.ones_b[:, 0:1], P[:], i == 0, last, [self.ones_b, P], [pz])
                    for jq in range(4):
                        p.mm(pi[:, jq * NSL:(jq + 1) * NSL], P[:, jq * 128:(jq + 1) * 128], ovl[:, nt, :],
                             i == 0 and jq == 0, last and jq == 3, [P, ovl], [pi], sync=(last and jq == 3))
                    for jq in range(4):
                        p.mm(pc[:, jq:jq + 1], P[:, jq * 128:(jq + 1) * 128], self.ones_b[:, 0:1],
                             i == 0 and jq == 0, last and jq == 3, [P, self.ones_b], [pc], sync=(last and jq == 3))
                p.ts("dve", zr[:], pz[0:1, :], 1e-30, None, ALU.max, None, [pz], [zr])
                p.op("dve", lambda e: e.reciprocal(zr[:], zr[:]), [zr], [zr])
                p.dma("sp", gr[:], self.ngT[h * 3:h * 3 + 1, cs], [self.ngT], [gr])
                p.op("act", lambda e: e.activation(gr[:], gr[:], AF.Sigmoid), [gr], [gr])
                p.tt("dve", gr[:], gr[:], zr[:], ALU.mult, [gr, zr], [gr])
                pcb = ps[3]
                p.mm(pcb[:], self.ones_f[0:1, :], gr[0:1, :], True, True, [self.cf, gr], [pcb])
                o = self.stage("f")
                p.copy("act", o[:], po[:], [po], [o])
                p.tt("dve", acc[:, h, :], o[:], pcb[:], ALU.mult, [o, pcb], [acc])
                p.ts("dve", zc[:], pc[:, 0:4], 1e-30, None, ALU.max, None, [pc], [zc])
                p.op("dve", lambda e: e.reciprocal(zc[:], zc[:]), [zc], [zc])
                for jq in range(4):
                    if h == 0:
                        p.ts("dve", imp[:, jq, :], pi[:, jq * NSL:(jq + 1) * NSL], zc[:, jq:jq + 1], None, ALU.mult, None,
                             [pi, zc], [imp])
                    else:
                        p.stt(imp[:, jq, :], pi[:, jq * NSL:(jq + 1) * NSL], zc[:, jq:jq + 1], imp[:, jq, :],
                              ALU.mult, ALU.add, [pi, zc, imp], [imp])
            pst = ps[3]
            for jq in range(4):
                p.tt("dve", impF[:], imp[:, jq, :], At[:, jq, :], ALU.mult, [imp, At], [impF])
                p.tt("dve", impF[:], impF[:], Bt[:, jq, :], ALU.add, [impF, Bt], [impF])
                p.op("dve", lambda e: e.max(out=m8[:], in_=impF[:]), [impF], [m8])
                p.op("dve", lambda e: e.match_replace(out=wk1[:], in_to_replace=m8[:], in_values=impF[:], imm_value=-1e9),
                     [m8, impF], [wk1])
                p.op("dve", lambda e: e.max(out=m8[:], in_=wk1[:]), [wk1], [m8])
                p.op("dve", lambda e: e.match_replace(out=wk2[:], in_to_replace=m8[:], in_values=wk1[:], imm_value=-1e9),
                     [m8, wk1], [wk2])
                p.tt("dve", sel[:], wk2[:], impF[:], ALU.not_equal, [wk2, impF], [sel])
                p.op("pe", lambda e, jq=jq: e.transpose(pst[0:NSL, jq * 128:(jq + 1) * 128], sel[:], self.ident),
                     [sel, self.cf], [pst])
            p.copy("act", selT[0:NSL, :], pst[0:NSL, :], [pst], [selT])
            branch4(qc, list(range(4 * qc + 4)), ksT, vs,
                    lambda kt, qc=qc: (4 + kt - 4 * qc) if kt >= 4 * qc else None, True, 1, False)
            branch4(qc, list(range(max(0, 4 * qc - 4), 4 * qc + 4)), kwT, vw,
                    lambda kt, qc=qc: 8 + (kt - 4 * qc + 4), False, 2, False)
            for h in range(4):
                ob = self.stage("b")
                p.copy("act", ob[:], acc[:, h, :], [acc], [ob])
                p.dma("pool", self.mixT[1536 + h * 128:1536 + (h + 1) * 128, cs], ob[:], [ob], [self.mixT])

    def out_proj(self, l):
        p, S, NT = self.p, self.S, self.NT
        ps = self.psum
        hT = self.hT
        for st in range(NT // 2):
            p.dma("sp", hT[:], self.mixT[:, st * 1024:(st + 1) * 1024].rearrange("(k p) n -> p k n", p=128),
                  [self.mixT], [hT])
            for ct in range(16):
                wa = self.load_w(self.wb_out, ct * 128, 128)
                for j in range(2):
                    tt = st * 2 + j
                    pa = ps[(2 * ct + j) % 8]
                    for k in range(KC):
                        p.mm(pa[:], wa[:, k, :], hT[:, k, j * 512:(j + 1) * 512], k == 0, k == KC - 1, [wa, hT], [pa])
                    xo = self.stage("f")
                    p.dma("sp", xo[:], self.xT[ct * 128:(ct + 1) * 128, tt * 512:(tt + 1) * 512], [self.xT.res[tt]], [xo])
                    p.tt("dve", xo[:], xo[:], pa[:], ALU.add, [xo, pa], [xo])
                    p.dma("pool", self.xT[ct * 128:(ct + 1) * 128, tt * 512:(tt + 1) * 512], xo[:], [xo], [self.xT.res[tt]])

    def xattn(self, l):
        p, S, NT = self.p, self.S, self.NT
        ps = self.psum
        G = self.gains
        hT = self.hT
        mT = self.memT
        if l == 0:
            mh = self.memhat
            for t in range(2):
                mi = self.cast_f[t]
                p.dma("sp", mi[:], self.mem[t * 128:(t + 1) * 128, :], [self.mem], [mi])
                ss = self.stage("f")
                junk = self.cast_f[1 - t] if False else self.xs[0]
                p.op("act", lambda e, mi=mi, ss=ss: e.activation(self.xs[1][:].rearrange("p k n -> p (k n)")[:, 0:2048],
                                                                   mi[:], AF.Square, accum_out=ss[:, 0:1]),
                     [mi], [ss, self.xs[1]])
                p.op("act", lambda e, ss=ss: e.activation(ss[:, 1:2], ss[:, 0:1], AF.Sqrt, bias=self.cf[:, 389:390],
                                                          scale=1.0 / D), [ss, self.cf], [ss])
                p.op("dve", lambda e, ss=ss: e.reciprocal(ss[:, 2:3], ss[:, 1:2]), [ss], [ss])
                p.ts("dve", mi[:], mi[:], ss[:, 2:3], None, ALU.mult, None, [mi, ss], [mi])
                for k in range(KC):
                    pk = ps[k % 4]
                    p.op("pe", lambda e, pk=pk, k=k, mi=mi: e.transpose(pk[:, 0:128], mi[:, k * 128:(k + 1) * 128], self.ident),
                         [mi, self.cf], [pk])
                    p.copy("dve", mh[:, k, t * 128:(t + 1) * 128], pk[:, 0:128], [pk], [mh])
        for k in range(KC):
            p.ts("dve", mT[:, k, :], self.memhat[:, k, :], G[:, 32 + k:33 + k], None, ALU.mult, None,
                 [self.memhat, G], [mT])
        kTm = self.kTm
        vm = self.vm
        for h in range(4):
            wa = self.load_w(self.wb_k, h * 128, 128)
            pa = ps[h % 4]
            for k in range(KC):
                p.mm(pa[:, 0:256], wa[:, k, :], mT[:, k, :], k == 0, k == KC - 1, [wa, mT], [pa])
            p.copy("act", kTm[:, h, :], pa[:, 0:256], [pa], [kTm])
        for cc in range(4):
            wa = self.load_w(self.wb_v, cc * 128, 128)
            for t in range(2):
                pa = ps[(cc * 2 + t) % 4]
                for k in range(KC):
                    p.mm(pa[:, 0:128], mT[:, k, t * 128:(t + 1) * 128], wa[:, k, :], k == 0, k == KC - 1, [wa, mT], [pa])
                p.copy("act", vm[:, t, cc * 128:(cc + 1) * 128], pa[:, 0:128], [pa], [vm])
        oT = self.oT
        for st in range(NT // 2):
            for j in range(2):
                self.norm_tile(st * 2 + j, G[:, 16:32], hT, j * 512)
            for h in range(4):
                wa = self.load_w(self.wb_q, h * 128, 128)
                for j in range(2):
                    pq = ps[0]
                    for k in range(KC):
                        p.mm(pq[:], wa[:, k, :], hT[:, k, j * 512:(j + 1) * 512], k == 0, k == KC - 1, [wa, hT], [pq])
                    qT = self.qbuf[(h * 2 + j) % 2]
                    p.copy("act", qT[:], pq[:], [pq], [qT])
                    po, pz = ps[4], ps[5]
                    for t in range(2):
                        pa = ps[1 + t]
                        p.mm(pa[:], kTm[:, h, t * 128:(t + 1) * 128], qT[:], True, True, [kTm, qT], [pa])
                        w = self.stage("b")
                        p.op("act", lambda e, w=w, pa=pa: e.activation(w[:], pa[:], AF.Exp, scale=SCALE), [pa], [w])
                        p.mm(po[:], vm[:, t, h * 128:(h + 1) * 128], w[:], t == 0, t == 1, [vm, w], [po])
                        p.mm(pz[:], self.ones_b[:], w[:], t == 0, t == 1, [self.ones_b, w], [pz])
                    rz = self.stage("f")
                    p.op("dve", lambda e, rz=rz, pz=pz: e.reciprocal(rz[:], pz[:]), [pz], [rz])
                    p.tt("dve", oT[:, h, j * 512:(j + 1) * 512], po[:], rz[:], ALU.mult, [po, rz], [oT])
            for ct in range(16):
                wa = self.load_w(self.wb_o, ct * 128, 128, kchunks=4)
                for j in range(2):
                    tt = st * 2 + j
                    pa = ps[(2 * ct + j) % 4]
                    for k in range(4):
                        p.mm(pa[:], wa[:, k, :], oT[:, k, j * 512:(j + 1) * 512], k == 0, k == 3, [wa, oT], [pa])
                    xo = self.stage("f")
                    p.dma("sp", xo[:], self.xT[ct * 128:(ct + 1) * 128, tt * 512:(tt + 1) * 512], [self.xT.res[tt]], [xo])
                    p.tt("dve", xo[:], xo[:], pa[:], ALU.add, [xo, pa], [xo])
                    p.dma("pool", self.xT[ct * 128:(ct + 1) * 128, tt * 512:(tt + 1) * 512], xo[:], [xo], [self.xT.res[tt]])

    def ffn(self, l):
        p, S, NT = self.p, self.S, self.NT
        ps = self.psum
        G = self.gains
        carry = self.carry
        p.memset("dve", carry[:], 0.0, [carry])
        CW = 88
        for st in range(NT // 2):
            ms = p.mark()
            hT = self.hT = p.sb([128, KC, 1024], BF16, "hT")
            mn = p.mark()
            self.xs = [p.sb([128, KC, 512], F32, "xs0")] * 2
            self.sq = p.sb([128, KC, 512], BF16, "sq")
            self.rstd = p.sb([128, 512], F32, "rstd")
            for j in range(2):
                self.norm_tile(st * 2 + j, G[:, 48:64], hT, j * 512)
            p.release(mn)
            aT = p.sb([128, 44, 1024], BF16, "aT")
            self.wt = [p.sb([128, 44, 128], BF16, f"wt{i}") for i in range(2)]
            ubuf = [p.sb([128, 1026], F32, f"ubuf{i}") for i in range(2)]
            cbuf = [p.sb([128, 1024], F32, f"cbuf{i}") for i in range(2)]
            for ct in range(44):
                res = []
                for gi, cti in enumerate((ct, ct + 44)):
                    wa = self.load_w(self.wb_up, cti * 128, 128)
                    ub = ubuf[gi]
                    p.copy("pool", ub[:, 0:2], carry[:, cti, :], [carry], [ub])
                    for j in range(2):
                        pa = ps[(2 * gi + j) % 4]
                        for k in range(KC):
                            p.mm(pa[:], wa[:, k, :], hT[:, k, j * 512:(j + 1) * 512], k == 0, k == KC - 1, [wa, hT], [pa])
                        p.copy("act", ub[:, 2 + j * 512:2 + (j + 1) * 512], pa[:], [pa], [ub])
                    p.copy("pool", carry[:, cti, :], ub[:, 1024:1026], [ub], [carry])
                    c = cbuf[gi]
                    p.ts("dve", c[:], ub[:, 0:1024], G[:, CW + cti:CW + cti + 1], G[:, CW + 3 * 88 + cti:CW + 3 * 88 + cti + 1],
                         ALU.mult, ALU.add, [ub, G], [c])
                    p.stt(c[:], ub[:, 1:1025], G[:, CW + 88 + cti:CW + 88 + cti + 1], c[:], ALU.mult, ALU.add, [ub, G, c], [c])
                    p.stt(c[:], ub[:, 2:1026], G[:, CW + 176 + cti:CW + 176 + cti + 1], c[:], ALU.mult, ALU.add, [ub, G, c], [c])
                    res.append(c)
                gte, val = res
                p.op("act", lambda e, gte=gte: e.activation(gte[:], gte[:], AF.Silu), [gte], [gte])
                p.tt("pool", aT[:, ct, :], gte[:], val[:], ALU.mult, [gte, val], [aT])
            for ct in range(16):
                wa = self.load_w(self.wb_dn, ct * 128, 128, kchunks=44)
                for j in range(2):
                    tt = st * 2 + j
                    pa = ps[4 + (2 * ct + j) % 4]
                    for k in range(44):
                        p.mm(pa[:], wa[:, k, :], aT[:, k, j * 512:(j + 1) * 512], k == 0, k == 43, [wa, aT], [pa])
                    xo = self.stage("f")
                    p.dma("sp", xo[:], self.xT[ct * 128:(ct + 1) * 128, tt * 512:(tt + 1) * 512], [self.xT.res[tt]], [xo])
                    p.tt("dve", xo[:], xo[:], pa[:], ALU.add, [xo, pa], [xo])
                    p.dma("pool", self.xT[ct * 128:(ct + 1) * 128, tt * 512:(tt + 1) * 512], xo[:], [xo], [self.xT.res[tt]])
            p.release(ms)


def build_nc(S, L):
    mk = MK(S, L)
    return mk.build()


_IN2D = {
    "x": lambda a: a.reshape(a.shape[1], D),
    "mem": lambda a: a.reshape(256, D),
    "positions": lambda a: a.reshape(1, -1),
    "mix_norm": lambda a: a.reshape(-1, 128),
    "w_in": lambda a: a.reshape(-1, NCOL),
    "ret_norm": lambda a: a.reshape(-1, 128),
    "hgrn_lb_logits": lambda a: a.reshape(-1, 128),
    "hgrn_norm": lambda a: a.reshape(-1, 128),
    "nsa_pos_k": lambda a: a.reshape(-1, 128),
    "nsa_pos_v": lambda a: a.reshape(-1, 128),
    "nsa_w_ck": lambda a: a.reshape(-1, 128),
    "nsa_w_cv": lambda a: a.reshape(-1, 128),
    "w_out": lambda a: a.reshape(-1, D),
    "xattn_norm": lambda a: a.reshape(-1, 128),
    "mem_norm": lambda a: a.reshape(-1, 128),
    "xattn_wq": lambda a: a.reshape(-1, 512),
    "xattn_wk": lambda a: a.reshape(-1, 512),
    "xattn_wv": lambda a: a.reshape(-1, 512),
    "xattn_wo": lambda a: a.reshape(-1, D),
    "ffn_norm": lambda a: a.reshape(-1, 128),
    "ffn_w_up": lambda a: a.reshape(-1, 2 * DFF),
    "ffn_conv_w": lambda a: a.reshape(-1, 128),
    "ffn_conv_b": lambda a: a.reshape(-1, 128),
    "ffn_w_down": lambda a: a.reshape(-1, D),
    "final_norm": lambda a: a.reshape(16, 128),
}


def kernel(**inputs):
    S = inputs["x"].shape[1]
    L = inputs["w_in"].shape[0]
    nc = build_nc(S, L)
    m = {}
    for k, f in _IN2D.items():
        m[k] = np.ascontiguousarray(f(np.asarray(inputs[k])))
    m.update(host_consts(S))
    import os
    if os.environ.get("MKTRACE"):
        res = run_bass_kernel_spmd(nc, [m], core_ids=[0], trace=True)
        print("EXEC_TIME_NS", res.exec_time_ns)
        global TRACE
        TRACE = res
    else:
        res = run_bass_kernel_spmd(nc, [m], core_ids=[0])
    global LAST
    LAST = res.results[0]
    return np.asarray(res.results[0]["out"]).reshape(1, S, D)
```

```python
import numpy as np
from contextlib import ExitStack
import concourse.bass as bass
import concourse.mybir as mybir
from concourse.bass_utils import run_bass_kernel_spmd

F32 = mybir.dt.float32
BF16 = mybir.dt.bfloat16
I32 = mybir.dt.int32
AF = mybir.ActivationFunctionType
ALU = mybir.AluOpType
AX = mybir.AxisListType

NSLOT = 12


class Res:
    __slots__ = ("name", "last_w", "readers")

    def __init__(self, name=""):
        self.name = name
        self.last_w = None
        self.readers = {}


class T:
    def __init__(self, ap, name, nres=1):
        self.ap = ap
        self.name = name
        self.res = [Res(f"{name}.{i}") for i in range(nres)]

    def __getitem__(self, idx):
        return self.ap[idx]

    @property
    def r(self):
        return self.res[0]


class Prog:
    ENG = ("pe", "act", "dve", "pool", "sp")

    def __init__(self, nc):
        self.nc = nc
        self.es = ExitStack()
        self.ops = {e: [] for e in self.ENG}
        self.sems = {}
        self.cnt = {}
        self.known = {e: {} for e in self.ENG}
        self.dma_i = {e: 0 for e in self.ENG}
        self.nalloc = 0
        self.scopes = []
        for e in ("pe", "act", "dve", "pool"):
            self._mksem(e)
        for q in ("sp", "act", "pool"):
            for s in range(NSLOT):
                self._mksem(("dma", q, s))

    def _mksem(self, key):
        nm = "s_" + "_".join(str(k) for k in (key if isinstance(key, tuple) else (key,)))
        self.sems[key] = self.es.enter_context(self.nc.semaphore(nm))
        self.cnt[key] = 0

    def sb(self, shape, dtype=F32, name=None, nres=1):
        self.nalloc += 1
        name = name or f"t{self.nalloc}"
        es = self.scopes[-1] if self.scopes else self.es
        t = es.enter_context(self.nc.sbuf_tensor(f"{name}_{self.nalloc}", list(shape), dtype))
        return T(t, name, nres)

    def mark(self):
        self.scopes.append(ExitStack())
        return len(self.scopes) - 1

    def release(self, mark):
        self.barrier()
        self.flush()
        while len(self.scopes) > mark:
            self.scopes.pop().close()

    def barrier(self):
        for e in self.ENG:
            waits = []
            for k, v in self.cnt.items():
                if v == 0 or (k == "pe" and e == "pe"):
                    continue
                if self.known[e].get(k, 0) < v:
                    self.known[e][k] = v
                    waits.append((k, v))
            if waits:
                self.ops[e].append((waits, None, None))

    def ps(self, shape, dtype=F32, name=None, nres=1):
        self.nalloc += 1
        name = name or f"p{self.nalloc}"
        es = self.scopes[-1] if self.scopes else self.es
        t = es.enter_context(self.nc.psum_tensor(f"{name}_{self.nalloc}", list(shape), dtype))
        return T(t, name, nres)

    def dram(self, name, shape, dtype, kind):
        t = self.nc.dram_tensor(name, list(shape), dtype, kind=kind)
        return T(t.ap(), name)

    def _deps(self, eng, reads, writes):
        deps = {}

        def add(kv):
            if kv is None:
                return
            k, v = kv
            if deps.get(k, 0) < v:
                deps[k] = v
        for r in reads:
            add(r.last_w)
        for w in writes:
            add(w.last_w)
            for k, v in w.readers.items():
                add((k, v))
        out = []
        kn = self.known[eng]
        for k, v in deps.items():
            if k == "pe" and eng == "pe":
                continue
            if kn.get(k, 0) >= v:
                continue
            kn[k] = v
            out.append((k, v))
        return out

    @staticmethod
    def _rl(x):
        out = []
        for i in x:
            if isinstance(i, T):
                out.extend(i.res)
            elif isinstance(i, Res):
                out.append(i)
            elif i is None:
                pass
            else:
                out.extend(Prog._rl(i))
        return out

    def op(self, eng, fn, reads=(), writes=(), sync=True):
        reads = self._rl(reads)
        writes = self._rl(writes)
        waits = self._deps(eng, reads, writes)
        if sync:
            self.cnt[eng] += 1
            v = self.cnt[eng]
            self.ops[eng].append((waits, fn, (eng, 1)))
        else:
            v = self.cnt[eng] + 1
            self.ops[eng].append((waits, fn, None))
        for r in reads:
            if r.readers.get(eng, 0) < v:
                r.readers[eng] = v
        for w in writes:
            w.last_w = (eng, v)
            w.readers = {}

    def dma(self, q, out_ap, in_ap, reads=(), writes=(), **kw):
        reads = self._rl(reads)
        writes = self._rl(writes)
        i = self.dma_i[q]
        self.dma_i[q] += 1
        key = ("dma", q, i % NSLOT)
        waits = self._deps(q, reads, writes)
        prev = self.cnt[key]
        if prev > 0 and self.known[q].get(key, 0) < prev:
            self.known[q][key] = prev
            waits.append((key, prev))
        self.cnt[key] += 16
        v = self.cnt[key]

        def fn(e, out_ap=out_ap, in_ap=in_ap, kw=kw):
            return e.dma_start(out=out_ap, in_=in_ap, **kw)
        self.ops[q].append((waits, fn, (key, 16)))
        for r in reads:
            if r.readers.get(key, 0) < v:
                r.readers[key] = v
        for w in writes:
            w.last_w = (key, v)
            w.readers = {}

    def finish(self, out_res):
        for r in self._rl(out_res):
            waits = self._deps("sp", [r], [])
            if waits:
                self.ops["sp"].append((waits, None, None))

    def flush(self):
        nc = self.nc
        sems = self.sems
        ops = self.ops
        if not any(ops[e] for e in self.ENG):
            return

        def replay(e, lst):
            for waits, fn, inc in lst:
                for k, v in waits:
                    e.wait_ge(sems[k], v)
                if fn is None:
                    continue
                ins = fn(e)
                if inc is not None:
                    ins.then_inc(sems[inc[0]], inc[1])

        with nc.Block() as block:
            @block.tensor
            def _(e):
                replay(e, ops["pe"])

            @block.scalar
            def _(e):
                replay(e, ops["act"])

            @block.vector
            def _(e):
                replay(e, ops["dve"])

            @block.gpsimd
            def _(e):
                replay(e, ops["pool"])

            @block.sync
            def _(e):
                replay(e, ops["sp"])
        self.nops = getattr(self, "nops", 0) + sum(len(v) for v in ops.values())
        self.ops = {e: [] for e in self.ENG}

    def build(self):
        self.flush()
        while self.scopes:
            self.scopes.pop().close()
        self.es.close()

    def mm(self, out, lhsT, rhs, start, stop, reads, writes, sync=None):
        if sync is None:
            sync = stop
        self.op("pe", lambda e: e.matmul(out, lhsT, rhs, start=start, stop=stop),
                reads, writes, sync=sync)

    def tr(self, out, in_, ident, reads, writes):
        self.op("pe", lambda e: e.transpose(out, in_, ident), reads, writes)

    def actf(self, out, in_, func, reads, writes, bias=None, scale=None, accum_out=None, eng="act"):
        kw = {}
        if bias is not None:
            kw["bias"] = bias
        if scale is not None:
            kw["scale"] = scale
        if accum_out is not None:
            kw["accum_out"] = accum_out
        self.op("act", lambda e: e.activation(out, in_, func, **kw), reads, writes)

    def tt(self, eng, out, in0, in1, op, reads, writes):
        self.op(eng, lambda e: e.tensor_tensor(out, in0, in1, op), reads, writes)

    def ts(self, eng, out, in0, s1, s2, op0, op1, reads, writes):
        if op1 is None:
            self.op(eng, lambda e: e.tensor_scalar(out, in0, s1, None, op0), reads, writes)
        else:
            self.op(eng, lambda e: e.tensor_scalar(out, in0, s1, s2, op0, op1), reads, writes)

    def stt(self, out, in0, scalar, in1, op0, op1, reads, writes):
        self.op("dve", lambda e: e.scalar_tensor_tensor(out, in0, scalar, in1, op0, op1), reads, writes)

    def copy(self, eng, out, in_, reads, writes):
        if eng == "act":
            self.op("act", lambda e: e.copy(out, in_), reads, writes)
        else:
            self.op(eng, lambda e: e.tensor_copy(out, in_), reads, writes)

    def memset(self, eng, ap, val, writes):
        self.op(eng, lambda e: e.memset(ap, val), (), writes)


import math
import ml_dtypes

D = 2048
KC = 16
HD = 128
DFF = 5632
NCOL = 6924
NROPE = 15
NCE = NCOL + NROPE * 128
EPS = 1e-6
SCALE = HD ** -0.5

C_RQ, C_RK, C_RV, C_RG = 0, 512, 1024, 1536
C_SQ, C_SK, C_SV = 2048, 2560, 3072
C_GQ, C_GF, C_GI, C_GG = 3584, 4096, 4608, 5120
C_NQ = 5632
C_KC, C_VC, C_KS, C_VS, C_KW, C_VW, C_NG = 6144, 6272, 6400, 6528, 6656, 6784, 6912
ROPE_COLS = [C_RQ + 128 * h for h in range(4)] + [C_RK + 128 * h for h in range(4)] + \
            [C_NQ + 128 * h for h in range(4)] + [C_KC, C_KS, C_KW]
FB_RQ, FB_RK, FB_NQ, FB_KC, FB_KS, FB_KW = 0, 4, 8, 12, 13, 14
FB_SQ, FB_SK, FB_GQ, FB_VC = 15, 19, 23, 27
NFB = 28
FMB_PLAIN = [(C_SQ + 128 * h, FB_SQ + h) for h in range(4)] + [(C_SK + 128 * h, FB_SK + h) for h in range(4)] + \
            [(C_GQ + 128 * h, FB_GQ + h) for h in range(4)] + [(C_VC, FB_VC)]
FF_GF, FF_RG, FF_GG = 0, 4, 8
NFF = 12
FMF = [(C_GF + 128 * h, FF_GF + h) for h in range(4)] + [(C_RG + 128 * h, FF_RG + h) for h in range(4)] + \
      [(C_GG + 128 * h, FF_GG + h) for h in range(4)]
TMB = [(C_RV, 512, 0), (C_SV, 512, 512), (C_GI, 512, 1024), (C_VS, 128, 1536), (C_VW, 128, 1664)]
TM_RV, TM_SV, TM_GI, TM_VS, TM_VW = 0, 512, 1024, 1536, 1664
NTMC = 1792


def host_consts(S):
    c = {}
    f = np.zeros((128, 968), np.float32)
    f[:, 0:128] = np.eye(128)
    f[:, 128:256] = 1.0
    s_ = np.arange(128)
    f[:, 256:384] = (s_[:, None] >= s_[None, :]).astype(np.float32)
    half = 64
    inv = (10000.0 ** (-np.arange(half, dtype=np.float32) / half)).astype(np.float32)
    f[:, 384] = np.concatenate([inv, inv])
    sign = np.concatenate([-np.ones(64), np.ones(64)]).astype(np.float32)
    f[:, 385] = sign
    f[:, 386] = -math.pi * sign
    f[:, 387] = -math.pi
    f[:, 388] = 1.0
    f[:, 389] = EPS
    f[:64, 392:456] = (np.arange(64)[:, None] <= np.arange(64)[None, :]).astype(np.float32)
    rm = np.ones(512, np.float32)
    rm[::64] = 0.0
    f[:, 456:968] = rm[None, :]
    c["cf32"] = f
    dec = np.zeros((128, 4, 5, 512), np.float32)
    ml = np.arange(128)[:, None].astype(np.float64)
    nl = np.arange(512)[None, :].astype(np.float64)
    for h in range(4):
        lg = math.log1p(-2.0 ** (-5.0 - h))
        dec[:, h, 0, :] = np.exp(lg * (nl - ml))
        for j in range(4):
            e = nl - (128 * j + ml)
            dec[:, h, 1 + j, :] = np.where(e >= 0, np.exp(lg * np.maximum(e, 0)), 0.0)
    c["dec"] = dec.reshape(128, -1)
    mk = np.zeros((128, 21, 512), np.float32)
    for j in range(4):
        mk[:, j, :] = (128 * j + ml < nl)
        mk[:, 4 + j, :] = (128 * j + ml <= nl)
    for i, dj in enumerate(range(-4, 4)):
        key = 128 * dj + ml
        mk[:, 8 + i, :] = (key <= nl) & (nl - key < 512)
    for i, dl in enumerate([0, 512, 1024, 1536, 2048]):
        mk[:, 16 + i, :] = (16 * ml + 31 <= dl + nl)
    c["masks"] = mk.reshape(128, -1).astype(ml_dtypes.bfloat16)
    NSL = S // 64
    NCMP = (S - 32) // 16 + 1
    NCT = (NCMP + 127) // 128
    ex = (np.arange(S)[None, :] // 64 == np.arange(128)[:, None]).astype(np.float32)
    c["ex"] = ex.astype(ml_dtypes.bfloat16)
    n_ = np.arange(NCT * 128)
    cs = n_ * 16
    ss = np.arange(NSL) * 64
    ov = np.minimum(cs[:, None] + 32, ss[None, :] + 64) - np.maximum(cs[:, None], ss[None, :])
    ov = np.clip(ov, 0, None) / 32.0
    ov[n_ >= NCMP] = 0.0
    c["ovl"] = ov.reshape(NCT, 128, NSL).transpose(1, 0, 2).reshape(128, NCT * NSL).astype(ml_dtypes.bfloat16)
    t_ = np.arange(S)[:, None]
    j_ = np.arange(NSL)[None, :]
    cur = t_ // 64
    future = j_ > cur
    A = np.ones((S, NSL), np.float32)
    B = np.zeros((S, NSL), np.float32)
    for cond, val in ((j_ == 0, 1e4), (j_ == cur, 1e4 + 1), (j_ == cur - 1, 1e4 + 2)):
        cond = np.broadcast_to(cond, (S, NSL))
        A[cond] = 0.0
        B[cond] = val
    fut = np.broadcast_to(future, (S, NSL))
    A[fut] = 0.0
    B[fut] = -1.0
    c["impA"] = A
    c["impB"] = B
    cs_ = np.zeros((128, 16 + 512), np.float32)
    for h in range(4):
        cs_[:, h * 4 + h] = 1.0
        cs_[h, 16 + h * 128:16 + (h + 1) * 128] = 1.0
    c["csel"] = cs_
    return c


class MK:
    def __init__(self, S, L, do_hg=True, do_nsa=True):
        self.S, self.L = S, L
        self.NT = S // 512
        self.do_hg, self.do_nsa = do_hg, do_nsa
        nc = bass.Bass("TRN2", target_bir_lowering=False)
        self.nc = nc
        p = self.p = Prog(nc)
        NT = self.NT
        di = lambda n, sh, dt=F32: p.dram(n, sh, dt, "ExternalInput")
        self.x = di("x", [S, D])
        self.mem = di("mem", [256, D])
        self.pos = di("positions", [1, S], I32)
        self.g_mix = di("mix_norm", [L * 16, 128])
        self.w_in = di("w_in", [L * D, NCOL])
        self.g_ret = di("ret_norm", [L * 4, 128])
        self.lbl = di("hgrn_lb_logits", [L * 4, 128])
        self.g_hg = di("hgrn_norm", [L * 4, 128])
        self.pos_k = di("nsa_pos_k", [L * 32, 128])
        self.pos_v = di("nsa_pos_v", [L * 32, 128])
        self.w_ck = di("nsa_w_ck", [L * 32 * 128, 128])
        self.w_cv = di("nsa_w_cv", [L * 32 * 128, 128])
        self.w_out = di("w_out", [L * D, D])
        self.g_xa = di("xattn_norm", [L * 16, 128])
        self.g_mem = di("mem_norm", [L * 16, 128])
        self.wq = di("xattn_wq", [L * D, 512])
        self.wk = di("xattn_wk", [L * D, 512])
        self.wv = di("xattn_wv", [L * D, 512])
        self.wo = di("xattn_wo", [L * 512, D])
        self.g_ffn = di("ffn_norm", [L * 16, 128])
        self.w_up = di("ffn_w_up", [L * D, 2 * DFF])
        self.cw = di("ffn_conv_w", [L * 3 * 88, 128])
        self.cb = di("ffn_conv_b", [L * 88, 128])
        self.w_dn = di("ffn_w_down", [L * DFF, D])
        self.g_fin = di("final_norm", [16, 128])
        self.cf32_d = di("cf32", [128, 968])
        self.dec_d = di("dec", [128, 4 * 5 * 512])
        self.masks_d = di("masks", [128, 21 * 512], BF16)
        self.NSL = S // 64
        self.NCMP = (S - 32) // 16 + 1
        self.NCT = (self.NCMP + 127) // 128
        self.ex_d = di("ex", [128, S], BF16)
        self.ovl_d = di("ovl", [128, self.NCT * self.NSL], BF16)
        self.impA_d = di("impA", [S, self.NSL])
        self.impB_d = di("impB", [S, self.NSL])
        self.csel_d = di("csel", [128, 528])
        self.out = p.dram("out", [S, D], F32, "ExternalOutput")
        ds = lambda n, sh, dt, nres=1: self._scratch(n, sh, dt, nres)
        self.xT = ds("xT", [D, S], F32, NT)
        self.cosT = ds("cosT", [128, S], F32)
        self.sinT = ds("sinT", [128, S], F32)
        self.wb_in_set = [ds("wb_in_a", [(55 + NROPE) * 128, KC * 128], BF16), ds("wb_in_b", [(55 + NROPE) * 128, KC * 128], BF16)]
        self.wb_out_set = [ds("wb_out_a", [16 * 128, KC * 128], BF16), ds("wb_out_b", [16 * 128, KC * 128], BF16)]
        self.wb_q_set = [ds("wb_q_a", [4 * 128, KC * 128], BF16), ds("wb_q_b", [4 * 128, KC * 128], BF16)]
        self.wb_k_set = [ds("wb_k_a", [4 * 128, KC * 128], BF16), ds("wb_k_b", [4 * 128, KC * 128], BF16)]
        self.wb_v_set = [ds("wb_v_a", [4 * 128, KC * 128], BF16), ds("wb_v_b", [4 * 128, KC * 128], BF16)]
        self.wb_o_set = [ds("wb_o_a", [16 * 128, 4 * 128], BF16), ds("wb_o_b", [16 * 128, 4 * 128], BF16)]
        self.wb_up_set = [ds("wb_up_a", [88 * 128, KC * 128], BF16), ds("wb_up_b", [88 * 128, KC * 128], BF16)]
        self.wb_dn_set = [ds("wb_dn_a", [16 * 128, 44 * 128], BF16), ds("wb_dn_b", [16 * 128, 44 * 128], BF16)]
        self.fmb = ds("fmb", [NFB * 128, S], BF16)
        self.fmf = ds("fmf", [NFF * 128, S], F32)
        self.ngT = ds("ngT", [12, S], F32)
        self.tmb = ds("tmb", [S, NTMC], BF16)
        self.mixT = ds("mixT", [D, S], BF16)
        self.cf = p.sb([128, 968], F32, "cf")
        p.dma("sp", self.cf[:], self.cf32_d[:], [self.cf32_d], [self.cf])
        self.ident = self.cf[:, 0:128]
        self.ones_f = self.cf[:, 128:256]
        self.uincl = self.cf[:, 256:384]
        self.ones_b = p.sb([128, 128], BF16, "ones_b")
        p.copy("dve", self.ones_b[:], self.ones_f, [self.cf], [self.ones_b])
        self.ident_b = p.sb([128, 128], BF16, "ident_b")
        p.copy("dve", self.ident_b[:], self.ident, [self.cf], [self.ident_b])
        self.perm_b = p.sb([128, 128], BF16, "perm_b")
        p.copy("dve", self.perm_b[:, 0:64], self.cf[:, 64:128], [self.cf], [self.perm_b])
        p.copy("dve", self.perm_b[:, 64:128], self.cf[:, 0:64], [self.cf], [self.perm_b])
        self.masks = p.sb([128, 4 * 512], BF16, "masks")
        p.dma("sp", self.masks[:], self.masks_d[:, 0:4 * 512], [self.masks_d], [self.masks])
        self.tmp_rows = p.sb([128, 128], F32, "tmp_rows")
        self.gains = p.sb([128, 88 + 4 * 88], F32, "gains")
        self.stg_f = [p.sb([128, 512], F32, f"stgf{i}") for i in range(4)]
        self.stg_b = [p.sb([128, 512], BF16, f"stgb{i}") for i in range(4)]
        self.memhat = p.sb([128, KC, 256], F32, "memhat")
        self.lb = p.sb([128, L * 4], F32, "lb")
        self.oml = p.sb([128, L * 4], F32, "oml")
        self.si = 0
        self.qi = 0
        npairs = 4 * sum(4 * qc + 6 for qc in range(self.NT))
        self.bg_every = max(1, npairs // 330)

    def _scratch(self, n, sh, dt, nres=1):
        import os
        if os.environ.get("MKDEBUG"):
            t = self.nc.dram_tensor(n, list(sh), dt, kind="ExternalOutput")
        else:
            t = self.nc.dram_tensor(n, list(sh), dt)
        tt = T(t.ap(), n, nres)
        return tt

    def phase(self):
        m = self.p.mark()
        self.psum = [self.p.ps([128, 512], F32, f"ps{i}") for i in range(8)]
        return m

    def mask(self, i):
        return self.masks[:, i * 512:(i + 1) * 512]

    def load_cols(self, src, r0, R, dst_ap, dst_t):
        p = self.p
        tmp = self.tmp_rows
        p.dma("sp", tmp[0:R, :], src[r0:r0 + R, :], [src], [tmp])
        ps = self.psum[7]
        p.op("pe", lambda e: e.transpose(ps[:, 0:R], tmp[0:R, :], self.ident[0:R, 0:R]), [tmp, self.cf], [ps])
        p.copy("dve", dst_ap, ps[:, 0:R], [ps], [dst_t])

    def precast_all(self, l, engs):
        ws = {nm: getattr(self, nm + "_set")[l % 2] for nm in
              ("wb_in", "wb_out", "wb_q", "wb_k", "wb_v", "wb_o", "wb_up", "wb_dn")}
        yield from self.precast(self.w_in, l * D, D, NCOL, ws["wb_in"], engs=engs)
        yield from self.precast(self.w_out, l * D, D, D, ws["wb_out"], engs=engs)
        yield from self.precast(self.wq, l * D, D, 512, ws["wb_q"], engs=engs)
        yield from self.precast(self.wk, l * D, D, 512, ws["wb_k"], engs=engs)
        yield from self.precast(self.wv, l * D, D, 512, ws["wb_v"], engs=engs)
        yield from self.precast(self.wo, l * 512, 512, D, ws["wb_o"], engs=engs)
        yield from self.precast(self.w_up, l * D, D, 2 * DFF, ws["wb_up"], engs=engs)
        yield from self.precast(self.w_dn, l * DFF, DFF, D, ws["wb_dn"], engs=engs)

    def precast(self, src, r0, K, N, dst, ext=None, engs=("act", "dve", "pool")):
        p = self.p
        CH = 2048
        i = 0
        for kt in range(K // 128):
            kc0 = kt * 128
            for c0 in range(0, N, CH):
                n = min(CH, N - c0)
                st = self.cast_f[i % 2]
                sb_ = self.cast_b[i % 2]
                p.dma("sp", st[:, 0:n], src[r0 + kt * 128:r0 + (kt + 1) * 128, c0:c0 + n], [src], [st])
                eng = engs[i % len(engs)]
                p.copy(eng, sb_[:, 0:n], st[:, 0:n], [st], [sb_])
                ct0 = c0 // 128
                nfull = n // 128
                if nfull:
                    p.dma("pool", dst[ct0 * 128:(ct0 + nfull) * 128, kc0:kc0 + 128].rearrange("(ct p) c -> p ct c", p=128),
                          sb_[:, 0:nfull * 128].rearrange("p (ct c) -> p ct c", c=128), [sb_], [dst])
                rem = n - nfull * 128
                if rem:
                    ctl = ct0 + nfull
                    p.dma("pool", dst[ctl * 128:(ctl + 1) * 128, kc0:kc0 + rem], sb_[:, nfull * 128:n], [sb_], [dst])
                if ext is not None:
                    for ri, rc in enumerate(ext):
                        if c0 <= rc and rc + 128 <= c0 + n:
                            et = 55 + ri
                            lo = rc - c0
                            p.dma("pool", dst[et * 128:(et + 1) * 128, kc0:kc0 + 64], sb_[:, lo + 64:lo + 128], [sb_], [dst])
                            p.dma("pool", dst[et * 128:(et + 1) * 128, kc0 + 64:kc0 + 128], sb_[:, lo:lo + 64], [sb_], [dst])
                        else:
                            assert not (rc < c0 + n and rc + 128 > c0), "rope head straddles cast chunk"
                i += 1
                yield

    def norm_tile(self, tt, gcols, hT, hoff):
        p = self.p
        xs = self.xs[tt % 2]
        p.dma("sp", xs[:], self.xT[:, tt * 512:(tt + 1) * 512].rearrange("(k p) n -> p k n", p=128),
              [self.xT.res[tt]], [xs])
        self.norm_from_sbuf(xs, gcols, hT, hoff)
        return xs

    def norm_from_sbuf(self, xs, gcols, hT, hoff, n=512):
        p = self.p
        sq = self.sq
        p.op("act", lambda e: e.activation(sq[:, :, 0:n], xs[:, :, 0:n], AF.Square), [xs], [sq])
        ps = self.psum[6]
        for k in range(KC):
            p.mm(ps[:, 0:n], self.ones_b[:], sq[:, k, 0:n], k == 0, k == KC - 1, [self.ones_b, sq], [ps])
        rs = self.rstd
        p.op("act", lambda e: e.activation(rs[:, 0:n], ps[:, 0:n], AF.Sqrt, bias=self.cf[:, 389:390], scale=1.0 / D),
             [ps, self.cf], [rs])
        p.op("dve", lambda e: e.reciprocal(rs[:, 0:n], rs[:, 0:n]), [rs], [rs])
        for k in range(KC):
            p.stt(hT[:, k, hoff:hoff + n], xs[:, k, 0:n], gcols[:, k:k + 1], rs[:, 0:n], ALU.mult, ALU.mult,
                  [xs, rs, self.gains], [hT])

    def load_w(self, wb, c0, n, kchunks=KC, r0=0):
        p = self.p
        wt = self.wt[self.qi % len(self.wt)]
        self.qi += 1
        ct = c0 // 128
        assert c0 % 128 == 0
        src = wb[ct * 128:(ct + 1) * 128, 0:kchunks * 128].rearrange("p (k c) -> p k c", c=128)
        if n == 128:
            p.dma("sp", wt[:, 0:kchunks, :], src, [wb], [wt])
        else:
            p.dma("sp", wt[:, 0:kchunks, 0:n], src[:, :, 0:n], [wb], [wt])
        return wt

    def build(self):
        p, S, L, NT = self.p, self.S, self.L, self.NT
        mk0 = self.phase()
        ps = self.psum
        self.cast_f = [p.sb([128, 2048], F32, f"castf{i}") for i in range(2)]
        self.xs = [p.sb([128, KC, 512], F32, f"xs{i}") for i in range(2)]
        G = self.gains

        x4 = self.xs[0]
        for tt in range(NT):
            xin = self.xs[0]
            p.dma("sp", xin[:].rearrange("p k (j c) -> p j (k c)", j=4)[:, :, :] if False else
                  xin[:].rearrange("p k n -> p (k n)").rearrange("p (j f) -> p j f", j=4),
                  self.x[tt * 512:(tt + 1) * 512, :].rearrange("(j p) f -> p j f", p=128), [self.x], [xin])
            xv = xin[:].rearrange("p k n -> p (k n)").rearrange("p (j f) -> p j f", j=4)
            xo = self.xs[1]
            for k in range(KC):
                pk = ps[k % 4]
                for j in range(4):
                    p.op("pe", lambda e, pk=pk, j=j, k=k: e.transpose(pk[:, j * 128:(j + 1) * 128],
                                                                       xv[:, j, k * 128:(k + 1) * 128], self.ident),
                         [xin, self.cf], [pk])
                p.copy("act" if k % 2 else "dve", xo[:, k, :], pk[:], [pk], [xo])
            p.dma("pool", self.xT[:, tt * 512:(tt + 1) * 512].rearrange("(k p) n -> p k n", p=128), xo[:],
                  [xo], [self.xT.res[tt]])

        posi = p.sb([128, 512], I32, "posi")
        self.stg_rope = [p.sb([128, 512], F32, f"stgr{i}") for i in range(4)]
        for tt in range(NT):
            p.dma("sp", posi[:], self.pos[0:1, tt * 512:(tt + 1) * 512].to_broadcast([128, 512]), [self.pos], [posi])
            pf = self.stg_rope[0]
            p.copy("dve", pf[:], posi[:], [posi], [pf])
            for which, (dst, shift) in enumerate(((self.cosT, 0.5 * math.pi), (self.sinT, 0.0))):
                a = self.stg_rope[1 + which]
                nf = self.stg_rope[3]
                p.ts("dve", a[:], pf[:], self.cf[:, 384:385], shift, ALU.mult, ALU.add, [pf, self.cf], [a])
                p.ts("dve", nf[:], a[:], 1.0 / (2.0 * math.pi), None, ALU.mult, None, [a], [nf])
                p.copy("dve", posi[:], nf[:], [nf], [posi])
                p.copy("dve", nf[:], posi[:], [posi], [nf])
                p.stt(a[:], nf[:], -2.0 * math.pi, a[:], ALU.mult, ALU.add, [nf, a], [a])
                p.ts("dve", nf[:], a[:], math.pi, 2.0 * math.pi, ALU.is_gt, ALU.mult, [a], [nf])
                p.tt("dve", a[:], a[:], nf[:], ALU.subtract, [a, nf], [a])
                p.ts("dve", nf[:], a[:], -math.pi, 2.0 * math.pi, ALU.is_lt, ALU.mult, [a], [nf])
                p.tt("dve", a[:], a[:], nf[:], ALU.add, [a, nf], [a])
                if which == 0:
                    p.op("act", lambda e, a=a: e.activation(a[:], a[:], AF.Sin), [a], [a])
                else:
                    p.op("act", lambda e, a=a: e.activation(a[:], a[:], AF.Sin, scale=self.cf[:, 385:386]),
                         [a, self.cf], [a])
                p.dma("pool", dst[:, tt * 512:(tt + 1) * 512], a[:], [a], [dst])

        L4 = L * 4
        lbe = self.stg_rope[0]
        self.load_cols(self.lbl, 0, L4, lbe[:, 0:L4], lbe)
        p.op("act", lambda e: e.activation(lbe[:, 0:L4], lbe[:, 0:L4], AF.Exp), [lbe], [lbe])
        ssum = self.stg_rope[1]
        p.copy("dve", ssum[:, 0:4], lbe[:, 0:4], [lbe], [ssum])
        for l in range(1, L):
            p.tt("dve", ssum[:, 0:4], ssum[:, 0:4], lbe[:, l * 4:(l + 1) * 4], ALU.add, [ssum, lbe], [ssum])
        p.op("dve", lambda e: e.reciprocal(ssum[:, 0:4], ssum[:, 0:4]), [ssum], [ssum])
        for l in range(L):
            p.tt("dve", lbe[:, l * 4:(l + 1) * 4], lbe[:, l * 4:(l + 1) * 4], ssum[:, 0:4], ALU.mult, [ssum, lbe], [lbe])
        lb = self.lb
        p.memset("dve", lb[:, 0:4], 0.0, [lb])
        for l in range(1, L):
            p.tt("dve", lb[:, l * 4:(l + 1) * 4], lb[:, (l - 1) * 4:l * 4], lbe[:, l * 4:(l + 1) * 4], ALU.add, [lb, lbe], [lb])
        p.ts("dve", self.oml[:], lb[:], -1.0, 1.0, ALU.mult, ALU.add, [lb], [self.oml])
        p.release(mk0)
        import os
        if os.environ.get("MKSTOP") == "p0":
            p.finish([self.xT, self.cosT, self.sinT])
            p.build()
            return self.nc
        for l in range(L):
            self.layer(l)

        mf = self.phase()
        ps = self.psum
        self.cast_f = [p.sb([128, 2048], F32, f"castf{i}") for i in range(2)]
        self.xs = [p.sb([128, KC, 512], F32, f"xs{i}") for i in range(2)]
        self.sq = p.sb([128, KC, 512], BF16, "sq")
        self.rstd = p.sb([128, 512], F32, "rstd")
        self.hT = p.sb([128, KC, 512], BF16, "hT")
        self.load_cols(self.g_fin, 0, 16, G[:, 0:16], G)
        for tt in range(NT):
            hT = self.hT
            self.norm_tile(tt, G[:, 0:16], hT, 0)
            xs = self.xs[tt % 2]
            yo = self.xs[(tt + 1) % 2]
            for k in range(KC):
                p.stt(yo[:, k, :], xs[:, k, :], G[:, k:k + 1], self.rstd[:], ALU.mult, ALU.mult, [xs, self.rstd, G], [yo])
            import os
            if os.environ.get("MKDEBUG"):
                d = self._scratch(f"dbg_rstd{tt}", [128, 512], F32)
                p.dma("sp", d[:], self.rstd[:], [self.rstd], [d])
                p.finish([d])
            ot = self.sq
            for j in range(4):
                of = self.cast_f[j % 2]
                for k in range(KC):
                    pk = ps[k % 4]
                    p.op("pe", lambda e, pk=pk, j=j, k=k, yo=yo: e.transpose(pk[:, 0:128], yo[:, k, j * 128:(j + 1) * 128],
                                                                       self.ident), [yo, self.cf], [pk])
                    p.copy("act" if k % 2 else "dve", of[:, k * 128:(k + 1) * 128], pk[:, 0:128], [pk], [of])
                p.dma("pool", self.out[tt * 512 + j * 128: tt * 512 + (j + 1) * 128, :], of[:], [of], [self.out])
        p.finish([self.out])
        p.release(mf)
        p.build()
        return self.nc

    def stage(self, kind):
        self.si += 1
        return (self.stg_f if kind == "f" else self.stg_b)[self.si % 4]

    def fresh(self, kind):
        return self.p.sb([128, 512], F32 if kind == "f" else BF16)

    def layer(self, l):
        p, S, L, NT = self.p, self.S, self.L, self.NT
        G = self.gains
        m = self.phase()
        self.load_cols(self.g_mix, l * 16, 16, G[:, 0:16], G)
        self.load_cols(self.g_xa, l * 16, 16, G[:, 16:32], G)
        self.load_cols(self.g_mem, l * 16, 16, G[:, 32:48], G)
        self.load_cols(self.g_ffn, l * 16, 16, G[:, 48:64], G)
        self.load_cols(self.g_ret, l * 4, 4, G[:, 64:68], G)
        self.load_cols(self.g_hg, l * 4, 4, G[:, 68:72], G)
        for j in range(3):
            self.load_cols(self.cw, (l * 3 + j) * 88, 88, G[:, 88 + j * 88: 88 + (j + 1) * 88], G)
        self.load_cols(self.cb, l * 88, 88, G[:, 88 + 3 * 88: 88 + 4 * 88], G)
        for nm in ("wb_in", "wb_out", "wb_q", "wb_k", "wb_v", "wb_o", "wb_up", "wb_dn"):
            setattr(self, nm, getattr(self, nm + "_set")[l % 2])
        if l == 0:
            self.cast_f = [p.sb([128, 2048], F32, f"castf{i}") for i in range(2)]
            self.cast_b = [p.sb([128, 2048], BF16, f"castb{i}") for i in range(2)]
            for _ in self.precast_all(0, ("act", "dve", "pool")):
                pass
        p.release(m)

        def norm_bufs(ntok):
            self.xs = [p.sb([128, KC, 512], F32, "xs0")] * 2
            self.sq = p.sb([128, KC, 512], BF16, "sq")
            self.rstd = p.sb([128, 512], F32, "rstd")
            self.hT = p.sb([128, KC, ntok], BF16, "hT")
        m = self.phase()
        norm_bufs(1024)
        self.wt = [p.sb([128, KC, 128], BF16, f"wt{i}") for i in range(3)]
        self.cast_f = [p.sb([128, 1024], F32, f"rope{i}") for i in range(2)]
        self.xbb = [p.sb([128, 512], BF16, f"xbb{i}") for i in range(2)]
        self.in_proj(l)
        p.release(m)
        m = self.phase()
        self.big_b = [p.sb([128, S], BF16, f"bigb{i}") for i in range(2)]
        self.dec_sb = p.sb([128, 5 * 512], F32, "dec_sb")
        self.qbuf = [p.sb([128, 512], BF16, f"qb{i}") for i in range(4)]
        self.rwbuf = [p.sb([128, 512], BF16, f"rw{i}") for i in range(3)]
        self.retention(l)
        p.release(m)
        m = self.phase()
        self.big_b = [p.sb([128, S], BF16, f"bigb{i}") for i in range(2)]
        self.spsum = p.sb([128, 512], F32, "spsum")
        self.qbuf = [p.sb([128, 512], BF16, f"qb{i}") for i in range(4)]
        self.bg = None
        if l + 1 < L:
            self.cast_f = [p.sb([128, 2048], F32, f"castf{i}") for i in range(2)]
            self.cast_b = [p.sb([128, 2048], BF16, f"castb{i}") for i in range(2)]
            self.bg = self.precast_all(l + 1, ("dve",))
        self.stickbreak(l)
        if self.bg is not None:
            for _ in self.bg:
                pass
            self.bg = None
        p.release(m)
        m = self.phase()
        if self.do_hg:
            self.hgrn(l)
        else:
            self.zero_mix(1024, 1536)
        p.release(m)
        m = self.phase()
        if self.do_nsa:
            self.nsa(l)
        else:
            self.zero_mix(1536, 2048)
        p.release(m)
        m = self.phase()
        self.hT = p.sb([128, KC, 1024], BF16, "hT")
        self.wt = [p.sb([128, KC, 128], BF16, f"wt{i}") for i in range(3)]
        self.out_proj(l)
        self.snap("dbg_x1")
        p.release(m)
        m = self.phase()
        norm_bufs(1024)
        self.wt = [p.sb([128, KC, 128], BF16, f"wt{i}") for i in range(3)]
        self.cast_f = [p.sb([128, 2048], F32, f"memin{i}") for i in range(2)]
        self.memT = p.sb([128, KC, 256], BF16, "memT")
        self.kTm = p.sb([128, 4, 256], BF16, "kTm")
        self.vm = p.sb([128, 2, 512], BF16, "vm")
        self.oT = p.sb([128, 4, 1024], BF16, "oT")
        self.qbuf = [p.sb([128, 512], BF16, f"qb{i}") for i in range(2)]
        self.xattn(l)
        self.snap("dbg_x2")
        p.release(m)
        m = self.phase()
        self.carry = p.sb([128, 88, 2], F32, "carry")
        self.ffn(l)
        self.snap("dbg_x3")
        p.release(m)

    def snap(self, name):
        import os
        if not os.environ.get("MKDEBUG"):
            return
        p = self.p
        d = self._scratch(name + f"_{self.p.nalloc}", [D, self.S], F32)
        self.p.nalloc += 1
        self.dbg = getattr(self, "dbg", {})
        self.dbg[name] = d
        for tt in range(self.NT):
            p.dma("sp", d[:, tt * 512:(tt + 1) * 512], self.xT[:, tt * 512:(tt + 1) * 512], [self.xT.res[tt]], [d])
        p.finish([d])

    def zero_mix(self, r0, r1):
        p = self.p
        z = self.stg_b[0]
        p.memset("dve", z[:], 0.0, [z])
        for r in range(r0, r1, 128):
            for tt in range(self.NT):
                p.dma("pool", self.mixT[r:r + 128, tt * 512:(tt + 1) * 512], z[:], [z], [self.mixT])

    def in_proj(self, l):
        p, S, NT = self.p, self.S, self.NT
        ps = self.psum
        G = self.gains
        hT = self.hT
        for st in range(NT // 2):
            for j in range(2):
                self.norm_tile(st * 2 + j, G[:, 0:16], hT, j * 512)
            cs = [p.sb([128, 512], F32, f"cs{st}_{i}") for i in range(0)]
            cos_t = self.cast_f[0]
            sin_t = self.cast_f[1]
            p.dma("sp", cos_t[:, 0:1024], self.cosT[:, st * 1024:(st + 1) * 1024], [self.cosT], [cos_t])
            p.dma("sp", sin_t[:, 0:1024], self.sinT[:, st * 1024:(st + 1) * 1024], [self.sinT], [sin_t])
            for ri, c0 in enumerate(ROPE_COLS):
                wa = self.load_w(self.wb_in, c0, 128)
                for j in range(2):
                    pa, pb = ps[(2 * j) % 4], ps[(2 * j + 1) % 4]
                    for k in range(KC):
                        p.mm(pa[:], wa[:, k, :], hT[:, k, j * 512:(j + 1) * 512], k == 0, k == KC - 1, [wa, hT], [pa])
                    xb = self.xbb[j]
                    p.copy("dve", xb[:], pa[:], [pa], [xb])
                    p.mm(pb[:], self.perm_b[:], xb[:], True, True, [self.perm_b, xb], [pb])
                    t1 = self.stage("f")
                    t2 = self.stage("f")
                    ob = self.stage("b")
                    p.tt("dve", t1[:], pa[:], cos_t[:, j * 512:(j + 1) * 512], ALU.mult, [pa, cos_t], [t1])
                    p.tt("dve", t2[:], pb[:], sin_t[:, j * 512:(j + 1) * 512], ALU.mult, [pb, sin_t], [t2])
                    p.tt("pool", ob[:], t1[:], t2[:], ALU.add, [t1, t2], [ob])
                    tt = st * 2 + j
                    p.dma("pool", self.fmb[ri * 128:(ri + 1) * 128, tt * 512:(tt + 1) * 512], ob[:], [ob], [self.fmb])
            for (c0, idx), kind in [(x_, "b") for x_ in FMB_PLAIN] + [(x_, "f") for x_ in FMF] + [((C_NG, 0), "g")]:
                ncol = 12 if kind == "g" else 128
                wa = self.load_w(self.wb_in, c0, ncol)
                for j in range(2):
                    pa = ps[j % 4]
                    for k in range(KC):
                        p.mm(pa[0:ncol, :], wa[:, k, 0:ncol], hT[:, k, j * 512:(j + 1) * 512], k == 0, k == KC - 1,
                             [wa, hT], [pa])
                    tt = st * 2 + j
                    if kind == "b":
                        ob = self.stage("b")
                        p.copy("act", ob[:], pa[:], [pa], [ob])
                        p.dma("pool", self.fmb[idx * 128:(idx + 1) * 128, tt * 512:(tt + 1) * 512], ob[:], [ob], [self.fmb])
                    elif kind == "f":
                        of = self.stage("f")
                        p.copy("act", of[:], pa[:], [pa], [of])
                        p.dma("pool", self.fmf[idx * 128:(idx + 1) * 128, tt * 512:(tt + 1) * 512], of[:], [of], [self.fmf])
                    else:
                        of = self.stage("f")
                        p.copy("act", of[0:12, :], pa[0:12, :], [pa], [of])
                        p.dma("pool", self.ngT[:, tt * 512:(tt + 1) * 512], of[0:12, :], [of], [self.ngT])
            for (c0, ncol, t0) in TMB:
                for cc in range(0, ncol, 128):
                    wa = self.load_w(self.wb_in, c0 + cc, 128)
                    for tk in range(8):
                        pa = ps[tk % 4]
                        for k in range(KC):
                            p.mm(pa[:, 0:128], hT[:, k, tk * 128:(tk + 1) * 128], wa[:, k, :], k == 0, k == KC - 1,
                                 [wa, hT], [pa])
                        ob = self.stage("b")
                        p.copy("act" if tk % 2 else "dve", ob[:, 0:128], pa[:, 0:128], [pa], [ob])
                        r0 = st * 1024 + tk * 128
                        p.dma("pool", self.tmb[r0:r0 + 128, t0 + cc:t0 + cc + 128], ob[:, 0:128], [ob], [self.tmb])

    def headnorm_gate(self, o_ps, center, gcol, gate_idx, mix_row, tt):
        p = self.p
        ps = self.psum
        o = self.stage("f")
        p.copy("act", o[:], o_ps[:], [o_ps], [o])
        st = ps[5]
        if center:
            p.mm(st[:], self.ones_f, o[:], True, True, [self.cf, o], [st])
            cen = self.stage("f")
            p.stt(cen[:], st[:], -1.0 / HD, o[:], ALU.mult, ALU.add, [st, o], [cen])
        else:
            cen = o
        sq = self.stage("f")
        p.op("act", lambda e: e.activation(sq[:], cen[:], AF.Square), [cen], [sq])
        p.mm(st[:], self.ones_f, sq[:], True, True, [self.cf, sq], [st])
        rs = self.stage("f")
        p.op("act", lambda e: e.activation(rs[:], st[:], AF.Sqrt, bias=self.cf[:, 389:390], scale=1.0 / HD),
             [st, self.cf], [rs])
        p.op("dve", lambda e: e.reciprocal(rs[:], rs[:]), [rs], [rs])
        y = sq
        p.stt(y[:], cen[:], gcol, rs[:], ALU.mult, ALU.mult, [cen, rs, self.gains], [y])
        g = self.stage("f")
        p.dma("sp", g[:], self.fmf[gate_idx * 128:(gate_idx + 1) * 128, tt * 512:(tt + 1) * 512], [self.fmf], [g])
        p.op("act", lambda e: e.activation(g[:], g[:], AF.Silu), [g], [g])
        ob = self.stage("b")
        p.tt("dve", ob[:], y[:], g[:], ALU.mult, [y, g], [ob])
        p.dma("pool", self.mixT[mix_row:mix_row + 128, tt * 512:(tt + 1) * 512], ob[:], [ob], [self.mixT])

    def load_head(self, k_idx, v_col):
        p, S = self.p, self.S
        kT = self.big_b[0]
        v = self.big_b[1]
        p.dma("sp", kT[:], self.fmb[k_idx * 128:(k_idx + 1) * 128, :], [self.fmb], [kT])
        p.dma("sp", v[:].rearrange("p (t e) -> p t e", e=128),
              self.tmb[:, v_col:v_col + 128].rearrange("(t p) e -> p t e", p=128), [self.tmb], [v])
        return kT, v

    def retention(self, l):
        p, S, NT = self.p, self.S, self.NT
        ps = self.psum
        dec = self.dec_sb
        for h in range(4):
            lg = math.log1p(-2.0 ** (-5.0 - h))
            kT, v = self.load_head(FB_RK + h, TM_RV + h * 128)
            p.dma("sp", dec[:], self.dec_d[:, h * 5 * 512:(h + 1) * 5 * 512], [self.dec_d], [dec])
            for qc in range(NT):
                qT = self.qbuf[qc % 2]
                p.dma("sp", qT[:], self.fmb[(FB_RQ + h) * 128:(FB_RQ + h + 1) * 128, qc * 512:(qc + 1) * 512],
                      [self.fmb], [qT])
                po = ps[4]
                kts = [kt for kt in range(4 * qc + 4)
                       if kt >= 4 * qc or lg * (512 * qc - 128 * kt - 127) > -85.0]
                n = len(kts)

                def rA(i):
                    kt = kts[i]
                    pa = ps[i % 3]
                    p.mm(pa[:], kT[:, kt * 128:(kt + 1) * 128], qT[:], True, True, [kT, qT], [pa])
                    w = self.rwbuf[i % 3]
                    if kt >= 4 * qc:
                        j = kt - 4 * qc
                        dt = dec[:, (1 + j) * 512:(2 + j) * 512]
                        c = SCALE
                    else:
                        dt = dec[:, 0:512]
                        c = SCALE * math.exp(lg * (512 * qc - 128 * kt))
                    p.stt(w[:], pa[:], c, dt, ALU.mult, ALU.mult, [pa, self.dec_sb], [w])

                def rE(i):
                    kt = kts[i]
                    w = self.rwbuf[i % 3]
                    p.mm(po[:], v[:, kt * 128:(kt + 1) * 128], w[:], i == 0, i == n - 1, [v, w], [po])

                for step in range(n + 2):
                    if step < n:
                        rA(step)
                    if 0 <= step - 2 < n:
                        rE(step - 2)
                self.headnorm_gate(po, True, self.gains[:, 64 + h:65 + h], FF_RG + h, h * 128, qc)

    def stickbreak(self, l):
        p, S, NT = self.p, self.S, self.NT
        ps = self.psum
        spsum = self.spsum
        ebuf = [p.sb([128, 512], F32, f"sbe{i}") for i in range(2)]
        spbuf = [p.sb([128, 512], F32, f"sbsp{i}") for i in range(3)]
        wbuf = [p.sb([128, 512], BF16, f"sbw{i}") for i in range(3)]
        for h in range(4):
            kT, v = self.load_head(FB_SK + h, TM_SV + h * 128)
            for qc in range(NT):
                qT = self.qbuf[qc % 2]
                p.dma("sp", qT[:], self.fmb[(FB_SQ + h) * 128:(FB_SQ + h + 1) * 128, qc * 512:(qc + 1) * 512],
                      [self.fmb], [qT])
                nqT = self.qbuf[2 + qc % 2]
                p.op("act", lambda e, nqT=nqT, qT=qT: e.mul(nqT[:], qT[:], -SCALE), [qT], [nqT])
                p.memset("pool", spsum[:], 0.0, [spsum])
                po = ps[7]
                kts = list(range(4 * qc + 3, -1, -1))
                n = len(kts)

                def stA(i):
                    kt = kts[i]
                    pa = ps[i % 3]
                    p.mm(pa[:], kT[:, kt * 128:(kt + 1) * 128], qT[:], True, True, [kT, qT], [pa])
                    e_ = ebuf[i % 2]
                    p.op("act", lambda e, e_=e_, pa=pa: e.activation(e_[:], pa[:], AF.Exp, scale=SCALE), [pa], [e_])
                    sp = spbuf[i % 3]
                    p.op("act", lambda e, e_=e_, sp=sp: e.activation(sp[:], e_[:], AF.Ln, bias=self.cf[:, 388:389], scale=1.0),
                         [e_, self.cf], [sp])
                    if kt >= 4 * qc:
                        p.tt("dve", sp[:], sp[:], self.mask(kt - 4 * qc), ALU.mult, [sp, self.masks], [sp])

                def stC(i):
                    kt = kts[i]
                    pc = ps[3 + i % 2]
                    sp = spbuf[i % 3]
                    p.mm(pc[:], self.uincl, sp[:], True, False, [self.cf, sp], [pc], sync=False)
                    p.mm(pc[:], self.ones_f, spsum[:], False, False, [self.cf, spsum], [pc], sync=False)
                    p.mm(pc[:], kT[:, kt * 128:(kt + 1) * 128], nqT[:], False, True, [kT, nqT], [pc])
                    w = wbuf[i % 3]
                    p.op("act", lambda e, w=w, pc=pc: e.activation(w[:], pc[:], AF.Exp, scale=-1.0), [pc], [w])
                    if kt >= 4 * qc:
                        p.tt("dve", w[:], w[:], self.mask(kt - 4 * qc), ALU.mult, [w, self.masks], [w])
                    p.tt("pool", spsum[:], spsum[:], sp[:], ALU.add, [spsum, sp], [spsum])

                def stE(i):
                    kt = kts[i]
                    w = wbuf[i % 3]
                    p.mm(po[:], v[:, kt * 128:(kt + 1) * 128], w[:], i == 0, i == n - 1, [v, w], [po])

                for step in range(n + 2):
                    if step < n:
                        stA(step)
                    if 0 <= step - 1 < n:
                        stC(step - 1)
                    if 0 <= step - 2 < n:
                        stE(step - 2)
                    if self.bg is not None and step % self.bg_every == 0:
                        try:
                            next(self.bg)
                        except StopIteration:
                            self.bg = None
                ob = self.stage("b")
                p.copy("act", ob[:], po[:], [po], [ob])
                p.dma("pool", self.mixT[512 + h * 128:512 + (h + 1) * 128, qc * 512:(qc + 1) * 512], ob[:], [ob], [self.mixT])

    def hgrn(self, l):
        p, S, NT = self.p, self.S, self.NT
        ps = self.psum
        f32t = lambda n: p.sb([128, 512], F32, n)
        gf, fk, lg, Gt, A1, A3, E1, E1n, E2, E3 = [f32t(n) for n in
                                                   ("gf", "fk", "lg", "Gt", "A1", "A3", "E1", "E1n", "E2", "E3")]
        kk = f32t("kk")
        kl = f32t("kl")
        qb, qg, qG, kg = [p.sb([128, 512], BF16, n) for n in ("qb", "qg", "qG", "kg")]
        klT = p.sb([64, 8, 128], BF16, "klT")
        vt = p.sb([64, 8, 128], BF16, "vt")
        Sf = p.sb([128, 128], F32, "Sf")
        Sb = p.sb([128, 128], BF16, "Sb")
        egl = p.sb([128, 8], F32, "egl")
        scs = [p.sb([64, 64], BF16, f"scs{i}") for i in range(2)]
        tri = self.cf[0:64, 392:456]
        rmask = self.cf[:, 456:968]
        v3 = lambda t: t[:].rearrange("p (c t) -> p c t", t=64)
        for h in range(4):
            col = l * 4 + h
            p.memset("dve", Sf[:], 0.0, [Sf])
            p.memset("pool", Sb[:], 0.0, [Sb])
            for tt in range(NT):
                cs = slice(tt * 512, (tt + 1) * 512)
                p.dma("sp", gf[:], self.fmf[(FF_GF + h) * 128:(FF_GF + h + 1) * 128, cs], [self.fmf], [gf])
                p.dma("sp", qb[:], self.fmb[(FB_GQ + h) * 128:(FB_GQ + h + 1) * 128, cs], [self.fmb], [qb])
                p.dma("sp", vt[:], self.tmb[tt * 512:(tt + 1) * 512, TM_GI + h * 128:TM_GI + (h + 1) * 128]
                      .rearrange("(c m) e -> m c e", m=64), [self.tmb], [vt])
                p.op("act", lambda e: e.activation(gf[:], gf[:], AF.Sigmoid), [gf], [gf])
                p.ts("dve", fk[:], gf[:], self.oml[:, col:col + 1], self.lb[:, col:col + 1], ALU.mult, ALU.add,
                     [gf, self.oml, self.lb], [fk])
                p.ts("dve", kk[:], fk[:], -1.0, 1.0, ALU.mult, ALU.add, [fk], [kk])
                p.ts("dve", fk[:], fk[:], 1e-6, None, ALU.max, None, [fk], [fk])
                p.op("act", lambda e: e.activation(lg[:], fk[:], AF.Ln), [fk], [lg])
                p.op("dve", lambda e: e.tensor_tensor_scan(Gt[:], rmask, lg[:], 0.0, ALU.mult, ALU.add),
                     [self.cf, lg], [Gt])
                G3 = v3(Gt)
                p.tt("dve", v3(A1), G3, G3[:, :, 31:32].to_broadcast([128, 8, 64]), ALU.subtract, [Gt], [A1])
                p.tt("dve", v3(A3), G3, G3[:, :, 63:64].to_broadcast([128, 8, 64]), ALU.subtract, [Gt], [A3])
                p.op("act", lambda e: e.activation(E1[:], A1[:], AF.Exp), [A1], [E1])
                p.op("act", lambda e: e.activation(E1n[:], A1[:], AF.Exp, scale=-1.0), [A1], [E1n])
                p.op("act", lambda e: e.activation(E2[:], Gt[:], AF.Exp), [Gt], [E2])
                p.op("act", lambda e: e.activation(E3[:], A3[:], AF.Exp, scale=-1.0), [A3], [E3])
                p.op("act", lambda e: e.activation(egl[:].rearrange("p (c o) -> p c o", o=1), G3[:, :, 63:64], AF.Exp),
                     [Gt], [egl])
                p.stt(qg[:], qb[:], SCALE, E1[:], ALU.mult, ALU.mult, [qb, E1], [qg])
                p.stt(qG[:], qb[:], SCALE, E2[:], ALU.mult, ALU.mult, [qb, E2], [qG])
                p.tt("dve", kg[:], kk[:], E1n[:], ALU.mult, [kk, E1n], [kg])
                p.tt("pool", kl[:], kk[:], E3[:], ALU.mult, [kk, E3], [kl])
                for half in range(2):
                    pk = ps[half]
                    for c4 in range(4):
                        c = half * 4 + c4
                        p.op("pe", lambda e, pk=pk, c=c, c4=c4: e.transpose(pk[0:64, c4 * 128:(c4 + 1) * 128],
                                                                           kl[:, c * 64:(c + 1) * 64], self.ident),
                             [kl, self.cf], [pk])
                    p.copy("act", klT[:, half * 4:(half + 1) * 4, :],
                           pk[0:64, :].rearrange("p (c d) -> p c d", d=128), [pk], [klT])
                po = ps[4]
                for c in range(8):
                    cc = slice(c * 64, (c + 1) * 64)
                    psc = ps[2 + c % 2]
                    p.mm(psc[0:64, 0:64], kg[:, cc], qg[:, cc], True, True, [kg, qg], [psc])
                    sc = scs[c % 2]
                    p.tt("dve", sc[:], psc[0:64, 0:64], tri, ALU.mult, [psc, self.cf], [sc])
                    p.mm(po[:, cc], vt[:, c, :], sc[:], True, False, [vt, sc], [po], sync=False)
                    p.mm(po[:, cc], Sb[:], qG[:, cc], False, True, [Sb, qG], [po])
                    pS = ps[6 + c % 2]
                    p.mm(pS[:, 0:128], klT[:, c, :], vt[:, c, :], True, True, [klT, vt], [pS])
                    p.stt(Sf[:], Sf[:], egl[:, c:c + 1], pS[:, 0:128], ALU.mult, ALU.add, [Sf, egl, pS], [Sf])
                    p.copy("act", Sb[:], Sf[:], [Sf], [Sb])
                self.headnorm_gate(po, False, self.gains[:, 68 + h:69 + h], FF_GG + h, 1024 + h * 128, tt)

    def nsa(self, l):
        p, S, NT = self.p, self.S, self.NT
        ps = self.psum
        NSL, NCMP, NCT = self.NSL, self.NCMP, self.NCT
        kcmpT = p.sb([128, NCT * 128], BF16, "kcmpT")
        vcmp = p.sb([128, NCT, 128], BF16, "vcmp")
        p.memset("pool", kcmpT[:], 0.0, [kcmpT])
        p.memset("pool", vcmp[:], 0.0, [vcmp])
        m0 = p.mark()
        kcT = p.sb([128, S], BF16, "kcT")
        vcT = p.sb([128, S], BF16, "vcT")
        p.dma("sp", kcT[:], self.fmb[FB_KC * 128:(FB_KC + 1) * 128, :], [self.fmb], [kcT])
        p.dma("sp", vcT[:], self.fmb[FB_VC * 128:(FB_VC + 1) * 128, :], [self.fmb], [vcT])
        wst = p.sb([128, 32, 128], F32, "wst")
        wck = p.sb([128, 32, 128], BF16, "wck")
        wcv = p.sb([128, 32, 128], BF16, "wcv")
        for src, dst in ((self.w_ck, wck), (self.w_cv, wcv)):
            p.dma("sp", wst[:], src[l * 4096:(l + 1) * 4096, :].rearrange("(li d) e -> d li e", d=128), [src], [wst])
            p.copy("dve", dst[:], wst[:], [wst], [dst])
        posf = p.sb([128, 64], F32, "posf")
        posb = p.sb([128, 64], BF16, "posb")
        self.load_cols(self.pos_k, l * 32, 32, posf[:, 0:32], posf)
        self.load_cols(self.pos_v, l * 32, 32, posf[:, 32:64], posf)
        p.copy("dve", posb[:], posf[:], [posf], [posb])
        bk = p.sb([128, 1], F32, "bk")
        pb = ps[1]
        for li in range(32):
            p.mm(pb[:, 0:1], wck[:, li, :], posb[:, li:li + 1], li == 0, li == 31, [wck, posb], [pb])
        p.copy("dve", bk[:], pb[:, 0:1], [pb], [bk])
        for c0 in range(0, NCMP, 512):
            n = min(512, NCMP - c0)
            pa = ps[0]
            for li in range(32):
                st_ = li + 16 * c0
                p.mm(pa[:, 0:n], wck[:, li, :], kcT[:, st_:st_ + 16 * (n - 1) + 1:16], li == 0, li == 31, [wck, kcT], [pa])
            p.ts("dve", kcmpT[:, c0:c0 + n], pa[:, 0:n], bk[:, 0:1], None, ALU.add, None, [pa, bk], [kcmpT])
        bvr = p.sb([1, 128], BF16, "bvr")
        pr = ps[2]
        for li in range(32):
            p.mm(pr[0:1, 0:128], posb[:, 32 + li:33 + li], wcv[:, li, :], li == 0, li == 31, [posb, wcv], [pr])
        p.copy("dve", bvr[:], pr[0:1, 0:128], [pr], [bvr])
        for nt in range(NCT):
            cnt = min(128, NCMP - nt * 128)
            pa = ps[3]
            for li in range(32):
                st_ = li + 16 * nt * 128
                p.mm(pa[0:cnt, 0:128], vcT[:, st_:st_ + 16 * (cnt - 1) + 1:16], wcv[:, li, :], li == 0, False,
                     [vcT, wcv], [pa], sync=False)
            p.mm(pa[0:cnt, 0:128], self.ones_b[0:1, 0:cnt], bvr[0:1, :], False, True, [self.ones_b, bvr], [pa])
            p.copy("act", vcmp[0:cnt, nt, :], pa[0:cnt, 0:128], [pa], [vcmp])
        p.release(m0)
        ksT = p.sb([128, S], BF16, "ksT")
        kwT = p.sb([128, S], BF16, "kwT")
        vs = p.sb([128, S // 128, 128], BF16, "vs")
        vw = p.sb([128, S // 128, 128], BF16, "vw")
        p.dma("sp", ksT[:], self.fmb[FB_KS * 128:(FB_KS + 1) * 128, :], [self.fmb], [ksT])
        p.dma("sp", kwT[:], self.fmb[FB_KW * 128:(FB_KW + 1) * 128, :], [self.fmb], [kwT])
        p.dma("sp", vs[:], self.tmb[:, TM_VS:TM_VS + 128].rearrange("(t p) e -> p t e", p=128), [self.tmb], [vs])
        p.dma("sp", vw[:], self.tmb[:, TM_VW:TM_VW + 128].rearrange("(t p) e -> p t e", p=128), [self.tmb], [vw])
        ex = p.sb([128, S], BF16, "ex")
        p.dma("sp", ex[:], self.ex_d[:], [self.ex_d], [ex])
        mk = p.sb([128, 17 * 512], BF16, "mk2")
        p.dma("sp", mk[:], self.masks_d[:, 4 * 512:21 * 512], [self.masks_d], [mk])
        msk = lambda i: mk[:, (i - 4) * 512:(i - 3) * 512]
        ovl = p.sb([128, NCT, NSL], BF16, "ovl")
        p.dma("sp", ovl[:], self.ovl_d[:].rearrange("p (t j) -> p t j", j=NSL), [self.ovl_d], [ovl])
        csel = p.sb([128, 528], F32, "csel")
        p.dma("sp", csel[:], self.csel_d[:], [self.csel_d], [csel])
        onesel = p.sb([128, 16], BF16, "onesel")
        p.copy("dve", onesel[:], csel[:, 0:16], [csel], [onesel])
        acc = p.sb([128, 4, 512], F32, "acc")
        imp = p.sb([128, 4, NSL], F32, "imp")
        At = p.sb([128, 4, NSL], F32, "At")
        Bt = p.sb([128, 4, NSL], F32, "Bt")
        impF = p.sb([128, NSL], F32, "impF")
        wk1 = p.sb([128, NSL], F32, "wk1")
        wk2 = p.sb([128, NSL], F32, "wk2")
        sel = p.sb([128, NSL], F32, "sel")
        m8 = p.sb([128, 8], F32, "m8")
        selT = p.sb([128, 512], BF16, "selT")
        qh = p.sb([128, 4, 512], BF16, "qh")
        mfb = [p.sb([128, 512], BF16, f"mfb{i}") for i in range(2)]
        pbuf = [p.sb([128, 512], BF16, f"pbuf{i}") for i in range(4)]
        zc = p.sb([128, 4], F32, "zc")
        z4 = p.sb([4, 512], F32, "z4")
        g4 = p.sb([4, 512], F32, "g4")
        zr = p.sb([1, 512], F32, "zr")
        gr = p.sb([1, 512], F32, "gr")

        def branch4(qc, kts, kT, vT, maskfn, expand, gate_b, first):
            items = [(i, kt, h) for i, kt in enumerate(kts) for h in range(4)]
            nit = len(items)
            mstate = {}

            def bA(t):
                i, kt, h = items[t]
                if h == 0:
                    if expand:
                        pm = ps[7]
                        p.mm(pm[:], ex[0:NSL, kt * 128:(kt + 1) * 128], selT[0:NSL, :], True, True, [ex, selT], [pm])
                        Mf = mfb[i % 2]
                        mi = maskfn(kt)
                        if mi is not None:
                            p.tt("dve", Mf[:], pm[:], msk(mi), ALU.mult, [pm, mk], [Mf])
                        else:
                            p.copy("dve", Mf[:], pm[:], [pm], [Mf])
                        mstate[i] = (Mf[:], [Mf])
                    else:
                        mstate[i] = (msk(maskfn(kt)), [mk])
                Mfa, Mfr = mstate[i]
                pa = ps[5 + t % 2]
                p.mm(pa[:], kT[:, kt * 128:(kt + 1) * 128], qh[:, h, :], True, True, [kT, qh], [pa])
                P = pbuf[t % 4]
                p.op("act", lambda e, P=P, pa=pa: e.activation(P[:], pa[:], AF.Exp, scale=SCALE), [pa], [P])
                p.tt("pool" if h % 2 else "dve", P[:], P[:], Mfa, ALU.mult, [P] + Mfr, [P])

            def bE(t):
                i, kt, h = items[t]
                P = pbuf[t % 4]
                last = i == len(kts) - 1
                p.mm(ps[h][:], vT[:, kt, :], P[:], i == 0, last, [vT, P], [ps[h]])
                p.mm(ps[4][0:4, :], onesel[:, h * 4:(h + 1) * 4], P[:], t == 0, t == nit - 1,
                     [onesel, P], [ps[4]], sync=(t == nit - 1))

            for step in range(nit + 2):
                if step < nit:
                    bA(step)
                if 0 <= step - 2 < nit:
                    bE(step - 2)
            p.ts("dve", z4[:], ps[4][0:4, :], 1e-30, None, ALU.max, None, [ps[4]], [z4])
            p.op("dve", lambda e: e.reciprocal(z4[:], z4[:]), [z4], [z4])
            p.dma("sp", g4[:], self.ngT[gate_b:12:3, qc * 512:(qc + 1) * 512], [self.ngT], [g4])
            p.op("act", lambda e: e.activation(g4[:], g4[:], AF.Sigmoid), [g4], [g4])
            p.tt("dve", g4[:], g4[:], z4[:], ALU.mult, [g4, z4], [g4])
            for h in range(4):
                pcb = ps[5 + h % 2]
                p.mm(pcb[:], csel[0:4, 16 + h * 128:16 + (h + 1) * 128], g4[0:4, :], True, True, [csel, g4], [pcb])
                o = self.stage("f")
                p.copy("act", o[:], ps[h][:], [ps[h]], [o])
                if first:
                    p.tt("dve", acc[:, h, :], o[:], pcb[:], ALU.mult, [o, pcb], [acc])
                else:
                    p.tt("dve", o[:], o[:], pcb[:], ALU.mult, [o, pcb], [o])
                    p.tt("pool", acc[:, h, :], acc[:, h, :], o[:], ALU.add, [acc, o], [acc])

        for qc in range(NT):
            cs = slice(qc * 512, (qc + 1) * 512)
            for h in range(4):
                p.dma("sp", qh[:, h, :], self.fmb[(FB_NQ + h) * 128:(FB_NQ + h + 1) * 128, cs], [self.fmb], [qh])
            p.dma("sp", At[:], self.impA_d[qc * 512:(qc + 1) * 512, :].rearrange("(j p) n -> p j n", p=128), [self.impA_d], [At])
            p.dma("sp", Bt[:], self.impB_d[qc * 512:(qc + 1) * 512, :].rearrange("(j p) n -> p j n", p=128), [self.impB_d], [Bt])
            nts = [nt for nt in range(NCT) if 512 * qc - 2048 * nt >= 0]
            for h in range(4):
                po, pi, pc, pz = ps[0], ps[1], ps[2], ps[4]
                for i, nt in enumerate(nts):
                    last = i == len(nts) - 1
                    dl = 512 * qc - 2048 * nt
                    pa = ps[5 + i % 2]
                    p.mm(pa[:], kcmpT[:, nt * 128:(nt + 1) * 128], qh[:, h, :], True, True, [kcmpT, qh], [pa])
                    P = self.stage("b")
                    p.op("act", lambda e, P=P, pa=pa: e.activation(P[:], pa[:], AF.Exp, scale=SCALE), [pa], [P])
                    if dl < 2560:
                        p.tt("dve", P[:], P[:], msk(16 + dl // 512), ALU.mult, [P, mk], [P])
                    p.mm(po[:], vcmp[:, nt, :], P[:], i == 0, last, [vcmp, P], [po])
                    p.mm(pz[0:1, :], self.ones_b[:, 0:1], P[:], i == 0, last, [self.ones_b, P], [pz])
                    for jq in range(4):
                        p.mm(pi[:, jq * NSL:(jq + 1) * NSL], P[:, jq * 128:(jq + 1) * 128], ovl[:, nt, :],
                             i == 0 and jq == 0, last and jq == 3, [P, ovl], [pi], sync=(last and jq == 3))
                    for jq in range(4):
                        p.mm(pc[:, jq:jq + 1], P[:, jq * 128:(jq + 1) * 128], self.ones_b[:, 0:1],
                             i == 0 and jq == 0, last and jq == 3, [P, self.ones_b], [pc], sync=(last and jq == 3))
                p.ts("dve", zr[:], pz[0:1, :], 1e-30, None, ALU.max, None, [pz], [zr])
                p.op("dve", lambda e: e.reciprocal(zr[:], zr[:]), [zr], [zr])
                p.dma("sp", gr[:], self.ngT[h * 3:h * 3 + 1, cs], [self.ngT], [gr])
                p.op("act", lambda e: e.activation(gr[:], gr[:], AF.Sigmoid), [gr], [gr])
                p.tt("dve", gr[:], gr[:], zr[:], ALU.mult, [gr, zr], [gr])
                pcb = ps[3]
                p.mm(pcb[:], self.ones_f[0:1, :], gr[0:1, :], True, True, [self.cf, gr], [pcb])
                o = self.stage("f")
                p.copy("act", o[:], po[:], [po], [o])
                p.tt("dve", acc[:, h, :], o[:], pcb[:], ALU.mult, [o, pcb], [acc])
                p.ts("dve", zc[:], pc[:, 0:4], 1e-30, None, ALU.max, None, [pc], [zc])
                p.op("dve", lambda e: e.reciprocal(zc[:], zc[:]), [zc], [zc])
                for jq in range(4):
                    if h == 0:
                        p.ts("dve", imp[:, jq, :], pi[:, jq * NSL:(jq + 1) * NSL], zc[:, jq:jq + 1], None, ALU.mult, None,
                             [pi, zc], [imp])
                    else:
                        p.stt(imp[:, jq, :], pi[:, jq * NSL:(jq + 1) * NSL], zc[:, jq:jq + 1], imp[:, jq, :],
                              ALU.mult, ALU.add, [pi, zc, imp], [imp])
            pst = ps[3]
            for jq in range(4):
                p.tt("dve", impF[:], imp[:, jq, :], At[:, jq, :], ALU.mult, [imp, At], [impF])
                p.tt("dve", impF[:], impF[:], Bt[:, jq, :], ALU.add, [impF, Bt], [impF])
                p.op("dve", lambda e: e.max(out=m8[:], in_=impF[:]), [impF], [m8])
                p.op("dve", lambda e: e.match_replace(out=wk1[:], in_to_replace=m8[:], in_values=impF[:], imm_value=-1e9),
                     [m8, impF], [wk1])
                p.op("dve", lambda e: e.max(out=m8[:], in_=wk1[:]), [wk1], [m8])
                p.op("dve", lambda e: e.match_replace(out=wk2[:], in_to_replace=m8[:], in_values=wk1[:], imm_value=-1e9),
                     [m8, wk1], [wk2])
                p.tt("dve", sel[:], wk2[:], impF[:], ALU.not_equal, [wk2, impF], [sel])
                p.op("pe", lambda e, jq=jq: e.transpose(pst[0:NSL, jq * 128:(jq + 1) * 128], sel[:], self.ident),
                     [sel, self.cf], [pst])
            p.copy("act", selT[0:NSL, :], pst[0:NSL, :], [pst], [selT])
            branch4(qc, list(range(4 * qc + 4)), ksT, vs,
                    lambda kt, qc=qc: (4 + kt - 4 * qc) if kt >= 4 * qc else None, True, 1, False)
            branch4(qc, list(range(max(0, 4 * qc - 4), 4 * qc + 4)), kwT, vw,
                    lambda kt, qc=qc: 8 + (kt - 4 * qc + 4), False, 2, False)
            for h in range(4):
                ob = self.stage("b")
                p.copy("act", ob[:], acc[:, h, :], [acc], [ob])
                p.dma("pool", self.mixT[1536 + h * 128:1536 + (h + 1) * 128, cs], ob[:], [ob], [self.mixT])

    def out_proj(self, l):
        p, S, NT = self.p, self.S, self.NT
        ps = self.psum
        hT = self.hT
        for st in range(NT // 2):
            p.dma("sp", hT[:], self.mixT[:, st * 1024:(st + 1) * 1024].rearrange("(k p) n -> p k n", p=128),
                  [self.mixT], [hT])
            for ct in range(16):
                wa = self.load_w(self.wb_out, ct * 128, 128)
                for j in range(2):
                    tt = st * 2 + j
                    pa = ps[(2 * ct + j) % 8]
                    for k in range(KC):
                        p.mm(pa[:], wa[:, k, :], hT[:, k, j * 512:(j + 1) * 512], k == 0, k == KC - 1, [wa, hT], [pa])
                    xo = self.stage("f")
                    p.dma("sp", xo[:], self.xT[ct * 128:(ct + 1) * 128, tt * 512:(tt + 1) * 512], [self.xT.res[tt]], [xo])
                    p.tt("dve", xo[:], xo[:], pa[:], ALU.add, [xo, pa], [xo])
                    p.dma("pool", self.xT[ct * 128:(ct + 1) * 128, tt * 512:(tt + 1) * 512], xo[:], [xo], [self.xT.res[tt]])

    def xattn(self, l):
        p, S, NT = self.p, self.S, self.NT
        ps = self.psum
        G = self.gains
        hT = self.hT
        mT = self.memT
        if l == 0:
            mh = self.memhat
            for t in range(2):
                mi = self.cast_f[t]
                p.dma("sp", mi[:], self.mem[t * 128:(t + 1) * 128, :], [self.mem], [mi])
                ss = self.stage("f")
                junk = self.cast_f[1 - t] if False else self.xs[0]
                p.op("act", lambda e, mi=mi, ss=ss: e.activation(self.xs[1][:].rearrange("p k n -> p (k n)")[:, 0:2048],
                                                                   mi[:], AF.Square, accum_out=ss[:, 0:1]),
                     [mi], [ss, self.xs[1]])
                p.op("act", lambda e, ss=ss: e.activation(ss[:, 1:2], ss[:, 0:1], AF.Sqrt, bias=self.cf[:, 389:390],
                                                          scale=1.0 / D), [ss, self.cf], [ss])
                p.op("dve", lambda e, ss=ss: e.reciprocal(ss[:, 2:3], ss[:, 1:2]), [ss], [ss])
                p.ts("dve", mi[:], mi[:], ss[:, 2:3], None, ALU.mult, None, [mi, ss], [mi])
                for k in range(KC):
                    pk = ps[k % 4]
                    p.op("pe", lambda e, pk=pk, k=k, mi=mi: e.transpose(pk[:, 0:128], mi[:, k * 128:(k + 1) * 128], self.ident),
                         [mi, self.cf], [pk])
                    p.copy("dve", mh[:, k, t * 128:(t + 1) * 128], pk[:, 0:128], [pk], [mh])
        for k in range(KC):
            p.ts("dve", mT[:, k, :], self.memhat[:, k, :], G[:, 32 + k:33 + k], None, ALU.mult, None,
                 [self.memhat, G], [mT])
        kTm = self.kTm
        vm = self.vm
        for h in range(4):
            wa = self.load_w(self.wb_k, h * 128, 128)
            pa = ps[h % 4]
            for k in range(KC):
                p.mm(pa[:, 0:256], wa[:, k, :], mT[:, k, :], k == 0, k == KC - 1, [wa, mT], [pa])
            p.copy("act", kTm[:, h, :], pa[:, 0:256], [pa], [kTm])
        for cc in range(4):
            wa = self.load_w(self.wb_v, cc * 128, 128)
            for t in range(2):
                pa = ps[(cc * 2 + t) % 4]
                for k in range(KC):
                    p.mm(pa[:, 0:128], mT[:, k, t * 128:(t + 1) * 128], wa[:, k, :], k == 0, k == KC - 1, [wa, mT], [pa])
                p.copy("act", vm[:, t, cc * 128:(cc + 1) * 128], pa[:, 0:128], [pa], [vm])
        oT = self.oT
        for st in range(NT // 2):
            for j in range(2):
                self.norm_tile(st * 2 + j, G[:, 16:32], hT, j * 512)
            for h in range(4):
                wa = self.load_w(self.wb_q, h * 128, 128)
                for j in range(2):
                    pq = ps[0]
                    for k in range(KC):
                        p.mm(pq[:], wa[:, k, :], hT[:, k, j * 512:(j + 1) * 512], k == 0, k == KC - 1, [wa, hT], [pq])
                    qT = self.qbuf[(h * 2 + j) % 2]
                    p.copy("act", qT[:], pq[:], [pq], [qT])
                    po, pz = ps[4], ps[5]
                    for t in range(2):
                        pa = ps[1 + t]
                        p.mm(pa[:], kTm[:, h, t * 128:(t + 1) * 128], qT[:], True, True, [kTm, qT], [pa])
                        w = self.stage("b")
                        p.op("act", lambda e, w=w, pa=pa: e.activation(w[:], pa[:], AF.Exp, scale=SCALE), [pa], [w])
                        p.mm(po[:], vm[:, t, h * 128:(h + 1) * 128], w[:], t == 0, t == 1, [vm, w], [po])
                        p.mm(pz[:], self.ones_b[:], w[:], t == 0, t == 1, [self.ones_b, w], [pz])
                    rz = self.stage("f")
                    p.op("dve", lambda e, rz=rz, pz=pz: e.reciprocal(rz[:], pz[:]), [pz], [rz])
                    p.tt("dve", oT[:, h, j * 512:(j + 1) * 512], po[:], rz[:], ALU.mult, [po, rz], [oT])
            for ct in range(16):
                wa = self.load_w(self.wb_o, ct * 128, 128, kchunks=4)
                for j in range(2):
                    tt = st * 2 + j
                    pa = ps[(2 * ct + j) % 4]
                    for k in range(4):
                        p.mm(pa[:], wa[:, k, :], oT[:, k, j * 512:(j + 1) * 512], k == 0, k == 3, [wa, oT], [pa])
                    xo = self.stage("f")
                    p.dma("sp", xo[:], self.xT[ct * 128:(ct + 1) * 128, tt * 512:(tt + 1) * 512], [self.xT.res[tt]], [xo])
                    p.tt("dve", xo[:], xo[:], pa[:], ALU.add, [xo, pa], [xo])
                    p.dma("pool", self.xT[ct * 128:(ct + 1) * 128, tt * 512:(tt + 1) * 512], xo[:], [xo], [self.xT.res[tt]])

    def ffn(self, l):
        p, S, NT = self.p, self.S, self.NT
        ps = self.psum
        G = self.gains
        carry = self.carry
        p.memset("dve", carry[:], 0.0, [carry])
        CW = 88
        for st in range(NT // 2):
            ms = p.mark()
            hT = self.hT = p.sb([128, KC, 1024], BF16, "hT")
            mn = p.mark()
            self.xs = [p.sb([128, KC, 512], F32, "xs0")] * 2
            self.sq = p.sb([128, KC, 512], BF16, "sq")
            self.rstd = p.sb([128, 512], F32, "rstd")
            for j in range(2):
                self.norm_tile(st * 2 + j, G[:, 48:64], hT, j * 512)
            p.release(mn)
            aT = p.sb([128, 44, 1024], BF16, "aT")
            self.wt = [p.sb([128, 44, 128], BF16, f"wt{i}") for i in range(2)]
            ubuf = [p.sb([128, 1026], F32, f"ubuf{i}") for i in range(2)]
            cbuf = [p.sb([128, 1024], F32, f"cbuf{i}") for i in range(2)]
            for ct in range(44):
                res = []
                for gi, cti in enumerate((ct, ct + 44)):
                    wa = self.load_w(self.wb_up, cti * 128, 128)
                    ub = ubuf[gi]
                    p.copy("pool", ub[:, 0:2], carry[:, cti, :], [carry], [ub])
                    for j in range(2):
                        pa = ps[(2 * gi + j) % 4]
                        for k in range(KC):
                            p.mm(pa[:], wa[:, k, :], hT[:, k, j * 512:(j + 1) * 512], k == 0, k == KC - 1, [wa, hT], [pa])
                        p.copy("act", ub[:, 2 + j * 512:2 + (j + 1) * 512], pa[:], [pa], [ub])
                    p.copy("pool", carry[:, cti, :], ub[:, 1024:1026], [ub], [carry])
                    c = cbuf[gi]
                    p.ts("dve", c[:], ub[:, 0:1024], G[:, CW + cti:CW + cti + 1], G[:, CW + 3 * 88 + cti:CW + 3 * 88 + cti + 1],
                         ALU.mult, ALU.add, [ub, G], [c])
                    p.stt(c[:], ub[:, 1:1025], G[:, CW + 88 + cti:CW + 88 + cti + 1], c[:], ALU.mult, ALU.add, [ub, G, c], [c])
                    p.stt(c[:], ub[:, 2:1026], G[:, CW + 176 + cti:CW + 176 + cti + 1], c[:], ALU.mult, ALU.add, [ub, G, c], [c])
                    res.append(c)
                gte, val = res
                p.op("act", lambda e, gte=gte: e.activation(gte[:], gte[:], AF.Silu), [gte], [gte])
                p.tt("pool", aT[:, ct, :], gte[:], val[:], ALU.mult, [gte, val], [aT])
            for ct in range(16):
                wa = self.load_w(self.wb_dn, ct * 128, 128, kchunks=44)
                for j in range(2):
                    tt = st * 2 + j
                    pa = ps[4 + (2 * ct + j) % 4]
                    for k in range(44):
                        p.mm(pa[:], wa[:, k, :], aT[:, k, j * 512:(j + 1) * 512], k == 0, k == 43, [wa, aT], [pa])
                    xo = self.stage("f")
                    p.dma("sp", xo[:], self.xT[ct * 128:(ct + 1) * 128, tt * 512:(tt + 1) * 512], [self.xT.res[tt]], [xo])
                    p.tt("dve", xo[:], xo[:], pa[:], ALU.add, [xo, pa], [xo])
                    p.dma("pool", self.xT[ct * 128:(ct + 1) * 128, tt * 512:(tt + 1) * 512], xo[:], [xo], [self.xT.res[tt]])
            p.release(ms)


def build_nc(S, L):
    mk = MK(S, L)
    return mk.build()


_IN2D = {
    "x": lambda a: a.reshape(a.shape[1], D),
    "mem": lambda a: a.reshape(256, D),
    "positions": lambda a: a.reshape(1, -1),
    "mix_norm": lambda a: a.reshape(-1, 128),
    "w_in": lambda a: a.reshape(-1, NCOL),
    "ret_norm": lambda a: a.reshape(-1, 128),
    "hgrn_lb_logits": lambda a: a.reshape(-1, 128),
    "hgrn_norm": lambda a: a.reshape(-1, 128),
    "nsa_pos_k": lambda a: a.reshape(-1, 128),
    "nsa_pos_v": lambda a: a.reshape(-1, 128),
    "nsa_w_ck": lambda a: a.reshape(-1, 128),
    "nsa_w_cv": lambda a: a.reshape(-1, 128),
    "w_out": lambda a: a.reshape(-1, D),
    "xattn_norm": lambda a: a.reshape(-1, 128),
    "mem_norm": lambda a: a.reshape(-1, 128),
    "xattn_wq": lambda a: a.reshape(-1, 512),
    "xattn_wk": lambda a: a.reshape(-1, 512),
    "xattn_wv": lambda a: a.reshape(-1, 512),
    "xattn_wo": lambda a: a.reshape(-1, D),
    "ffn_norm": lambda a: a.reshape(-1, 128),
    "ffn_w_up": lambda a: a.reshape(-1, 2 * DFF),
    "ffn_conv_w": lambda a: a.reshape(-1, 128),
    "ffn_conv_b": lambda a: a.reshape(-1, 128),
    "ffn_w_down": lambda a: a.reshape(-1, D),
    "final_norm": lambda a: a.reshape(16, 128),
}


def kernel(**inputs):
    S = inputs["x"].shape[1]
    L = inputs["w_in"].shape[0]
    nc = build_nc(S, L)
    m = {}
    for k, f in _IN2D.items():
        m[k] = np.ascontiguousarray(f(np.asarray(inputs[k])))
    m.update(host_consts(S))
    import os
    if os.environ.get("MKTRACE"):
        res = run_bass_kernel_spmd(nc, [m], core_ids=[0], trace=True)
        print("EXEC_TIME_NS", res.exec_time_ns)
        global TRACE
        TRACE = res
    else:
        res = run_bass_kernel_spmd(nc, [m], core_ids=[0])
    global LAST
    LAST = res.results[0]
    return np.asarray(res.results[0]["out"]).reshape(1, S, D)
```

```python
import numpy as np
from contextlib import ExitStack
import concourse.bass as bass
import concourse.mybir as mybir
from concourse.bass_utils import run_bass_kernel_spmd

F32 = mybir.dt.float32
BF16 = mybir.dt.bfloat16
I32 = mybir.dt.int32
AF = mybir.ActivationFunctionType
ALU = mybir.AluOpType
AX = mybir.AxisListType

NSLOT = 12


class Res:
    __slots__ = ("name", "last_w", "readers")

    def __init__(self, name=""):
        self.name = name
        self.last_w = None
        self.readers = {}


class T:
    def __init__(self, ap, name, nres=1):
        self.ap = ap
        self.name = name
        self.res = [Res(f"{name}.{i}") for i in range(nres)]

    def __getitem__(self, idx):
        return self.ap[idx]

    @property
    def r(self):
        return self.res[0]


class Prog:
    ENG = ("pe", "act", "dve", "pool", "sp")

    def __init__(self, nc):
        self.nc = nc
        self.es = ExitStack()
        self.ops = {e: [] for e in self.ENG}
        self.sems = {}
        self.cnt = {}
        self.known = {e: {} for e in self.ENG}
        self.dma_i = {e: 0 for e in self.ENG}
        self.nalloc = 0
        self.scopes = []
        for e in ("pe", "act", "dve", "pool"):
            self._mksem(e)
        for q in ("sp", "act", "pool"):
            for s in range(NSLOT):
                self._mksem(("dma", q, s))

    def _mksem(self, key):
        nm = "s_" + "_".join(str(k) for k in (key if isinstance(key, tuple) else (key,)))
        self.sems[key] = self.es.enter_context(self.nc.semaphore(nm))
        self.cnt[key] = 0

    def sb(self, shape, dtype=F32, name=None, nres=1):
        self.nalloc += 1
        name = name or f"t{self.nalloc}"
        es = self.scopes[-1] if self.scopes else self.es
        t = es.enter_context(self.nc.sbuf_tensor(f"{name}_{self.nalloc}", list(shape), dtype))
        return T(t, name, nres)

    def mark(self):
        self.scopes.append(ExitStack())
        return len(self.scopes) - 1

    def release(self, mark):
        self.barrier()
        self.flush()
        while len(self.scopes) > mark:
            self.scopes.pop().close()

    def barrier(self):
        for e in self.ENG:
            waits = []
            for k, v in self.cnt.items():
                if v == 0 or (k == "pe" and e == "pe"):
                    continue
                if self.known[e].get(k, 0) < v:
                    self.known[e][k] = v
                    waits.append((k, v))
            if waits:
                self.ops[e].append((waits, None, None))

    def ps(self, shape, dtype=F32, name=None, nres=1):
        self.nalloc += 1
        name = name or f"p{self.nalloc}"
        es = self.scopes[-1] if self.scopes else self.es
        t = es.enter_context(self.nc.psum_tensor(f"{name}_{self.nalloc}", list(shape), dtype))
        return T(t, name, nres)

    def dram(self, name, shape, dtype, kind):
        t = self.nc.dram_tensor(name, list(shape), dtype, kind=kind)
        return T(t.ap(), name)

    def _deps(self, eng, reads, writes):
        deps = {}

        def add(kv):
            if kv is None:
                return
            k, v = kv
            if deps.get(k, 0) < v:
                deps[k] = v
        for r in reads:
            add(r.last_w)
        for w in writes:
            add(w.last_w)
            for k, v in w.readers.items():
                add((k, v))
        out = []
        kn = self.known[eng]
        for k, v in deps.items():
            if k == "pe" and eng == "pe":
                continue
            if kn.get(k, 0) >= v:
                continue
            kn[k] = v
            out.append((k, v))
        return out

    @staticmethod
    def _rl(x):
        out = []
        for i in x:
            if isinstance(i, T):
                out.extend(i.res)
            elif isinstance(i, Res):
                out.append(i)
            elif i is None:
                pass
            else:
                out.extend(Prog._rl(i))
        return out

    def op(self, eng, fn, reads=(), writes=(), sync=True):
        reads = self._rl(reads)
        writes = self._rl(writes)
        waits = self._deps(eng, reads, writes)
        if sync:
            self.cnt[eng] += 1
            v = self.cnt[eng]
            self.ops[eng].append((waits, fn, (eng, 1)))
        else:
            v = self.cnt[eng] + 1
            self.ops[eng].append((waits, fn, None))
        for r in reads:
            if r.readers.get(eng, 0) < v:
                r.readers[eng] = v
        for w in writes:
            w.last_w = (eng, v)
            w.readers = {}

    def dma(self, q, out_ap, in_ap, reads=(), writes=(), **kw):
        reads = self._rl(reads)
        writes = self._rl(writes)
        i = self.dma_i[q]
        self.dma_i[q] += 1
        key = ("dma", q, i % NSLOT)
        waits = self._deps(q, reads, writes)
        prev = self.cnt[key]
        if prev > 0 and self.known[q].get(key, 0) < prev:
            self.known[q][key] = prev
            waits.append((key, prev))
        self.cnt[key] += 16
        v = self.cnt[key]

        def fn(e, out_ap=out_ap, in_ap=in_ap, kw=kw):
            return e.dma_start(out=out_ap, in_=in_ap, **kw)
        self.ops[q].append((waits, fn, (key, 16)))
        for r in reads:
            if r.readers.get(key, 0) < v:
                r.readers[key] = v
        for w in writes:
            w.last_w = (key, v)
            w.readers = {}

    def finish(self, out_res):
        for r in self._rl(out_res):
            waits = self._deps("sp", [r], [])
            if waits:
                self.ops["sp"].append((waits, None, None))

    def flush(self):
        nc = self.nc
        sems = self.sems
        ops = self.ops
        if not any(ops[e] for e in self.ENG):
            return

        def replay(e, lst):
            for waits, fn, inc in lst:
                for k, v in waits:
                    e.wait_ge(sems[k], v)
                if fn is None:
                    continue
                ins = fn(e)
                if inc is not None:
                    ins.then_inc(sems[inc[0]], inc[1])

        with nc.Block() as block:
            @block.tensor
            def _(e):
                replay(e, ops["pe"])

            @block.scalar
            def _(e):
                replay(e, ops["act"])

            @block.vector
            def _(e):
                replay(e, ops["dve"])

            @block.gpsimd
            def _(e):
                replay(e, ops["pool"])

            @block.sync
            def _(e):
                replay(e, ops["sp"])
        self.nops = getattr(self, "nops", 0) + sum(len(v) for v in ops.values())
        self.ops = {e: [] for e in self.ENG}

    def build(self):
        self.flush()
        while self.scopes:
            self.scopes.pop().close()
        self.es.close()

    def mm(self, out, lhsT, rhs, start, stop, reads, writes, sync=None):
        if sync is None:
            sync = stop
        self.op("pe", lambda e: e.matmul(out, lhsT, rhs, start=start, stop=stop),
                reads, writes, sync=sync)

    def tr(self, out, in_, ident, reads, writes):
        self.op("pe", lambda e: e.transpose(out, in_, ident), reads, writes)

    def actf(self, out, in_, func, reads, writes, bias=None, scale=None, accum_out=None, eng="act"):
        kw = {}
        if bias is not None:
            kw["bias"] = bias
        if scale is not None:
            kw["scale"] = scale
        if accum_out is not None:
            kw["accum_out"] = accum_out
        self.op("act", lambda e: e.activation(out, in_, func, **kw), reads, writes)

    def tt(self, eng, out, in0, in1, op, reads, writes):
        self.op(eng, lambda e: e.tensor_tensor(out, in0, in1, op), reads, writes)

    def ts(self, eng, out, in0, s1, s2, op0, op1, reads, writes):
        if op1 is None:
            self.op(eng, lambda e: e.tensor_scalar(out, in0, s1, None, op0), reads, writes)
        else:
            self.op(eng, lambda e: e.tensor_scalar(out, in0, s1, s2, op0, op1), reads, writes)

    def stt(self, out, in0, scalar, in1, op0, op1, reads, writes):
        self.op("dve", lambda e: e.scalar_tensor_tensor(out, in0, scalar, in1, op0, op1), reads, writes)

    def copy(self, eng, out, in_, reads, writes):
        if eng == "act":
            self.op("act", lambda e: e.copy(out, in_), reads, writes)
        else:
            self.op(eng, lambda e: e.tensor_copy(out, in_), reads, writes)

    def memset(self, eng, ap, val, writes):
        self.op(eng, lambda e: e.memset(ap, val), (), writes)


import math
import ml_dtypes

D = 2048
KC = 16
HD = 128
DFF = 5632
NCOL = 6924
NROPE = 15
NCE = NCOL + NROPE * 128
EPS = 1e-6
SCALE = HD ** -0.5

C_RQ, C_RK, C_RV, C_RG = 0, 512, 1024, 1536
C_SQ, C_SK, C_SV = 2048, 2560, 3072
C_GQ, C_GF, C_GI, C_GG = 3584, 4096, 4608, 5120
C_NQ = 5632
C_KC, C_VC, C_KS, C_VS, C_KW, C_VW, C_NG = 6144, 6272, 6400, 6528, 6656, 6784, 6912
ROPE_COLS = [C_RQ + 128 * h for h in range(4)] + [C_RK + 128 * h for h in range(4)] + \
            [C_NQ + 128 * h for h in range(4)] + [C_KC, C_KS, C_KW]
FB_RQ, FB_RK, FB_NQ, FB_KC, FB_KS, FB_KW = 0, 4, 8, 12, 13, 14
FB_SQ, FB_SK, FB_GQ, FB_VC = 15, 19, 23, 27
NFB = 28
FMB_PLAIN = [(C_SQ + 128 * h, FB_SQ + h) for h in range(4)] + [(C_SK + 128 * h, FB_SK + h) for h in range(4)] + \
            [(C_GQ + 128 * h, FB_GQ + h) for h in range(4)] + [(C_VC, FB_VC)]
FF_GF, FF_RG, FF_GG = 0, 4, 8
NFF = 12
FMF = [(C_GF + 128 * h, FF_GF + h) for h in range(4)] + [(C_RG + 128 * h, FF_RG + h) for h in range(4)] + \
      [(C_GG + 128 * h, FF_GG + h) for h in range(4)]
TMB = [(C_RV, 512, 0), (C_SV, 512, 512), (C_GI, 512, 1024), (C_VS, 128, 1536), (C_VW, 128, 1664)]
TM_RV, TM_SV, TM_GI, TM_VS, TM_VW = 0, 512, 1024, 1536, 1664
NTMC = 1792


def host_consts(S):
    c = {}
    f = np.zeros((128, 968), np.float32)
    f[:, 0:128] = np.eye(128)
    f[:, 128:256] = 1.0
    s_ = np.arange(128)
    f[:, 256:384] = (s_[:, None] >= s_[None, :]).astype(np.float32)
    half = 64
    inv = (10000.0 ** (-np.arange(half, dtype=np.float32) / half)).astype(np.float32)
    f[:, 384] = np.concatenate([inv, inv])
    sign = np.concatenate([-np.ones(64), np.ones(64)]).astype(np.float32)
    f[:, 385] = sign
    f[:, 386] = -math.pi * sign
    f[:, 387] = -math.pi
    f[:, 388] = 1.0
    f[:, 389] = EPS
    f[:64, 392:456] = (np.arange(64)[:, None] <= np.arange(64)[None, :]).astype(np.float32)
    rm = np.ones(512, np.float32)
    rm[::64] = 0.0
    f[:, 456:968] = rm[None, :]
    c["cf32"] = f
    dec = np.zeros((128, 4, 5, 512), np.float32)
    ml = np.arange(128)[:, None].astype(np.float64)
    nl = np.arange(512)[None, :].astype(np.float64)
    for h in range(4):
        lg = math.log1p(-2.0 ** (-5.0 - h))
        dec[:, h, 0, :] = np.exp(lg * (nl - ml))
        for j in range(4):
            e = nl - (128 * j + ml)
            dec[:, h, 1 + j, :] = np.where(e >= 0, np.exp(lg * np.maximum(e, 0)), 0.0)
    c["dec"] = dec.reshape(128, -1)
    mk = np.zeros((128, 21, 512), np.float32)
    for j in range(4):
        mk[:, j, :] = (128 * j + ml < nl)
        mk[:, 4 + j, :] = (128 * j + ml <= nl)
    for i, dj in enumerate(range(-4, 4)):
        key = 128 * dj + ml
        mk[:, 8 + i, :] = (key <= nl) & (nl - key < 512)
    for i, dl in enumerate([0, 512, 1024, 1536, 2048]):
        mk[:, 16 + i, :] = (16 * ml + 31 <= dl + nl)
    c["masks"] = mk.reshape(128, -1).astype(ml_dtypes.bfloat16)
    NSL = S // 64
    NCMP = (S - 32) // 16 + 1
    NCT = (NCMP + 127) // 128
    ex = (np.arange(S)[None, :] // 64 == np.arange(128)[:, None]).astype(np.float32)
    c["ex"] = ex.astype(ml_dtypes.bfloat16)
    n_ = np.arange(NCT * 128)
    cs = n_ * 16
    ss = np.arange(NSL) * 64
    ov = np.minimum(cs[:, None] + 32, ss[None, :] + 64) - np.maximum(cs[:, None], ss[None, :])
    ov = np.clip(ov, 0, None) / 32.0
    ov[n_ >= NCMP] = 0.0
    c["ovl"] = ov.reshape(NCT, 128, NSL).transpose(1, 0, 2).reshape(128, NCT * NSL).astype(ml_dtypes.bfloat16)
    t_ = np.arange(S)[:, None]
    j_ = np.arange(NSL)[None, :]
    cur = t_ // 64
    future = j_ > cur
    A = np.ones((S, NSL), np.float32)
    B = np.zeros((S, NSL), np.float32)
    for cond, val in ((j_ == 0, 1e4), (j_ == cur, 1e4 + 1), (j_ == cur - 1, 1e4 + 2)):
        cond = np.broadcast_to(cond, (S, NSL))
        A[cond] = 0.0
        B[cond] = val
    fut = np.broadcast_to(future, (S, NSL))
    A[fut] = 0.0
    B[fut] = -1.0
    c["impA"] = A
    c["impB"] = B
    cs_ = np.zeros((128, 16 + 512), np.float32)
    for h in range(4):
        cs_[:, h * 4 + h] = 1.0
        cs_[h, 16 + h * 128:16 + (h + 1) * 128] = 1.0
    c["csel"] = cs_
    return c


class MK:
    def __init__(self, S, L, do_hg=True, do_nsa=True):
        self.S, self.L = S, L
        self.NT = S // 512
        self.do_hg, self.do_nsa = do_hg, do_nsa
        nc = bass.Bass("TRN2", target_bir_lowering=False)
        self.nc = nc
        p = self.p = Prog(nc)
        NT = self.NT
        di = lambda n, sh, dt=F32: p.dram(n, sh, dt, "ExternalInput")
        self.x = di("x", [S, D])
        self.mem = di("mem", [256, D])
        self.pos = di("positions", [1, S], I32)
        self.g_mix = di("mix_norm", [L * 16, 128])
        self.w_in = di("w_in", [L * D, NCOL])
        self.g_ret = di("ret_norm", [L * 4, 128])
        self.lbl = di("hgrn_lb_logits", [L * 4, 128])
        self.g_hg = di("hgrn_norm", [L * 4, 128])
        self.pos_k = di("nsa_pos_k", [L * 32, 128])
        self.pos_v = di("nsa_pos_v", [L * 32, 128])
        self.w_ck = di("nsa_w_ck", [L * 32 * 128, 128])
        self.w_cv = di("nsa_w_cv", [L * 32 * 128, 128])
        self.w_out = di("w_out", [L * D, D])
        self.g_xa = di("xattn_norm", [L * 16, 128])
        self.g_mem = di("mem_norm", [L * 16, 128])
        self.wq = di("xattn_wq", [L * D, 512])
        self.wk = di("xattn_wk", [L * D, 512])
        self.wv = di("xattn_wv", [L * D, 512])
        self.wo = di("xattn_wo", [L * 512, D])
        self.g_ffn = di("ffn_norm", [L * 16, 128])
        self.w_up = di("ffn_w_up", [L * D, 2 * DFF])
        self.cw = di("ffn_conv_w", [L * 3 * 88, 128])
        self.cb = di("ffn_conv_b", [L * 88, 128])
        self.w_dn = di("ffn_w_down", [L * DFF, D])
        self.g_fin = di("final_norm", [16, 128])
        self.cf32_d = di("cf32", [128, 968])
        self.dec_d = di("dec", [128, 4 * 5 * 512])
        self.masks_d = di("masks", [128, 21 * 512], BF16)
        self.NSL = S // 64
        self.NCMP = (S - 32) // 16 + 1
        self.NCT = (self.NCMP + 127) // 128
        self.ex_d = di("ex", [128, S], BF16)
        self.ovl_d = di("ovl", [128, self.NCT * self.NSL], BF16)
        self.impA_d = di("impA", [S, self.NSL])
        self.impB_d = di("impB", [S, self.NSL])
        self.csel_d = di("csel", [128, 528])
        self.out = p.dram("out", [S, D], F32, "ExternalOutput")
        ds = lambda n, sh, dt, nres=1: self._scratch(n, sh, dt, nres)
        self.xT = ds("xT", [D, S], F32, NT)
        self.cosT = ds("cosT", [128, S], F32)
        self.sinT = ds("sinT", [128, S], F32)
        self.wb_in_set = [ds("wb_in_a", [(55 + NROPE) * 128, KC * 128], BF16), ds("wb_in_b", [(55 + NROPE) * 128, KC * 128], BF16)]
        self.wb_out_set = [ds("wb_out_a", [16 * 128, KC * 128], BF16), ds("wb_out_b", [16 * 128, KC * 128], BF16)]
        self.wb_q_set = [ds("wb_q_a", [4 * 128, KC * 128], BF16), ds("wb_q_b", [4 * 128, KC * 128], BF16)]
        self.wb_k_set = [ds("wb_k_a", [4 * 128, KC * 128], BF16), ds("wb_k_b", [4 * 128, KC * 128], BF16)]
        self.wb_v_set = [ds("wb_v_a", [4 * 128, KC * 128], BF16), ds("wb_v_b", [4 * 128, KC * 128], BF16)]
        self.wb_o_set = [ds("wb_o_a", [16 * 128, 4 * 128], BF16), ds("wb_o_b", [16 * 128, 4 * 128], BF16)]
        self.wb_up_set = [ds("wb_up_a", [88 * 128, KC * 128], BF16), ds("wb_up_b", [88 * 128, KC * 128], BF16)]
        self.wb_dn_set = [ds("wb_dn_a", [16 * 128, 44 * 128], BF16), ds("wb_dn_b", [16 * 128, 44 * 128], BF16)]
        self.fmb = ds("fmb", [NFB * 128, S], BF16)
        self.fmf = ds("fmf", [NFF * 128, S], F32)
        self.ngT = ds("ngT", [12, S], F32)
        self.tmb = ds("tmb", [S, NTMC], BF16)
        self.mixT = ds("mixT", [D, S], BF16)
        self.cf = p.sb([128, 968], F32, "cf")
        p.dma("sp", self.cf[:], self.cf32_d[:], [self.cf32_d], [self.cf])
        self.ident = self.cf[:, 0:128]
        self.ones_f = self.cf[:, 128:256]
        self.uincl = self.cf[:, 256:384]
        self.ones_b = p.sb([128, 128], BF16, "ones_b")
        p.copy("dve", self.ones_b[:], self.ones_f, [self.cf], [self.ones_b])
        self.ident_b = p.sb([128, 128], BF16, "ident_b")
        p.copy("dve", self.ident_b[:], self.ident, [self.cf], [self.ident_b])
        self.perm_b = p.sb([128, 128], BF16, "perm_b")
        p.copy("dve", self.perm_b[:, 0:64], self.cf[:, 64:128], [self.cf], [self.perm_b])
        p.copy("dve", self.perm_b[:, 64:128], self.cf[:, 0:64], [self.cf], [self.perm_b])
        self.masks = p.sb([128, 4 * 512], BF16, "masks")
        p.dma("sp", self.masks[:], self.masks_d[:, 0:4 * 512], [self.masks_d], [self.masks])
        self.tmp_rows = p.sb([128, 128], F32, "tmp_rows")
        self.gains = p.sb([128, 88 + 4 * 88], F32, "gains")
        self.stg_f = [p.sb([128, 512], F32, f"stgf{i}") for i in range(4)]
        self.stg_b = [p.sb([128, 512], BF16, f"stgb{i}") for i in range(4)]
        self.memhat = p.sb([128, KC, 256], F32, "memhat")
        self.lb = p.sb([128, L * 4], F32, "lb")
        self.oml = p.sb([128, L * 4], F32, "oml")
        self.si = 0
        self.qi = 0
        npairs = 4 * sum(4 * qc + 6 for qc in range(self.NT))
        self.bg_every = max(1, npairs // 330)

    def _scratch(self, n, sh, dt, nres=1):
        import os
        if os.environ.get("MKDEBUG"):
            t = self.nc.dram_tensor(n, list(sh), dt, kind="ExternalOutput")
        else:
            t = self.nc.dram_tensor(n, list(sh), dt)
        tt = T(t.ap(), n, nres)
        return tt

    def phase(self):
        m = self.p.mark()
        self.psum = [self.p.ps([128, 512], F32, f"ps{i}") for i in range(8)]
        return m

    def mask(self, i):
        return self.masks[:, i * 512:(i + 1) * 512]

    def load_cols(self, src, r0, R, dst_ap, dst_t):
        p = self.p
        tmp = self.tmp_rows
        p.dma("sp", tmp[0:R, :], src[r0:r0 + R, :], [src], [tmp])
        ps = self.psum[7]
        p.op("pe", lambda e: e.transpose(ps[:, 0:R], tmp[0:R, :], self.ident[0:R, 0:R]), [tmp, self.cf], [ps])
        p.copy("dve", dst_ap, ps[:, 0:R], [ps], [dst_t])

    def precast_all(self, l, engs):
        ws = {nm: getattr(self, nm + "_set")[l % 2] for nm in
              ("wb_in", "wb_out", "wb_q", "wb_k", "wb_v", "wb_o", "wb_up", "wb_dn")}
        yield from self.precast(self.w_in, l * D, D, NCOL, ws["wb_in"], engs=engs)
        yield from self.precast(self.w_out, l * D, D, D, ws["wb_out"], engs=engs)
        yield from self.precast(self.wq, l * D, D, 512, ws["wb_q"], engs=engs)
        yield from self.precast(self.wk, l * D, D, 512, ws["wb_k"], engs=engs)
        yield from self.precast(self.wv, l * D, D, 512, ws["wb_v"], engs=engs)
        yield from self.precast(self.wo, l * 512, 512, D, ws["wb_o"], engs=engs)
        yield from self.precast(self.w_up, l * D, D, 2 * DFF, ws["wb_up"], engs=engs)
        yield from self.precast(self.w_dn, l * DFF, DFF, D, ws["wb_dn"], engs=engs)

    def precast(self, src, r0, K, N, dst, ext=None, engs=("act", "dve", "pool")):
        p = self.p
        CH = 2048
        i = 0
        for kt in range(K // 128):
            kc0 = kt * 128
            for c0 in range(0, N, CH):
                n = min(CH, N - c0)
                st = self.cast_f[i % 2]
                sb_ = self.cast_b[i % 2]
                p.dma("sp", st[:, 0:n], src[r0 + kt * 128:r0 + (kt + 1) * 128, c0:c0 + n], [src], [st])
                eng = engs[i % len(engs)]
                p.copy(eng, sb_[:, 0:n], st[:, 0:n], [st], [sb_])
                ct0 = c0 // 128
                nfull = n // 128
                if nfull:
                    p.dma("pool", dst[ct0 * 128:(ct0 + nfull) * 128, kc0:kc0 + 128].rearrange("(ct p) c -> p ct c", p=128),
                          sb_[:, 0:nfull * 128].rearrange("p (ct c) -> p ct c", c=128), [sb_], [dst])
                rem = n - nfull * 128
                if rem:
                    ctl = ct0 + nfull
                    p.dma("pool", dst[ctl * 128:(ctl + 1) * 128, kc0:kc0 + rem], sb_[:, nfull * 128:n], [sb_], [dst])
                if ext is not None:
                    for ri, rc in enumerate(ext):
                        if c0 <= rc and rc + 128 <= c0 + n:
                            et = 55 + ri
                            lo = rc - c0
                            p.dma("pool", dst[et * 128:(et + 1) * 128, kc0:kc0 + 64], sb_[:, lo + 64:lo + 128], [sb_], [dst])
                            p.dma("pool", dst[et * 128:(et + 1) * 128, kc0 + 64:kc0 + 128], sb_[:, lo:lo + 64], [sb_], [dst])
                        else:
                            assert not (rc < c0 + n and rc + 128 > c0), "rope head straddles cast chunk"
                i += 1
                yield

    def norm_tile(self, tt, gcols, hT, hoff):
        p = self.p
        xs = self.xs[tt % 2]
        p.dma("sp", xs[:], self.xT[:, tt * 512:(tt + 1) * 512].rearrange("(k p) n -> p k n", p=128),
              [self.xT.res[tt]], [xs])
        self.norm_from_sbuf(xs, gcols, hT, hoff)
        return xs

    def norm_from_sbuf(self, xs, gcols, hT, hoff, n=512):
        p = self.p
        sq = self.sq
        p.op("act", lambda e: e.activation(sq[:, :, 0:n], xs[:, :, 0:n], AF.Square), [xs], [sq])
        ps = self.psum[6]
        for k in range(KC):
            p.mm(ps[:, 0:n], self.ones_b[:], sq[:, k, 0:n], k == 0, k == KC - 1, [self.ones_b, sq], [ps])
        rs = self.rstd
        p.op("act", lambda e: e.activation(rs[:, 0:n], ps[:, 0:n], AF.Sqrt, bias=self.cf[:, 389:390], scale=1.0 / D),
             [ps, self.cf], [rs])
        p.op("dve", lambda e: e.reciprocal(rs[:, 0:n], rs[:, 0:n]), [rs], [rs])
        for k in range(KC):
            p.stt(hT[:, k, hoff:hoff + n], xs[:, k, 0:n], gcols[:, k:k + 1], rs[:, 0:n], ALU.mult, ALU.mult,
                  [xs, rs, self.gains], [hT])

    def load_w(self, wb, c0, n, kchunks=KC, r0=0):
        p = self.p
        wt = self.wt[self.qi % len(self.wt)]
        self.qi += 1
        ct = c0 // 128
        assert c0 % 128 == 0
        src = wb[ct * 128:(ct + 1) * 128, 0:kchunks * 128].rearrange("p (k c) -> p k c", c=128)
        if n == 128:
            p.dma("sp", wt[:, 0:kchunks, :], src, [wb], [wt])
        else:
            p.dma("sp", wt[:, 0:kchunks, 0:n], src[:, :, 0:n], [wb], [wt])
        return wt

    def build(self):
        p, S, L, NT = self.p, self.S, self.L, self.NT
        mk0 = self.phase()
        ps = self.psum
        self.cast_f = [p.sb([128, 2048], F32, f"castf{i}") for i in range(2)]
        self.xs = [p.sb([128, KC, 512], F32, f"xs{i}") for i in range(2)]
        G = self.gains

        x4 = self.xs[0]
        for tt in range(NT):
            xin = self.xs[0]
            p.dma("sp", xin[:].rearrange("p k (j c) -> p j (k c)", j=4)[:, :, :] if False else
                  xin[:].rearrange("p k n -> p (k n)").rearrange("p (j f) -> p j f", j=4),
                  self.x[tt * 512:(tt + 1) * 512, :].rearrange("(j p) f -> p j f", p=128), [self.x], [xin])
            xv = xin[:].rearrange("p k n -> p (k n)").rearrange("p (j f) -> p j f", j=4)
            xo = self.xs[1]
            for k in range(KC):
                pk = ps[k % 4]
                for j in range(4):
                    p.op("pe", lambda e, pk=pk, j=j, k=k: e.transpose(pk[:, j * 128:(j + 1) * 128],
                                                                       xv[:, j, k * 128:(k + 1) * 128], self.ident),
                         [xin, self.cf], [pk])
                p.copy("act" if k % 2 else "dve", xo[:, k, :], pk[:], [pk], [xo])
            p.dma("pool", self.xT[:, tt * 512:(tt + 1) * 512].rearrange("(k p) n -> p k n", p=128), xo[:],
                  [xo], [self.xT.res[tt]])

        posi = p.sb([128, 512], I32, "posi")
        self.stg_rope = [p.sb([128, 512], F32, f"stgr{i}") for i in range(4)]
        for tt in range(NT):
            p.dma("sp", posi[:], self.pos[0:1, tt * 512:(tt + 1) * 512].to_broadcast([128, 512]), [self.pos], [posi])
            pf = self.stg_rope[0]
            p.copy("dve", pf[:], posi[:], [posi], [pf])
            for which, (dst, shift) in enumerate(((self.cosT, 0.5 * math.pi), (self.sinT, 0.0))):
                a = self.stg_rope[1 + which]
                nf = self.stg_rope[3]
                p.ts("dve", a[:], pf[:], self.cf[:, 384:385], shift, ALU.mult, ALU.add, [pf, self.cf], [a])
                p.ts("dve", nf[:], a[:], 1.0 / (2.0 * math.pi), None, ALU.mult, None, [a], [nf])
                p.copy("dve", posi[:], nf[:], [nf], [posi])
                p.copy("dve", nf[:], posi[:], [posi], [nf])
                p.stt(a[:], nf[:], -2.0 * math.pi, a[:], ALU.mult, ALU.add, [nf, a], [a])
                p.ts("dve", nf[:], a[:], math.pi, 2.0 * math.pi, ALU.is_gt, ALU.mult, [a], [nf])
                p.tt("dve", a[:], a[:], nf[:], ALU.subtract, [a, nf], [a])
                p.ts("dve", nf[:], a[:], -math.pi, 2.0 * math.pi, ALU.is_lt, ALU.mult, [a], [nf])
                p.tt("dve", a[:], a[:], nf[:], ALU.add, [a, nf], [a])
                if which == 0:
                    p.op("act", lambda e, a=a: e.activation(a[:], a[:], AF.Sin), [a], [a])
                else:
                    p.op("act", lambda e, a=a: e.activation(a[:], a[:], AF.Sin, scale=self.cf[:, 385:386]),
                         [a, self.cf], [a])
                p.dma("pool", dst[:, tt * 512:(tt + 1) * 512], a[:], [a], [dst])

        L4 = L * 4
        lbe = self.stg_rope[0]
        self.load_cols(self.lbl, 0, L4, lbe[:, 0:L4], lbe)
        p.op("act", lambda e: e.activation(lbe[:, 0:L4], lbe[:, 0:L4], AF.Exp), [lbe], [lbe])
        ssum = self.stg_rope[1]
        p.copy("dve", ssum[:, 0:4], lbe[:, 0:4], [lbe], [ssum])
        for l in range(1, L):
            p.tt("dve", ssum[:, 0:4], ssum[:, 0:4], lbe[:, l * 4:(l + 1) * 4], ALU.add, [ssum, lbe], [ssum])
        p.op("dve", lambda e: e.reciprocal(ssum[:, 0:4], ssum[:, 0:4]), [ssum], [ssum])
        for l in range(L):
            p.tt("dve", lbe[:, l * 4:(l + 1) * 4], lbe[:, l * 4:(l + 1) * 4], ssum[:, 0:4], ALU.mult, [ssum, lbe], [lbe])
        lb = self.lb
        p.memset("dve", lb[:, 0:4], 0.0, [lb])
        for l in range(1, L):
            p.tt("dve", lb[:, l * 4:(l + 1) * 4], lb[:, (l - 1) * 4:l * 4], lbe[:, l * 4:(l + 1) * 4], ALU.add, [lb, lbe], [lb])
        p.ts("dve", self.oml[:], lb[:], -1.0, 1.0, ALU.mult, ALU.add, [lb], [self.oml])
        p.release(mk0)
        import os
        if os.environ.get("MKSTOP") == "p0":
            p.finish([self.xT, self.cosT, self.sinT])
            p.build()
            return self.nc
        for l in range(L):
            self.layer(l)

        mf = self.phase()
        ps = self.psum
        self.cast_f = [p.sb([128, 2048], F32, f"castf{i}") for i in range(2)]
        self.xs = [p.sb([128, KC, 512], F32, f"xs{i}") for i in range(2)]
        self.sq = p.sb([128, KC, 512], BF16, "sq")
        self.rstd = p.sb([128, 512], F32, "rstd")
        self.hT = p.sb([128, KC, 512], BF16, "hT")
        self.load_cols(self.g_fin, 0, 16, G[:, 0:16], G)
        for tt in range(NT):
            hT = self.hT
            self.norm_tile(tt, G[:, 0:16], hT, 0)
            xs = self.xs[tt % 2]
            yo = self.xs[(tt + 1) % 2]
            for k in range(KC):
                p.stt(yo[:, k, :], xs[:, k, :], G[:, k:k + 1], self.rstd[:], ALU.mult, ALU.mult, [xs, self.rstd, G], [yo])
            import os
            if os.environ.get("MKDEBUG"):
                d = self._scratch(f"dbg_rstd{tt}", [128, 512], F32)
                p.dma("sp", d[:], self.rstd[:], [self.rstd], [d])
                p.finish([d])
            ot = self.sq
            for j in range(4):
                of = self.cast_f[j % 2]
                for k in range(KC):
                    pk = ps[k % 4]
                    p.op("pe", lambda e, pk=pk, j=j, k=k, yo=yo: e.transpose(pk[:, 0:128], yo[:, k, j * 128:(j + 1) * 128],
                                                                       self.ident), [yo, self.cf], [pk])
                    p.copy("act" if k % 2 else "dve", of[:, k * 128:(k + 1) * 128], pk[:, 0:128], [pk], [of])
                p.dma("pool", self.out[tt * 512 + j * 128: tt * 512 + (j + 1) * 128, :], of[:], [of], [self.out])
        p.finish([self.out])
        p.release(mf)
        p.build()
        return self.nc

    def stage(self, kind):
        self.si += 1
        return (self.stg_f if kind == "f" else self.stg_b)[self.si % 4]

    def fresh(self, kind):
        return self.p.sb([128, 512], F32 if kind == "f" else BF16)

    def layer(self, l):
        p, S, L, NT = self.p, self.S, self.L, self.NT
        G = self.gains
        m = self.phase()
        self.load_cols(self.g_mix, l * 16, 16, G[:, 0:16], G)
        self.load_cols(self.g_xa, l * 16, 16, G[:, 16:32], G)
        self.load_cols(self.g_mem, l * 16, 16, G[:, 32:48], G)
        self.load_cols(self.g_ffn, l * 16, 16, G[:, 48:64], G)
        self.load_cols(self.g_ret, l * 4, 4, G[:, 64:68], G)
        self.load_cols(self.g_hg, l * 4, 4, G[:, 68:72], G)
        for j in range(3):
            self.load_cols(self.cw, (l * 3 + j) * 88, 88, G[:, 88 + j * 88: 88 + (j + 1) * 88], G)
        self.load_cols(self.cb, l * 88, 88, G[:, 88 + 3 * 88: 88 + 4 * 88], G)
        for nm in ("wb_in", "wb_out", "wb_q", "wb_k", "wb_v", "wb_o", "wb_up", "wb_dn"):
            setattr(self, nm, getattr(self, nm + "_set")[l % 2])
        if l == 0:
            self.cast_f = [p.sb([128, 2048], F32, f"castf{i}") for i in range(2)]
            self.cast_b = [p.sb([128, 2048], BF16, f"castb{i}") for i in range(2)]
            for _ in self.precast_all(0, ("act", "dve", "pool")):
                pass
        p.release(m)

        def norm_bufs(ntok):
            self.xs = [p.sb([128, KC, 512], F32, "xs0")] * 2
            self.sq = p.sb([128, KC, 512], BF16, "sq")
            self.rstd = p.sb([128, 512], F32, "rstd")
            self.hT = p.sb([128, KC, ntok], BF16, "hT")
        m = self.phase()
        norm_bufs(1024)
        self.wt = [p.sb([128, KC, 128], BF16, f"wt{i}") for i in range(3)]
        self.cast_f = [p.sb([128, 1024], F32, f"rope{i}") for i in range(2)]
        self.xbb = [p.sb([128, 512], BF16, f"xbb{i}") for i in range(2)]
        self.wtw = p.sb([128, KC, 512], BF16, "wtw")
        self.tmo = [p.sb([128, 512], BF16, f"tmo{i}") for i in range(2)]
        self.in_proj(l)
        p.release(m)
        m = self.phase()
        self.big_b = [p.sb([128, S], BF16, f"bigb{i}") for i in range(2)]
        self.dec_sb = p.sb([128, 5 * 512], F32, "dec_sb")
        self.qbuf = [p.sb([128, 512], BF16, f"qb{i}") for i in range(4)]
        self.rwbuf = [p.sb([128, 512], BF16, f"rw{i}") for i in range(3)]
        self.retention(l)
        p.release(m)
        m = self.phase()
        self.big_b = [p.sb([128, S], BF16, f"bigb{i}") for i in range(2)]
        self.spsum = p.sb([128, 512], F32, "spsum")
        self.qbuf = [p.sb([128, 512], BF16, f"qb{i}") for i in range(4)]
        self.bg = None
        if l + 1 < L:
            self.cast_f = [p.sb([128, 2048], F32, f"castf{i}") for i in range(2)]
            self.cast_b = [p.sb([128, 2048], BF16, f"castb{i}") for i in range(2)]
            self.bg = self.precast_all(l + 1, ("dve",))
        self.stickbreak(l)
        if self.bg is not None:
            for _ in self.bg:
                pass
            self.bg = None
        p.release(m)
        m = self.phase()
        if self.do_hg:
            self.hgrn(l)
        else:
            self.zero_mix(1024, 1536)
        p.release(m)
        m = self.phase()
        if self.do_nsa:
            self.nsa(l)
        else:
            self.zero_mix(1536, 2048)
        p.release(m)
        m = self.phase()
        self.hT = p.sb([128, KC, 1024], BF16, "hT")
        self.wt = [p.sb([128, KC, 128], BF16, f"wt{i}") for i in range(3)]
        self.out_proj(l)
        self.snap("dbg_x1")
        p.release(m)
        m = self.phase()
        norm_bufs(1024)
        self.wt = [p.sb([128, KC, 128], BF16, f"wt{i}") for i in range(3)]
        self.cast_f = [p.sb([128, 2048], F32, f"memin{i}") for i in range(2)]
        self.memT = p.sb([128, KC, 256], BF16, "memT")
        self.kTm = p.sb([128, 4, 256], BF16, "kTm")
        self.vm = p.sb([128, 2, 512], BF16, "vm")
        self.oT = p.sb([128, 4, 1024], BF16, "oT")
        self.qbuf = [p.sb([128, 512], BF16, f"qb{i}") for i in range(2)]
        self.xattn(l)
        self.snap("dbg_x2")
        p.release(m)
        m = self.phase()
        self.carry = p.sb([128, 88, 2], F32, "carry")
        self.ffn(l)
        self.snap("dbg_x3")
        p.release(m)

    def snap(self, name):
        import os
        if not os.environ.get("MKDEBUG"):
            return
        p = self.p
        d = self._scratch(name + f"_{self.p.nalloc}", [D, self.S], F32)
        self.p.nalloc += 1
        self.dbg = getattr(self, "dbg", {})
        self.dbg[name] = d
        for tt in range(self.NT):
            p.dma("sp", d[:, tt * 512:(tt + 1) * 512], self.xT[:, tt * 512:(tt + 1) * 512], [self.xT.res[tt]], [d])
        p.finish([d])

    def zero_mix(self, r0, r1):
        p = self.p
        z = self.stg_b[0]
        p.memset("dve", z[:], 0.0, [z])
        for r in range(r0, r1, 128):
            for tt in range(self.NT):
                p.dma("pool", self.mixT[r:r + 128, tt * 512:(tt + 1) * 512], z[:], [z], [self.mixT])

    def in_proj(self, l):
        p, S, NT = self.p, self.S, self.NT
        ps = self.psum
        G = self.gains
        hT = self.hT
        for st in range(NT // 2):
            for j in range(2):
                self.norm_tile(st * 2 + j, G[:, 0:16], hT, j * 512)
            cs = [p.sb([128, 512], F32, f"cs{st}_{i}") for i in range(0)]
            cos_t = self.cast_f[0]
            sin_t = self.cast_f[1]
            p.dma("sp", cos_t[:, 0:1024], self.cosT[:, st * 1024:(st + 1) * 1024], [self.cosT], [cos_t])
            p.dma("sp", sin_t[:, 0:1024], self.sinT[:, st * 1024:(st + 1) * 1024], [self.sinT], [sin_t])
            for ri, c0 in enumerate(ROPE_COLS):
                wa = self.load_w(self.wb_in, c0, 128)
                for j in range(2):
                    pa, pb = ps[(2 * j) % 4], ps[(2 * j + 1) % 4]
                    for k in range(KC):
                        p.mm(pa[:], wa[:, k, :], hT[:, k, j * 512:(j + 1) * 512], k == 0, k == KC - 1, [wa, hT], [pa])
                    xb = self.xbb[j]
                    p.copy("dve", xb[:], pa[:], [pa], [xb])
                    p.mm(pb[:], self.perm_b[:], xb[:], True, True, [self.perm_b, xb], [pb])
                    t1 = self.stage("f")
                    t2 = self.stage("f")
                    ob = self.stage("b")
                    p.tt("dve", t1[:], pa[:], cos_t[:, j * 512:(j + 1) * 512], ALU.mult, [pa, cos_t], [t1])
                    p.tt("dve", t2[:], pb[:], sin_t[:, j * 512:(j + 1) * 512], ALU.mult, [pb, sin_t], [t2])
                    p.tt("pool", ob[:], t1[:], t2[:], ALU.add, [t1, t2], [ob])
                    tt = st * 2 + j
                    p.dma("pool", self.fmb[ri * 128:(ri + 1) * 128, tt * 512:(tt + 1) * 512], ob[:], [ob], [self.fmb])
            for (c0, idx), kind in [(x_, "b") for x_ in FMB_PLAIN] + [(x_, "f") for x_ in FMF] + [((C_NG, 0), "g")]:
                ncol = 12 if kind == "g" else 128
                wa = self.load_w(self.wb_in, c0, ncol)
                for j in range(2):
                    pa = ps[j % 4]
                    for k in range(KC):
                        p.mm(pa[0:ncol, :], wa[:, k, 0:ncol], hT[:, k, j * 512:(j + 1) * 512], k == 0, k == KC - 1,
                             [wa, hT], [pa])
                    tt = st * 2 + j
                    if kind == "b":
                        ob = self.stage("b")
                        p.copy("act", ob[:], pa[:], [pa], [ob])
                        p.dma("pool", self.fmb[idx * 128:(idx + 1) * 128, tt * 512:(tt + 1) * 512], ob[:], [ob], [self.fmb])
                    elif kind == "f":
                        of = self.stage("f")
                        p.copy("act", of[:], pa[:], [pa], [of])
                        p.dma("pool", self.fmf[idx * 128:(idx + 1) * 128, tt * 512:(tt + 1) * 512], of[:], [of], [self.fmf])
                    else:
                        of = self.stage("f")
                        p.copy("act", of[0:12, :], pa[0:12, :], [pa], [of])
                        p.dma("pool", self.ngT[:, tt * 512:(tt + 1) * 512], of[0:12, :], [of], [self.ngT])
            for (c0, ncol, t0) in TMB:
                if ncol == 512:
                    wtw = self.wtw
                    for i4 in range(4):
                        ct = (c0 + i4 * 128) // 128
                        p.dma("sp", wtw[:, :, i4 * 128:(i4 + 1) * 128],
                              self.wb_in[ct * 128:(ct + 1) * 128, 0:KC * 128].rearrange("p (k c) -> p k c", c=128),
                              [self.wb_in], [wtw])
                    for tk in range(8):
                        pa = ps[4 + tk % 4]
                        for k in range(KC):
                            p.mm(pa[:], hT[:, k, tk * 128:(tk + 1) * 128], wtw[:, k, :], k == 0, k == KC - 1,
                                 [wtw, hT], [pa])
                        ob = self.tmo[tk % 2]
                        p.copy("act" if tk % 2 else "dve", ob[:], pa[:], [pa], [ob])
                        r0 = st * 1024 + tk * 128
                        p.dma("pool", self.tmb[r0:r0 + 128, t0:t0 + 512], ob[:], [ob], [self.tmb])
                    continue
                for cc in range(0, ncol, 128):
                    wa = self.load_w(self.wb_in, c0 + cc, 128)
                    for tk in range(8):
                        pa = ps[tk % 4]
                        for k in range(KC):
                            p.mm(pa[:, 0:128], hT[:, k, tk * 128:(tk + 1) * 128], wa[:, k, :], k == 0, k == KC - 1,
                                 [wa, hT], [pa])
                        ob = self.stage("b")
                        p.copy("act" if tk % 2 else "dve", ob[:, 0:128], pa[:, 0:128], [pa], [ob])
                        r0 = st * 1024 + tk * 128
                        p.dma("pool", self.tmb[r0:r0 + 128, t0 + cc:t0 + cc + 128], ob[:, 0:128], [ob], [self.tmb])

    def headnorm_gate(self, o_ps, center, gcol, gate_idx, mix_row, tt):
        p = self.p
        ps = self.psum
        o = self.stage("f")
        p.copy("act", o[:], o_ps[:], [o_ps], [o])
        st = ps[5]
        if center:
            p.mm(st[:], self.ones_f, o[:], True, True, [self.cf, o], [st])
            cen = self.stage("f")
            p.stt(cen[:], st[:], -1.0 / HD, o[:], ALU.mult, ALU.add, [st, o], [cen])
        else:
            cen = o
        sq = self.stage("f")
        p.op("act", lambda e: e.activation(sq[:], cen[:], AF.Square), [cen], [sq])
        p.mm(st[:], self.ones_f, sq[:], True, True, [self.cf, sq], [st])
        rs = self.stage("f")
        p.op("act", lambda e: e.activation(rs[:], st[:], AF.Sqrt, bias=self.cf[:, 389:390], scale=1.0 / HD),
             [st, self.cf], [rs])
        p.op("dve", lambda e: e.reciprocal(rs[:], rs[:]), [rs], [rs])
        y = sq
        p.stt(y[:], cen[:], gcol, rs[:], ALU.mult, ALU.mult, [cen, rs, self.gains], [y])
        g = self.stage("f")
        p.dma("sp", g[:], self.fmf[gate_idx * 128:(gate_idx + 1) * 128, tt * 512:(tt + 1) * 512], [self.fmf], [g])
        p.op("act", lambda e: e.activation(g[:], g[:], AF.Silu), [g], [g])
        ob = self.stage("b")
        p.tt("dve", ob[:], y[:], g[:], ALU.mult, [y, g], [ob])
        p.dma("pool", self.mixT[mix_row:mix_row + 128, tt * 512:(tt + 1) * 512], ob[:], [ob], [self.mixT])

    def load_head(self, k_idx, v_col):
        p, S = self.p, self.S
        kT = self.big_b[0]
        v = self.big_b[1]
        p.dma("sp", kT[:], self.fmb[k_idx * 128:(k_idx + 1) * 128, :], [self.fmb], [kT])
        p.dma("sp", v[:].rearrange("p (t e) -> p t e", e=128),
              self.tmb[:, v_col:v_col + 128].rearrange("(t p) e -> p t e", p=128), [self.tmb], [v])
        return kT, v

    def retention(self, l):
        p, S, NT = self.p, self.S, self.NT
        ps = self.psum
        dec = self.dec_sb
        for h in range(4):
            lg = math.log1p(-2.0 ** (-5.0 - h))
            kT, v = self.load_head(FB_RK + h, TM_RV + h * 128)
            p.dma("sp", dec[:], self.dec_d[:, h * 5 * 512:(h + 1) * 5 * 512], [self.dec_d], [dec])
            for qc in range(NT):
                qT = self.qbuf[qc % 2]
                p.dma("sp", qT[:], self.fmb[(FB_RQ + h) * 128:(FB_RQ + h + 1) * 128, qc * 512:(qc + 1) * 512],
                      [self.fmb], [qT])
                po = ps[4]
                kts = [kt for kt in range(4 * qc + 4)
                       if kt >= 4 * qc or lg * (512 * qc - 128 * kt - 127) > -85.0]
                n = len(kts)

                def rA(i):
                    kt = kts[i]
                    pa = ps[i % 3]
                    p.mm(pa[:], kT[:, kt * 128:(kt + 1) * 128], qT[:], True, True, [kT, qT], [pa])
                    w = self.rwbuf[i % 3]
                    if kt >= 4 * qc:
                        j = kt - 4 * qc
                        dt = dec[:, (1 + j) * 512:(2 + j) * 512]
                        c = SCALE
                    else:
                        dt = dec[:, 0:512]
                        c = SCALE * math.exp(lg * (512 * qc - 128 * kt))
                    p.stt(w[:], pa[:], c, dt, ALU.mult, ALU.mult, [pa, self.dec_sb], [w])

                def rE(i):
                    kt = kts[i]
                    w = self.rwbuf[i % 3]
                    p.mm(po[:], v[:, kt * 128:(kt + 1) * 128], w[:], i == 0, i == n - 1, [v, w], [po])

                for step in range(n + 2):
                    if step < n:
                        rA(step)
                    if 0 <= step - 2 < n:
                        rE(step - 2)
                self.headnorm_gate(po, True, self.gains[:, 64 + h:65 + h], FF_RG + h, h * 128, qc)

    def stickbreak(self, l):
        p, S, NT = self.p, self.S, self.NT
        ps = self.psum
        spsum = self.spsum
        ebuf = [p.sb([128, 512], F32, f"sbe{i}") for i in range(2)]
        spbuf = [p.sb([128, 512], F32, f"sbsp{i}") for i in range(3)]
        wbuf = [p.sb([128, 512], BF16, f"sbw{i}") for i in range(3)]
        for h in range(4):
            kT, v = self.load_head(FB_SK + h, TM_SV + h * 128)
            for qc in range(NT):
                qT = self.qbuf[qc % 2]
                p.dma("sp", qT[:], self.fmb[(FB_SQ + h) * 128:(FB_SQ + h + 1) * 128, qc * 512:(qc + 1) * 512],
                      [self.fmb], [qT])
                nqT = self.qbuf[2 + qc % 2]
                p.op("act", lambda e, nqT=nqT, qT=qT: e.mul(nqT[:], qT[:], -SCALE), [qT], [nqT])
                p.memset("pool", spsum[:], 0.0, [spsum])
                po = ps[7]
                kts = list(range(4 * qc + 3, -1, -1))
                n = len(kts)

                def stA(i):
                    kt = kts[i]
                    pa = ps[i % 3]
                    p.mm(pa[:], kT[:, kt * 128:(kt + 1) * 128], qT[:], True, True, [kT, qT], [pa])
                    e_ = ebuf[i % 2]
                    p.op("act", lambda e, e_=e_, pa=pa: e.activation(e_[:], pa[:], AF.Exp, scale=SCALE), [pa], [e_])
                    sp = spbuf[i % 3]
                    p.op("act", lambda e, e_=e_, sp=sp: e.activation(sp[:], e_[:], AF.Ln, bias=self.cf[:, 388:389], scale=1.0),
                         [e_, self.cf], [sp])
                    if kt >= 4 * qc:
                        p.tt("dve", sp[:], sp[:], self.mask(kt - 4 * qc), ALU.mult, [sp, self.masks], [sp])

                def stC(i):
                    kt = kts[i]
                    pc = ps[3 + i % 2]
                    sp = spbuf[i % 3]
                    p.mm(pc[:], self.uincl, sp[:], True, False, [self.cf, sp], [pc], sync=False)
                    p.mm(pc[:], self.ones_f, spsum[:], False, False, [self.cf, spsum], [pc], sync=False)
                    p.mm(pc[:], kT[:, kt * 128:(kt + 1) * 128], nqT[:], False, True, [kT, nqT], [pc])
                    w = wbuf[i % 3]
                    p.op("act", lambda e, w=w, pc=pc: e.activation(w[:], pc[:], AF.Exp, scale=-1.0), [pc], [w])
                    if kt >= 4 * qc:
                        p.tt("dve", w[:], w[:], self.mask(kt - 4 * qc), ALU.mult, [w, self.masks], [w])
                    p.tt("pool", spsum[:], spsum[:], sp[:], ALU.add, [spsum, sp], [spsum])

                def stE(i):
                    kt = kts[i]
                    w = wbuf[i % 3]
                    p.mm(po[:], v[:, kt * 128:(kt + 1) * 128], w[:], i == 0, i == n - 1, [v, w], [po])

                for step in range(n + 2):
                    if step < n:
                        stA(step)
                    if 0 <= step - 1 < n:
                        stC(step - 1)
                    if 0 <= step - 2 < n:
                        stE(step - 2)
                    if self.bg is not None and step % self.bg_every == 0:
                        try:
                            next(self.bg)
                        except StopIteration:
                            self.bg = None
                ob = self.stage("b")
                p.copy("act", ob[:], po[:], [po], [ob])
                p.dma("pool", self.mixT[512 + h * 128:512 + (h + 1) * 128, qc * 512:(qc + 1) * 512], ob[:], [ob], [self.mixT])

    def hgrn(self, l):
        p, S, NT = self.p, self.S, self.NT
        ps = self.psum
        f32t = lambda n: p.sb([128, 512], F32, n)
        gf, fk, lg, Gt, A1, A3, E1, E1n, E2, E3 = [f32t(n) for n in
                                                   ("gf", "fk", "lg", "Gt", "A1", "A3", "E1", "E1n", "E2", "E3")]
        kk = f32t("kk")
        kl = f32t("kl")
        qb, qg, qG, kg = [p.sb([128, 512], BF16, n) for n in ("qb", "qg", "qG", "kg")]
        klT = p.sb([64, 8, 128], BF16, "klT")
        vt = p.sb([64, 8, 128], BF16, "vt")
        Sf = p.sb([128, 128], F32, "Sf")
        Sb = p.sb([128, 128], BF16, "Sb")
        egl = p.sb([128, 8], F32, "egl")
        scs = [p.sb([64, 64], BF16, f"scs{i}") for i in range(2)]
        tri = self.cf[0:64, 392:456]
        rmask = self.cf[:, 456:968]
        v3 = lambda t: t[:].rearrange("p (c t) -> p c t", t=64)
        for h in range(4):
            col = l * 4 + h
            p.memset("dve", Sf[:], 0.0, [Sf])
            p.memset("pool", Sb[:], 0.0, [Sb])
            for tt in range(NT):
                cs = slice(tt * 512, (tt + 1) * 512)
                p.dma("sp", gf[:], self.fmf[(FF_GF + h) * 128:(FF_GF + h + 1) * 128, cs], [self.fmf], [gf])
                p.dma("sp", qb[:], self.fmb[(FB_GQ + h) * 128:(FB_GQ + h + 1) * 128, cs], [self.fmb], [qb])
                p.dma("sp", vt[:], self.tmb[tt * 512:(tt + 1) * 512, TM_GI + h * 128:TM_GI + (h + 1) * 128]
                      .rearrange("(c m) e -> m c e", m=64), [self.tmb], [vt])
                p.op("act", lambda e: e.activation(gf[:], gf[:], AF.Sigmoid), [gf], [gf])
                p.ts("dve", fk[:], gf[:], self.oml[:, col:col + 1], self.lb[:, col:col + 1], ALU.mult, ALU.add,
                     [gf, self.oml, self.lb], [fk])
                p.ts("dve", kk[:], fk[:], -1.0, 1.0, ALU.mult, ALU.add, [fk], [kk])
                p.ts("dve", fk[:], fk[:], 1e-6, None, ALU.max, None, [fk], [fk])
                p.op("act", lambda e: e.activation(lg[:], fk[:], AF.Ln), [fk], [lg])
                p.op("dve", lambda e: e.tensor_tensor_scan(Gt[:], rmask, lg[:], 0.0, ALU.mult, ALU.add),
                     [self.cf, lg], [Gt])
                G3 = v3(Gt)
                p.tt("dve", v3(A1), G3, G3[:, :, 31:32].to_broadcast([128, 8, 64]), ALU.subtract, [Gt], [A1])
                p.tt("dve", v3(A3), G3, G3[:, :, 63:64].to_broadcast([128, 8, 64]), ALU.subtract, [Gt], [A3])
                p.op("act", lambda e: e.activation(E1[:], A1[:], AF.Exp), [A1], [E1])
                p.op("act", lambda e: e.activation(E1n[:], A1[:], AF.Exp, scale=-1.0), [A1], [E1n])
                p.op("act", lambda e: e.activation(E2[:], Gt[:], AF.Exp), [Gt], [E2])
                p.op("act", lambda e: e.activation(E3[:], A3[:], AF.Exp, scale=-1.0), [A3], [E3])
                p.op("act", lambda e: e.activation(egl[:].rearrange("p (c o) -> p c o", o=1), G3[:, :, 63:64], AF.Exp),
                     [Gt], [egl])
                p.stt(qg[:], qb[:], SCALE, E1[:], ALU.mult, ALU.mult, [qb, E1], [qg])
                p.stt(qG[:], qb[:], SCALE, E2[:], ALU.mult, ALU.mult, [qb, E2], [qG])
                p.tt("dve", kg[:], kk[:], E1n[:], ALU.mult, [kk, E1n], [kg])
                p.tt("pool", kl[:], kk[:], E3[:], ALU.mult, [kk, E3], [kl])
                for half in range(2):
                    pk = ps[half]
                    for c4 in range(4):
                        c = half * 4 + c4
                        p.op("pe", lambda e, pk=pk, c=c, c4=c4: e.transpose(pk[0:64, c4 * 128:(c4 + 1) * 128],
                                                                           kl[:, c * 64:(c + 1) * 64], self.ident),
                             [kl, self.cf], [pk])
                    p.copy("act", klT[:, half * 4:(half + 1) * 4, :],
                           pk[0:64, :].rearrange("p (c d) -> p c d", d=128), [pk], [klT])
                po = ps[4]
                for c in range(8):
                    cc = slice(c * 64, (c + 1) * 64)
                    psc = ps[2 + c % 2]
                    p.mm(psc[0:64, 0:64], kg[:, cc], qg[:, cc], True, True, [kg, qg], [psc])
                    sc = scs[c % 2]
                    p.tt("dve", sc[:], psc[0:64, 0:64], tri, ALU.mult, [psc, self.cf], [sc])
                    p.mm(po[:, cc], vt[:, c, :], sc[:], True, False, [vt, sc], [po], sync=False)
                    p.mm(po[:, cc], Sb[:], qG[:, cc], False, True, [Sb, qG], [po])
                    pS = ps[6 + c % 2]
                    p.mm(pS[:, 0:128], klT[:, c, :], vt[:, c, :], True, True, [klT, vt], [pS])
                    p.stt(Sf[:], Sf[:], egl[:, c:c + 1], pS[:, 0:128], ALU.mult, ALU.add, [Sf, egl, pS], [Sf])
                    p.copy("act", Sb[:], Sf[:], [Sf], [Sb])
                self.headnorm_gate(po, False, self.gains[:, 68 + h:69 + h], FF_GG + h, 1024 + h * 128, tt)

    def nsa(self, l):
        p, S, NT = self.p, self.S, self.NT
        ps = self.psum
        NSL, NCMP, NCT = self.NSL, self.NCMP, self.NCT
        kcmpT = p.sb([128, NCT * 128], BF16, "kcmpT")
        vcmp = p.sb([128, NCT, 128], BF16, "vcmp")
        p.memset("pool", kcmpT[:], 0.0, [kcmpT])
        p.memset("pool", vcmp[:], 0.0, [vcmp])
        m0 = p.mark()
        kcT = p.sb([128, S], BF16, "kcT")
        vcT = p.sb([128, S], BF16, "vcT")
        p.dma("sp", kcT[:], self.fmb[FB_KC * 128:(FB_KC + 1) * 128, :], [self.fmb], [kcT])
        p.dma("sp", vcT[:], self.fmb[FB_VC * 128:(FB_VC + 1) * 128, :], [self.fmb], [vcT])
        wst = p.sb([128, 32, 128], F32, "wst")
        wck = p.sb([128, 32, 128], BF16, "wck")
        wcv = p.sb([128, 32, 128], BF16, "wcv")
        for src, dst in ((self.w_ck, wck), (self.w_cv, wcv)):
            p.dma("sp", wst[:], src[l * 4096:(l + 1) * 4096, :].rearrange("(li d) e -> d li e", d=128), [src], [wst])
            p.copy("dve", dst[:], wst[:], [wst], [dst])
        posf = p.sb([128, 64], F32, "posf")
        posb = p.sb([128, 64], BF16, "posb")
        self.load_cols(self.pos_k, l * 32, 32, posf[:, 0:32], posf)
        self.load_cols(self.pos_v, l * 32, 32, posf[:, 32:64], posf)
        p.copy("dve", posb[:], posf[:], [posf], [posb])
        bk = p.sb([128, 1], F32, "bk")
        pb = ps[1]
        for li in range(32):
            p.mm(pb[:, 0:1], wck[:, li, :], posb[:, li:li + 1], li == 0, li == 31, [wck, posb], [pb])
        p.copy("dve", bk[:], pb[:, 0:1], [pb], [bk])
        for c0 in range(0, NCMP, 512):
            n = min(512, NCMP - c0)
            pa = ps[0]
            for li in range(32):
                st_ = li + 16 * c0
                p.mm(pa[:, 0:n], wck[:, li, :], kcT[:, st_:st_ + 16 * (n - 1) + 1:16], li == 0, li == 31, [wck, kcT], [pa])
            p.ts("dve", kcmpT[:, c0:c0 + n], pa[:, 0:n], bk[:, 0:1], None, ALU.add, None, [pa, bk], [kcmpT])
        bvr = p.sb([1, 128], BF16, "bvr")
        pr = ps[2]
        for li in range(32):
            p.mm(pr[0:1, 0:128], posb[:, 32 + li:33 + li], wcv[:, li, :], li == 0, li == 31, [posb, wcv], [pr])
        p.copy("dve", bvr[:], pr[0:1, 0:128], [pr], [bvr])
        for nt in range(NCT):
            cnt = min(128, NCMP - nt * 128)
            pa = ps[3]
            for li in range(32):
                st_ = li + 16 * nt * 128
                p.mm(pa[0:cnt, 0:128], vcT[:, st_:st_ + 16 * (cnt - 1) + 1:16], wcv[:, li, :], li == 0, False,
                     [vcT, wcv], [pa], sync=False)
            p.mm(pa[0:cnt, 0:128], self.ones_b[0:1, 0:cnt], bvr[0:1, :], False, True, [self.ones_b, bvr], [pa])
            p.copy("act", vcmp[0:cnt, nt, :], pa[0:cnt, 0:128], [pa], [vcmp])
        p.release(m0)
        ksT = p.sb([128, S], BF16, "ksT")
        kwT = p.sb([128, S], BF16, "kwT")
        vs = p.sb([128, S // 128, 128], BF16, "vs")
        vw = p.sb([128, S // 128, 128], BF16, "vw")
        p.dma("sp", ksT[:], self.fmb[FB_KS * 128:(FB_KS + 1) * 128, :], [self.fmb], [ksT])
        p.dma("sp", kwT[:], self.fmb[FB_KW * 128:(FB_KW + 1) * 128, :], [self.fmb], [kwT])
        p.dma("sp", vs[:], self.tmb[:, TM_VS:TM_VS + 128].rearrange("(t p) e -> p t e", p=128), [self.tmb], [vs])
        p.dma("sp", vw[:], self.tmb[:, TM_VW:TM_VW + 128].rearrange("(t p) e -> p t e", p=128), [self.tmb], [vw])
        ex = p.sb([128, S], BF16, "ex")
        p.dma("sp", ex[:], self.ex_d[:], [self.ex_d], [ex])
        mk = p.sb([128, 17 * 512], BF16, "mk2")
        p.dma("sp", mk[:], self.masks_d[:, 4 * 512:21 * 512], [self.masks_d], [mk])
        msk = lambda i: mk[:, (i - 4) * 512:(i - 3) * 512]
        ovl = p.sb([128, NCT, NSL], BF16, "ovl")
        p.dma("sp", ovl[:], self.ovl_d[:].rearrange("p (t j) -> p t j", j=NSL), [self.ovl_d], [ovl])
        csel = p.sb([128, 528], F32, "csel")
        p.dma("sp", csel[:], self.csel_d[:], [self.csel_d], [csel])
        onesel = p.sb([128, 16], BF16, "onesel")
        p.copy("dve", onesel[:], csel[:, 0:16], [csel], [onesel])
        acc = p.sb([128, 4, 512], F32, "acc")
        imp = p.sb([128, 4, NSL], F32, "imp")
        At = p.sb([128, 4, NSL], F32, "At")
        Bt = p.sb([128, 4, NSL], F32, "Bt")
        impF = p.sb([128, NSL], F32, "impF")
        wk1 = p.sb([128, NSL], F32, "wk1")
        wk2 = p.sb([128, NSL], F32, "wk2")
        sel = p.sb([128, NSL], F32, "sel")
        m8 = p.sb([128, 8], F32, "m8")
        selT = p.sb([128, 512], BF16, "selT")
        qh = p.sb([128, 4, 512], BF16, "qh")
        mfb = [p.sb([128, 512], BF16, f"mfb{i}") for i in range(2)]
        pbuf = [p.sb([128, 512], BF16, f"pbuf{i}") for i in range(4)]
        zc = p.sb([128, 4], F32, "zc")
        z4 = p.sb([4, 512], F32, "z4")
        g4 = p.sb([4, 512], F32, "g4")
        zr = p.sb([1, 512], F32, "zr")
        gr = p.sb([1, 512], F32, "gr")

        def branch4(qc, kts, kT, vT, maskfn, expand, gate_b, first):
            items = [(i, kt, h) for i, kt in enumerate(kts) for h in range(4)]
            nit = len(items)
            mstate = {}

            def bA(t):
                i, kt, h = items[t]
                if h == 0:
                    if expand:
                        pm = ps[7]
                        p.mm(pm[:], ex[0:NSL, kt * 128:(kt + 1) * 128], selT[0:NSL, :], True, True, [ex, selT], [pm])
                        Mf = mfb[i % 2]
                        mi = maskfn(kt)
                        if mi is not None:
                            p.tt("dve", Mf[:], pm[:], msk(mi), ALU.mult, [pm, mk], [Mf])
                        else:
                            p.copy("dve", Mf[:], pm[:], [pm], [Mf])
                        mstate[i] = (Mf[:], [Mf])
                    else:
                        mstate[i] = (msk(maskfn(kt)), [mk])
                Mfa, Mfr = mstate[i]
                pa = ps[5 + t % 2]
                p.mm(pa[:], kT[:, kt * 128:(kt + 1) * 128], qh[:, h, :], True, True, [kT, qh], [pa])
                P = pbuf[t % 4]
                p.op("act", lambda e, P=P, pa=pa: e.activation(P[:], pa[:], AF.Exp, scale=SCALE), [pa], [P])
                p.tt("pool" if h % 2 else "dve", P[:], P[:], Mfa, ALU.mult, [P] + Mfr, [P])

            def bE(t):
                i, kt, h = items[t]
                P = pbuf[t % 4]
                last = i == len(kts) - 1
                p.mm(ps[h][:], vT[:, kt, :], P[:], i == 0, last, [vT, P], [ps[h]])
                p.mm(ps[4][0:4, :], onesel[:, h * 4:(h + 1) * 4], P[:], t == 0, t == nit - 1,
                     [onesel, P], [ps[4]], sync=(t == nit - 1))

            for step in range(nit + 2):
                if step < nit:
                    bA(step)
                if 0 <= step - 2 < nit:
                    bE(step - 2)
            p.ts("dve", z4[:], ps[4][0:4, :], 1e-30, None, ALU.max, None, [ps[4]], [z4])
            p.op("dve", lambda e: e.reciprocal(z4[:], z4[:]), [z4], [z4])
            p.dma("sp", g4[:], self.ngT[gate_b:12:3, qc * 512:(qc + 1) * 512], [self.ngT], [g4])
            p.op("act", lambda e: e.activation(g4[:], g4[:], AF.Sigmoid), [g4], [g4])
            p.tt("dve", g4[:], g4[:], z4[:], ALU.mult, [g4, z4], [g4])
            for h in range(4):
                pcb = ps[5 + h % 2]
                p.mm(pcb[:], csel[0:4, 16 + h * 128:16 + (h + 1) * 128], g4[0:4, :], True, True, [csel, g4], [pcb])
                o = self.stage("f")
                p.copy("act", o[:], ps[h][:], [ps[h]], [o])
                if first:
                    p.tt("dve", acc[:, h, :], o[:], pcb[:], ALU.mult, [o, pcb], [acc])
                else:
                    p.tt("dve", o[:], o[:], pcb[:], ALU.mult, [o, pcb], [o])
                    p.tt("pool", acc[:, h, :], acc[:, h, :], o[:], ALU.add, [acc, o], [acc])

        for qc in range(NT):
            cs = slice(qc * 512, (qc + 1) * 512)
            for h in range(4):
                p.dma("sp", qh[:, h, :], self.fmb[(FB_NQ + h) * 128:(FB_NQ + h + 1) * 128, cs], [self.fmb], [qh])
            p.dma("sp", At[:], self.impA_d[qc * 512:(qc + 1) * 512, :].rearrange("(j p) n -> p j n", p=128), [self.impA_d], [At])
            p.dma("sp", Bt[:], self.impB_d[qc * 512:(qc + 1) * 512, :].rearrange("(j p) n -> p j n", p=128), [self.impB_d], [Bt])
            nts = [nt for nt in range(NCT) if 512 * qc - 2048 * nt >= 0]
            for h in range(4):
                po, pi, pc, pz = ps[0], ps[1], ps[2], ps[4]
                for i, nt in enumerate(nts):
                    last = i == len(nts) - 1
                    dl = 512 * qc - 2048 * nt
                    pa = ps[5 + i % 2]
                    p.mm(pa[:], kcmpT[:, nt * 128:(nt + 1) * 128], qh[:, h, :], True, True, [kcmpT, qh], [pa])
                    P = self.stage("b")
                    p.op("act", lambda e, P=P, pa=pa: e.activation(P[:], pa[:], AF.Exp, scale=SCALE), [pa], [P])
                    if dl < 2560:
                        p.tt("dve", P[:], P[:], msk(16 + dl // 512), ALU.mult, [P, mk], [P])
                    p.mm(po[:], vcmp[:, nt, :], P[:], i == 0, last, [vcmp, P], [po])
                    p.mm(pz[0:1, :], self.ones_b[:, 0:1], P[:], i == 0, last, [self.ones_b, P], [pz])
                    for jq in range(4):
                        p.mm(pi[:, jq * NSL:(jq + 1) * NSL], P[:, jq * 128:(jq + 1) * 128], ovl[:, nt, :],
                             i == 0 and jq == 0, last and jq == 3, [P, ovl], [pi], sync=(last and jq == 3))
                    for jq in range(4):
                        p.mm(pc[:, jq:jq + 1], P[:, jq * 128:(jq + 1) * 128], self.ones_b[:, 0:1],
                             i == 0 and jq == 0, last and jq == 3, [P, self.ones_b], [pc], sync=(last and jq == 3))
                p.ts("dve", zr[:], pz[0:1, :], 1e-30, None, ALU.max, None, [pz], [zr])
                p.op("dve", lambda e: e.reciprocal(zr[:], zr[:]), [zr], [zr])
                p.dma("sp", gr[:], self.ngT[h * 3:h * 3 + 1, cs], [self.ngT], [gr])
                p.op("act", lambda e: e.activation(gr[:], gr[:], AF.Sigmoid), [gr], [gr])
                p.tt("dve", gr[:], gr[:], zr[:], ALU.mult, [gr, zr], [gr])
                pcb = ps[3]
                p.mm(pcb[:], self.ones_f[0:1, :], gr[0:1, :], True, True, [self.cf, gr], [pcb])
                o = self.stage("f")
                p.copy("act", o[:], po[:], [po], [o])
                p.tt("dve", acc[:, h, :], o[:], pcb[:], ALU.mult, [o, pcb], [acc])
                p.ts("dve", zc[:], pc[:, 0:4], 1e-30, None, ALU.max, None, [pc], [zc])
                p.op("dve", lambda e: e.reciprocal(zc[:], zc[:]), [zc], [zc])
                for jq in range(4):
                    if h == 0:
                        p.ts("dve", imp[:, jq, :], pi[:, jq * NSL:(jq + 1) * NSL], zc[:, jq:jq + 1], None, ALU.mult, None,
                             [pi, zc], [imp])
                    else:
                        p.stt(imp[:, jq, :], pi[:, jq * NSL:(jq + 1) * NSL], zc[:, jq:jq + 1], imp[:, jq, :],
                              ALU.mult, ALU.add, [pi, zc, imp], [imp])
            pst = ps[3]
            for jq in range(4):
                p.tt("dve", impF[:], imp[:, jq, :], At[:, jq, :], ALU.mult, [imp, At], [impF])
                p.tt("dve", impF[:], impF[:], Bt[:, jq, :], ALU.add, [impF, Bt], [impF])
                p.op("dve", lambda e: e.max(out=m8[:], in_=impF[:]), [impF], [m8])
                p.op("dve", lambda e: e.match_replace(out=wk1[:], in_to_replace=m8[:], in_values=impF[:], imm_value=-1e9),
                     [m8, impF], [wk1])
                p.op("dve", lambda e: e.max(out=m8[:], in_=wk1[:]), [wk1], [m8])
                p.op("dve", lambda e: e.match_replace(out=wk2[:], in_to_replace=m8[:], in_values=wk1[:], imm_value=-1e9),
                     [m8, wk1], [wk2])
                p.tt("dve", sel[:], wk2[:], impF[:], ALU.not_equal, [wk2, impF], [sel])
                p.op("pe", lambda e, jq=jq: e.transpose(pst[0:NSL, jq * 128:(jq + 1) * 128], sel[:], self.ident),
                     [sel, self.cf], [pst])
            p.copy("act", selT[0:NSL, :], pst[0:NSL, :], [pst], [selT])
            branch4(qc, list(range(4 * qc + 4)), ksT, vs,
                    lambda kt, qc=qc: (4 + kt - 4 * qc) if kt >= 4 * qc else None, True, 1, False)
            branch4(qc, list(range(max(0, 4 * qc - 4), 4 * qc + 4)), kwT, vw,
                    lambda kt, qc=qc: 8 + (kt - 4 * qc + 4), False, 2, False)
            for h in range(4):
                ob = self.stage("b")
                p.copy("act", ob[:], acc[:, h, :], [acc], [ob])
                p.dma("pool", self.mixT[1536 + h * 128:1536 + (h + 1) * 128, cs], ob[:], [ob], [self.mixT])

    def out_proj(self, l):
        p, S, NT = self.p, self.S, self.NT
        ps = self.psum
        hT = self.hT
        for st in range(NT // 2):
            p.dma("sp", hT[:], self.mixT[:, st * 1024:(st + 1) * 1024].rearrange("(k p) n -> p k n", p=128),
                  [self.mixT], [hT])
            for ct in range(16):
                wa = self.load_w(self.wb_out, ct * 128, 128)
                for j in range(2):
                    tt = st * 2 + j
                    pa = ps[(2 * ct + j) % 8]
                    for k in range(KC):
                        p.mm(pa[:], wa[:, k, :], hT[:, k, j * 512:(j + 1) * 512], k == 0, k == KC - 1, [wa, hT], [pa])
                    xo = self.stage("f")
                    p.dma("sp", xo[:], self.xT[ct * 128:(ct + 1) * 128, tt * 512:(tt + 1) * 512], [self.xT.res[tt]], [xo])
                    p.tt("dve", xo[:], xo[:], pa[:], ALU.add, [xo, pa], [xo])
                    p.dma("pool", self.xT[ct * 128:(ct + 1) * 128, tt * 512:(tt + 1) * 512], xo[:], [xo], [self.xT.res[tt]])

    def xattn(self, l):
        p, S, NT = self.p, self.S, self.NT
        ps = self.psum
        G = self.gains
        hT = self.hT
        mT = self.memT
        if l == 0:
            mh = self.memhat
            for t in range(2):
                mi = self.cast_f[t]
                p.dma("sp", mi[:], self.mem[t * 128:(t + 1) * 128, :], [self.mem], [mi])
                ss = self.stage("f")
                junk = self.cast_f[1 - t] if False else self.xs[0]
                p.op("act", lambda e, mi=mi, ss=ss: e.activation(self.xs[1][:].rearrange("p k n -> p (k n)")[:, 0:2048],
                                                                   mi[:], AF.Square, accum_out=ss[:, 0:1]),
                     [mi], [ss, self.xs[1]])
                p.op("act", lambda e, ss=ss: e.activation(ss[:, 1:2], ss[:, 0:1], AF.Sqrt, bias=self.cf[:, 389:390],
                                                          scale=1.0 / D), [ss, self.cf], [ss])
                p.op("dve", lambda e, ss=ss: e.reciprocal(ss[:, 2:3], ss[:, 1:2]), [ss], [ss])
                p.ts("dve", mi[:], mi[:], ss[:, 2:3], None, ALU.mult, None, [mi, ss], [mi])
                for k in range(KC):
                    pk = ps[k % 4]
                    p.op("pe", lambda e, pk=pk, k=k, mi=mi: e.transpose(pk[:, 0:128], mi[:, k * 128:(k + 1) * 128], self.ident),
                         [mi, self.cf], [pk])
                    p.copy("dve", mh[:, k, t * 128:(t + 1) * 128], pk[:, 0:128], [pk], [mh])
        for k in range(KC):
            p.ts("dve", mT[:, k, :], self.memhat[:, k, :], G[:, 32 + k:33 + k], None, ALU.mult, None,
                 [self.memhat, G], [mT])
        kTm = self.kTm
        vm = self.vm
        for h in range(4):
            wa = self.load_w(self.wb_k, h * 128, 128)
            pa = ps[h % 4]
            for k in range(KC):
                p.mm(pa[:, 0:256], wa[:, k, :], mT[:, k, :], k == 0, k == KC - 1, [wa, mT], [pa])
            p.copy("act", kTm[:, h, :], pa[:, 0:256], [pa], [kTm])
        for cc in range(4):
            wa = self.load_w(self.wb_v, cc * 128, 128)
            for t in range(2):
                pa = ps[(cc * 2 + t) % 4]
                for k in range(KC):
                    p.mm(pa[:, 0:128], mT[:, k, t * 128:(t + 1) * 128], wa[:, k, :], k == 0, k == KC - 1, [wa, mT], [pa])
                p.copy("act", vm[:, t, cc * 128:(cc + 1) * 128], pa[:, 0:128], [pa], [vm])
        oT = self.oT
        for st in range(NT // 2):
            for j in range(2):
                self.norm_tile(st * 2 + j, G[:, 16:32], hT, j * 512)
            for h in range(4):
                wa = self.load_w(self.wb_q, h * 128, 128)
                for j in range(2):
                    pq = ps[0]
                    for k in range(KC):
                        p.mm(pq[:], wa[:, k, :], hT[:, k, j * 512:(j + 1) * 512], k == 0, k == KC - 1, [wa, hT], [pq])
                    qT = self.qbuf[(h * 2 + j) % 2]
                    p.copy("act", qT[:], pq[:], [pq], [qT])
                    po, pz = ps[4], ps[5]
                    for t in range(2):
                        pa = ps[1 + t]
                        p.mm(pa[:], kTm[:, h, t * 128:(t + 1) * 128], qT[:], True, True, [kTm, qT], [pa])
                        w = self.stage("b")
                        p.op("act", lambda e, w=w, pa=pa: e.activation(w[:], pa[:], AF.Exp, scale=SCALE), [pa], [w])
                        p.mm(po[:], vm[:, t, h * 128:(h + 1) * 128], w[:], t == 0, t == 1, [vm, w], [po])
                        p.mm(pz[:], self.ones_b[:], w[:], t == 0, t == 1, [self.ones_b, w], [pz])
                    rz = self.stage("f")
                    p.op("dve", lambda e, rz=rz, pz=pz: e.reciprocal(rz[:], pz[:]), [pz], [rz])
                    p.tt("dve", oT[:, h, j * 512:(j + 1) * 512], po[:], rz[:], ALU.mult, [po, rz], [oT])
            for ct in range(16):
                wa = self.load_w(self.wb_o, ct * 128, 128, kchunks=4)
                for j in range(2):
                    tt = st * 2 + j
                    pa = ps[(2 * ct + j) % 4]
                    for k in range(4):
                        p.mm(pa[:], wa[:, k, :], oT[:, k, j * 512:(j + 1) * 512], k == 0, k == 3, [wa, oT], [pa])
                    xo = self.stage("f")
                    p.dma("sp", xo[:], self.xT[ct * 128:(ct + 1) * 128, tt * 512:(tt + 1) * 512], [self.xT.res[tt]], [xo])
                    p.tt("dve", xo[:], xo[:], pa[:], ALU.add, [xo, pa], [xo])
                    p.dma("pool", self.xT[ct * 128:(ct + 1) * 128, tt * 512:(tt + 1) * 512], xo[:], [xo], [self.xT.res[tt]])

    def ffn(self, l):
        p, S, NT = self.p, self.S, self.NT
        ps = self.psum
        G = self.gains
        carry = self.carry
        p.memset("dve", carry[:], 0.0, [carry])
        CW = 88
        for st in range(NT // 2):
            ms = p.mark()
            hT = self.hT = p.sb([128, KC, 1024], BF16, "hT")
            mn = p.mark()
            self.xs = [p.sb([128, KC, 512], F32, "xs0")] * 2
            self.sq = p.sb([128, KC, 512], BF16, "sq")
            self.rstd = p.sb([128, 512], F32, "rstd")
            for j in range(2):
                self.norm_tile(st * 2 + j, G[:, 48:64], hT, j * 512)
            p.release(mn)
            aT = p.sb([128, 44, 1024], BF16, "aT")
            self.wt = [p.sb([128, 44, 128], BF16, f"wt{i}") for i in range(2)]
            ubuf = [p.sb([128, 1026], F32, f"ubuf{i}") for i in range(2)]
            cbuf = [p.sb([128, 1024], F32, f"cbuf{i}") for i in range(2)]
            for ct in range(44):
                res = []
                for gi, cti in enumerate((ct, ct + 44)):
                    wa = self.load_w(self.wb_up, cti * 128, 128)
                    ub = ubuf[gi]
                    p.copy("pool", ub[:, 0:2], carry[:, cti, :], [carry], [ub])
                    for j in range(2):
                        pa = ps[(2 * gi + j) % 4]
                        for k in range(KC):
                            p.mm(pa[:], wa[:, k, :], hT[:, k, j * 512:(j + 1) * 512], k == 0, k == KC - 1, [wa, hT], [pa])
                        p.copy("act", ub[:, 2 + j * 512:2 + (j + 1) * 512], pa[:], [pa], [ub])
                    p.copy("pool", carry[:, cti, :], ub[:, 1024:1026], [ub], [carry])
                    c = cbuf[gi]
                    p.ts("dve", c[:], ub[:, 0:1024], G[:, CW + cti:CW + cti + 1], G[:, CW + 3 * 88 + cti:CW + 3 * 88 + cti + 1],
                         ALU.mult, ALU.add, [ub, G], [c])
                    p.stt(c[:], ub[:, 1:1025], G[:, CW + 88 + cti:CW + 88 + cti + 1], c[:], ALU.mult, ALU.add, [ub, G, c], [c])
                    p.stt(c[:], ub[:, 2:1026], G[:, CW + 176 + cti:CW + 176 + cti + 1], c[:], ALU.mult, ALU.add, [ub, G, c], [c])
                    res.append(c)
                gte, val = res
                p.op("act", lambda e, gte=gte: e.activation(gte[:], gte[:], AF.Silu), [gte], [gte])
                p.tt("pool", aT[:, ct, :], gte[:], val[:], ALU.mult, [gte, val], [aT])
            for ct in range(16):
                wa = self.load_w(self.wb_dn, ct * 128, 128, kchunks=44)
                for j in range(2):
                    tt = st * 2 + j
                    pa = ps[4 + (2 * ct + j) % 4]
                    for k in range(44):
                        p.mm(pa[:], wa[:, k, :], aT[:, k, j * 512:(j + 1) * 512], k == 0, k == 43, [wa, aT], [pa])
                    xo = self.stage("f")
                    p.dma("sp", xo[:], self.xT[ct * 128:(ct + 1) * 128, tt * 512:(tt + 1) * 512], [self.xT.res[tt]], [xo])
                    p.tt("dve", xo[:], xo[:], pa[:], ALU.add, [xo, pa], [xo])
                    p.dma("pool", self.xT[ct * 128:(ct + 1) * 128, tt * 512:(tt + 1) * 512], xo[:], [xo], [self.xT.res[tt]])
            p.release(ms)


def build_nc(S, L):
    mk = MK(S, L)
    return mk.build()


_IN2D = {
    "x": lambda a: a.reshape(a.shape[1], D),
    "mem": lambda a: a.reshape(256, D),
    "positions": lambda a: a.reshape(1, -1),
    "mix_norm": lambda a: a.reshape(-1, 128),
    "w_in": lambda a: a.reshape(-1, NCOL),
    "ret_norm": lambda a: a.reshape(-1, 128),
    "hgrn_lb_logits": lambda a: a.reshape(-1, 128),
    "hgrn_norm": lambda a: a.reshape(-1, 128),
    "nsa_pos_k": lambda a: a.reshape(-1, 128),
    "nsa_pos_v": lambda a: a.reshape(-1, 128),
    "nsa_w_ck": lambda a: a.reshape(-1, 128),
    "nsa_w_cv": lambda a: a.reshape(-1, 128),
    "w_out": lambda a: a.reshape(-1, D),
    "xattn_norm": lambda a: a.reshape(-1, 128),
    "mem_norm": lambda a: a.reshape(-1, 128),
    "xattn_wq": lambda a: a.reshape(-1, 512),
    "xattn_wk": lambda a: a.reshape(-1, 512),
    "xattn_wv": lambda a: a.reshape(-1, 512),
    "xattn_wo": lambda a: a.reshape(-1, D),
    "ffn_norm": lambda a: a.reshape(-1, 128),
    "ffn_w_up": lambda a: a.reshape(-1, 2 * DFF),
    "ffn_conv_w": lambda a: a.reshape(-1, 128),
    "ffn_conv_b": lambda a: a.reshape(-1, 128),
    "ffn_w_down": lambda a: a.reshape(-1, D),
    "final_norm": lambda a: a.reshape(16, 128),
}


def kernel(**inputs):
    S = inputs["x"].shape[1]
    L = inputs["w_in"].shape[0]
    nc = build_nc(S, L)
    m = {}
    for k, f in _IN2D.items():
        m[k] = np.ascontiguousarray(f(np.asarray(inputs[k])))
    m.update(host_consts(S))
    import os
    if os.environ.get("MKTRACE"):
        res = run_bass_kernel_spmd(nc, [m], core_ids=[0], trace=True)
        print("EXEC_TIME_NS", res.exec_time_ns)
        global TRACE
        TRACE = res
    else:
        res = run_bass_kernel_spmd(nc, [m], core_ids=[0])
    global LAST
    LAST = res.results[0]
    return np.asarray(res.results[0]["out"]).reshape(1, S, D)
```
